# Optimizing a Trainium2 kernel written in Bass

```python
import jax
import jax.numpy as jnp
from jax import lax
import numpy as np

D_MODEL = 2048
BATCH = 8
SEQ = 4096
DEPTH = 4
DEC_BATCH = 8
DEC_SEQ = 16
PAST_LEN = 4096

CHUNK = 64
NORM_EPS = 1e-6
L2_EPS = 1e-6
GLA_HEADS = 4
GLA_DK = 64
GLA_DV = 128
GLA_LORA = 16
GLA_GATE_TEMP = 16.0
GDN_HEADS = 6
GDN_DK = 128
GDN_DV = 128
CONV_W = 4
RW_HEADS = 12
RW_N = 64
RW_DECAY_LORA = 64
RW_AAA_LORA = 64
RW_MV_LORA = 32
RW_GATE_LORA = 128
RW_GN_EPS = 64e-5

GLA_QK = GLA_HEADS * GLA_DK
GLA_V = GLA_HEADS * GLA_DV
GDN_QK = GDN_HEADS * GDN_DK
GDN_V = GDN_HEADS * GDN_DV
GDN_CONV_CH = 2 * GDN_QK + GDN_V
RW_C = RW_HEADS * RW_N
MIX_WIDTH = GLA_V + GDN_V + RW_C
GLA_SIZES = (GLA_QK, GLA_QK, GLA_V, GLA_LORA, GLA_V)
GDN_SIZES = (GDN_CONV_CH, GDN_HEADS, GDN_HEADS, GDN_V)
RW_SIZES = (RW_C, RW_C, RW_C, RW_DECAY_LORA, RW_AAA_LORA, RW_GATE_LORA)
GLA_PROJ = sum(GLA_SIZES)
GDN_PROJ = sum(GDN_SIZES)
RW_PROJ = sum(RW_SIZES)
PROJ_WIDTH = GLA_PROJ + GDN_PROJ + RW_PROJ
D_FF = -(-8 * D_MODEL // (3 * 256)) * 256

kernel_name = 'hybrid_gla_gdn_rwkv7_stream_step'


def _split(z, sizes):
    idx = [int(i) for i in np.cumsum(sizes)[:-1]]
    return jnp.split(z, idx, axis=-1)


def rmsnorm(x, g):
    x32 = x.astype(jnp.float32)
    y = x32 * lax.rsqrt(jnp.mean(x32 * x32, axis=-1, keepdims=True) + NORM_EPS)
    return (y * g.astype(jnp.float32)).astype(x.dtype)


def l2norm(x):
    return x * lax.rsqrt(jnp.sum(x * x, axis=-1, keepdims=True) + L2_EPS)


def causal_conv(x, buf, w):
    xc = jnp.concatenate([buf.astype(x.dtype), x], axis=1)
    t = x.shape[1]
    y = xc[:, 0:t] * w[0]
    for j in range(1, CONV_W):
        y = y + xc[:, j:j + t] * w[j]
    return y, xc[:, t:]


def _to_blocks(a, c):
    b, t = a.shape[0], a.shape[1]
    a = a.reshape((b, t // c, c) + a.shape[2:])
    return jnp.moveaxis(a, (1, 3), (0, 2))


def _from_blocks(o):
    o = jnp.moveaxis(o, (0, 2), (1, 3))
    return o.reshape((o.shape[0], o.shape[1] * o.shape[2]) + o.shape[3:])


def gla_recurrence(q, k, v, loga, s0):
    c = min(CHUNK, q.shape[1])
    causal = jnp.tril(jnp.ones((c, c), bool))

    def step(s, inp):
        qi, ki, vi, li = inp
        b = jnp.cumsum(li, axis=2)
        o = jnp.einsum('bhtd,bhde->bhte', qi * jnp.exp(b), s)
        diff = b[:, :, :, None, :] - b[:, :, None, :, :]
        dec = jnp.exp(jnp.where(causal[:, :, None], diff, -jnp.inf))
        scores = jnp.einsum('bhtd,bhsd,bhtsd->bhts', qi, ki, dec)
        o = o + jnp.einsum('bhts,bhse->bhte', scores, vi)
        b_last = b[:, :, -1:, :]
        s_new = jnp.exp(b[:, :, -1, :])[..., None] * s + jnp.einsum(
            'bhsd,bhse->bhde', ki * jnp.exp(b_last - b), vi)
        return s_new, o

    xs = tuple(_to_blocks(a, c) for a in (q, k, v, loga))
    s, o = lax.scan(step, s0, xs)
    return _from_blocks(o), s


def gdn_recurrence(q, k, v, beta, loga, s0):
    c = min(CHUNK, q.shape[1])
    causal = jnp.tril(jnp.ones((c, c), bool))
    strict = jnp.tril(jnp.ones((c, c), bool), -1)
    eye = jnp.eye(c, dtype=jnp.float32)

    def step(s, inp):
        qi, ki, vi, bi, li = inp
        b = jnp.cumsum(li, axis=-1)
        dec = jnp.exp(jnp.where(causal, b[..., :, None] - b[..., None, :], -jnp.inf))
        kk = jnp.einsum('bhtd,bhsd->bhts', ki, ki)
        lower = jnp.where(strict, bi[..., :, None] * dec * kk, 0.0)
        eb = jnp.exp(b)[..., None]
        rhs = bi[..., None] * (vi - eb * jnp.einsum('bhtd,bhde->bhte', ki, s))
        delta = lax.linalg.triangular_solve(lower + eye, rhs, left_side=True,
                                            lower=True, unit_diagonal=True)
        qk = jnp.einsum('bhtd,bhsd->bhts', qi, ki) * dec
        o = eb * jnp.einsum('bhtd,bhde->bhte', qi, s) + jnp.einsum('bhts,bhse->bhte', qk, delta)
        b_last = b[..., -1:]
        s_new = jnp.exp(b_last)[..., None] * s + jnp.einsum(
            'bhsd,bhse->bhde', ki * jnp.exp(b_last - b)[..., None], delta)
        return s_new, o

    xs = tuple(_to_blocks(a, c) for a in (q, k, v, beta, loga))
    s, o = lax.scan(step, s0, xs)
    return _from_blocks(o), s


def rwkv_recurrence(r, w, k, v, kk, a, s0):
    def step(s, inp):
        rt, wt, kt, vt, kkt, at = inp
        sa = jnp.einsum('bhk,bhkv->bhv', kkt, s)
        s = wt[..., None] * s - (kkt * at)[..., None] * sa[..., None, :] + kt[..., None] * vt[..., None, :]
        return s, jnp.einsum('bhk,bhkv->bhv', rt, s)

    xs = tuple(jnp.swapaxes(u, 0, 1) for u in (r, w, k, v, kk, a))
    s, o = lax.scan(step, s0, xs)
    return jnp.swapaxes(o, 0, 1), s


def mixer_layer(h, l, v_first, s_gla, s_gdn, c_gdn, s_rw, c_rw, prm):
    f32 = jnp.float32
    b, t, _ = h.shape
    P = lambda name: prm[name][l].astype(f32)
    z = jnp.matmul(h, prm['w_in'][l]).astype(f32)
    z_gla, z_gdn, z_rw = _split(z, (GLA_PROJ, GDN_PROJ, RW_PROJ))

    gq, gk, gv, ga, gg = _split(z_gla, GLA_SIZES)
    q = gq.reshape(b, t, GLA_HEADS, GLA_DK) * (GLA_DK ** -0.5)
    k = gk.reshape(b, t, GLA_HEADS, GLA_DK)
    v = gv.reshape(b, t, GLA_HEADS, GLA_DV)
    loga = jax.nn.log_sigmoid(ga @ P('gla_a_up') + P('gla_a_bias')) / GLA_GATE_TEMP
    o, s_gla_new = gla_recurrence(q, k, v, loga.reshape(b, t, GLA_HEADS, GLA_DK), s_gla.astype(f32))
    o_gla = (rmsnorm(o, P('gla_norm_g')) * jax.nn.silu(gg.reshape(b, t, GLA_HEADS, GLA_DV))).reshape(b, t, GLA_V)

    dqkv, dbeta, da, dg = _split(z_gdn, GDN_SIZES)
    conv, c_gdn_new = causal_conv(dqkv, c_gdn, P('gdn_conv_w'))
    cq, ck, cv = _split(jax.nn.silu(conv), (GDN_QK, GDN_QK, GDN_V))
    q = l2norm(cq.reshape(b, t, GDN_HEADS, GDN_DK)) * (GDN_DK ** -0.5)
    k = l2norm(ck.reshape(b, t, GDN_HEADS, GDN_DK))
    v = cv.reshape(b, t, GDN_HEADS, GDN_DV)
    beta = jax.nn.sigmoid(dbeta)
    loga = -jnp.exp(P('gdn_A_log')) * jax.nn.softplus(da + P('gdn_dt_bias'))
    o, s_gdn_new = gdn_recurrence(q, k, v, beta, loga, s_gdn.astype(f32))
    o_gdn = (rmsnorm(o, P('gdn_norm_g')) * jax.nn.silu(dg.reshape(b, t, GDN_HEADS, GDN_DV))).reshape(b, t, GDN_V)

    z_prev = jnp.concatenate([c_rw.astype(f32), z_rw[:, :-1]], axis=1)
    zm = z_rw + (z_prev - z_rw) * P('rw_mu')
    c_rw_new = z_rw[:, t - 1:]
    xr, xk, xv, xw, xa, xg = _split(zm, RW_SIZES)
    w_log = -jax.nn.softplus(-(P('rw_w0') + jnp.tanh(xw) @ P('rw_w_up'))) - 0.5
    decay = jnp.exp(-jnp.exp(w_log))
    a = jax.nn.sigmoid(P('rw_a0') + xa @ P('rw_a_up'))
    if l == 0:
        v_first = xv
    else:
        nu = jax.nn.sigmoid(prm['rw_v0'][l - 1].astype(f32)
                            + (xv @ prm['rw_v_down'][l - 1].astype(f32)) @ prm['rw_v_up'][l - 1].astype(f32))
        xv = xv + (v_first - xv) * nu
    hs = lambda u: u.reshape(b, t, RW_HEADS, RW_N)
    kk = l2norm(hs(xk * P('rw_k_k')))
    xk = xk * (1.0 + (a - 1.0) * P('rw_k_a'))
    r4, k4, v4 = hs(xr), hs(xk), hs(xv)
    o, s_rw_new = rwkv_recurrence(r4, hs(decay), k4, v4, kk, hs(a), s_rw.astype(f32))
    mu = jnp.mean(o, axis=-1, keepdims=True)
    var = jnp.mean(jnp.square(o - mu), axis=-1, keepdims=True)
    o = ((o - mu) * lax.rsqrt(var + RW_GN_EPS)).reshape(b, t, RW_C) * P('rw_ln_g') + P('rw_ln_b')
    bonus = jnp.sum(r4 * k4 * P('rw_r_k'), axis=-1, keepdims=True) * v4
    o_rw = (o + bonus.reshape(b, t, RW_C)) * (jax.nn.sigmoid(xg) @ P('rw_g_up'))

    mix = jnp.matmul(jnp.concatenate([o_gla, o_gdn, o_rw], axis=-1).astype(h.dtype), prm['w_out'][l])
    new = (s_gla_new.astype(s_gla.dtype), s_gdn_new.astype(s_gdn.dtype), c_gdn_new.astype(c_gdn.dtype),
           s_rw_new.astype(s_rw.dtype), c_rw_new.astype(c_rw.dtype))
    return mix, new, v_first


def trunk(x, s_gla, s_gdn, c_gdn, s_rw, c_rw, prm):
    outs = ([], [], [], [], [])
    v_first = None
    for l in range(DEPTH):
        h = rmsnorm(x, prm['norm1_g'][l])
        mix, new, v_first = mixer_layer(h, l, v_first, s_gla[l], s_gdn[l], c_gdn[l], s_rw[l], c_rw[l], prm)
        x = x + mix
        h = rmsnorm(x, prm['norm2_g'][l])
        x = x + jnp.matmul(jax.nn.silu(jnp.matmul(h, prm['w_ffn_gate'][l])) * jnp.matmul(h, prm['w_ffn_up'][l]),
                           prm['w_ffn_down'][l])
        for lst, arr in zip(outs, new):
            lst.append(arr)
    y = rmsnorm(x, prm['final_norm_g'])
    return y, [jnp.stack(lst) for lst in outs]


def setup_inputs(seed: int = 0) -> dict:
    key = jax.random.key(seed)
    ks = iter(jax.random.split(key, 48))
    f32 = jnp.float32
    nrm = lambda shape, scale: jax.random.normal(next(ks), shape, f32) * scale
    L = DEPTH
    dt = jnp.exp(jax.random.uniform(next(ks), (L, GDN_HEADS), f32, float(np.log(1e-3)), float(np.log(1e-1))))
    return {
        'x_prompt': nrm((BATCH, SEQ, D_MODEL), 1.0),
        'x_sample': nrm((DEC_BATCH, DEC_SEQ, D_MODEL), 1.0),
        'state_gla': nrm((L, DEC_BATCH, GLA_HEADS, GLA_DK, GLA_DV), 0.5),
        'state_gdn': nrm((L, DEC_BATCH, GDN_HEADS, GDN_DK, GDN_DV), 0.3),
        'cache_gdn_conv': nrm((L, DEC_BATCH, CONV_W - 1, GDN_CONV_CH), 1.0),
        'state_rwkv': nrm((L, DEC_BATCH, RW_HEADS, RW_N, RW_N), 0.3),
        'cache_rwkv_shift': nrm((L, DEC_BATCH, 1, RW_PROJ), 1.0),
        'norm1_g': 1.0 + nrm((L, D_MODEL), 0.02),
        'w_in': nrm((L, D_MODEL, PROJ_WIDTH), D_MODEL ** -0.5),
        'gla_a_up': nrm((L, GLA_LORA, GLA_QK), GLA_LORA ** -0.5),
        'gla_a_bias': 1.0 + nrm((L, GLA_QK), 0.5),
        'gla_norm_g': 1.0 + nrm((L, GLA_DV), 0.02),
        'gdn_conv_w': nrm((L, CONV_W, GDN_CONV_CH), CONV_W ** -0.5),
        'gdn_A_log': jnp.log(jax.random.uniform(next(ks), (L, GDN_HEADS), f32, 1.0, 16.0)),
        'gdn_dt_bias': dt + jnp.log(-jnp.expm1(-dt)),
        'gdn_norm_g': 1.0 + nrm((L, GDN_DV), 0.02),
        'rw_mu': jax.random.uniform(next(ks), (L, RW_PROJ), f32),
        'rw_w0': jax.random.uniform(next(ks), (L, RW_C), f32, -6.0, -1.0),
        'rw_w_up': nrm((L, RW_DECAY_LORA, RW_C), 0.1),
        'rw_a0': nrm((L, RW_C), 0.1),
        'rw_a_up': nrm((L, RW_AAA_LORA, RW_C), 0.1),
        'rw_v0': nrm((L - 1, RW_C), 0.1),
        'rw_v_down': nrm((L - 1, RW_C, RW_MV_LORA), RW_C ** -0.5),
        'rw_v_up': nrm((L - 1, RW_MV_LORA, RW_C), 0.1),
        'rw_g_up': nrm((L, RW_GATE_LORA, RW_C), RW_GATE_LORA ** -0.5),
        'rw_k_k': 0.85 + nrm((L, RW_C), 0.02),
        'rw_k_a': 1.0 + nrm((L, RW_C), 0.02),
        'rw_r_k': nrm((L, RW_HEADS, RW_N), 0.1),
        'rw_ln_g': 1.0 + nrm((L, RW_C), 0.02),
        'rw_ln_b': nrm((L, RW_C), 0.02),
        'w_out': nrm((L, MIX_WIDTH, D_MODEL), 0.5 * MIX_WIDTH ** -0.5),
        'norm2_g': 1.0 + nrm((L, D_MODEL), 0.02),
        'w_ffn_gate': nrm((L, D_MODEL, D_FF), D_MODEL ** -0.5),
        'w_ffn_up': nrm((L, D_MODEL, D_FF), D_MODEL ** -0.5),
        'w_ffn_down': nrm((L, D_FF, D_MODEL), 0.5 * D_FF ** -0.5),
        'final_norm_g': 1.0 + nrm((D_MODEL,), 0.02),
    }


def reference(x_prompt, x_sample, state_gla, state_gdn, cache_gdn_conv, state_rwkv, cache_rwkv_shift,
              norm1_g, w_in, gla_a_up, gla_a_bias, gla_norm_g, gdn_conv_w, gdn_A_log, gdn_dt_bias,
              gdn_norm_g, rw_mu, rw_w0, rw_w_up, rw_a0, rw_a_up, rw_v0, rw_v_down, rw_v_up, rw_g_up,
              rw_k_k, rw_k_a, rw_r_k, rw_ln_g, rw_ln_b, w_out, norm2_g, w_ffn_gate, w_ffn_up,
              w_ffn_down, final_norm_g):
    prm = dict(norm1_g=norm1_g, w_in=w_in, gla_a_up=gla_a_up, gla_a_bias=gla_a_bias, gla_norm_g=gla_norm_g,
               gdn_conv_w=gdn_conv_w, gdn_A_log=gdn_A_log, gdn_dt_bias=gdn_dt_bias, gdn_norm_g=gdn_norm_g,
               rw_mu=rw_mu, rw_w0=rw_w0, rw_w_up=rw_w_up, rw_a0=rw_a0, rw_a_up=rw_a_up, rw_v0=rw_v0,
               rw_v_down=rw_v_down, rw_v_up=rw_v_up, rw_g_up=rw_g_up, rw_k_k=rw_k_k, rw_k_a=rw_k_a,
               rw_r_k=rw_r_k, rw_ln_g=rw_ln_g, rw_ln_b=rw_ln_b, w_out=w_out, norm2_g=norm2_g,
               w_ffn_gate=w_ffn_gate, w_ffn_up=w_ffn_up, w_ffn_down=w_ffn_down, final_norm_g=final_norm_g)
    bp, dtp = x_prompt.shape[0], x_prompt.dtype
    y_prompt, (gla_p, gdn_p, conv_p, rwkv_p, shift_p) = trunk(
        x_prompt,
        jnp.zeros((DEPTH, bp, GLA_HEADS, GLA_DK, GLA_DV), dtp),
        jnp.zeros((DEPTH, bp, GDN_HEADS, GDN_DK, GDN_DV), dtp),
        jnp.zeros((DEPTH, bp, CONV_W - 1, GDN_CONV_CH), dtp),
        jnp.zeros((DEPTH, bp, RW_HEADS, RW_N, RW_N), dtp),
        jnp.zeros((DEPTH, bp, 1, RW_PROJ), dtp),
        prm)
    y_sample, (gla_s, gdn_s, conv_s, rwkv_s, shift_s) = trunk(
        x_sample, state_gla, state_gdn, cache_gdn_conv, state_rwkv, cache_rwkv_shift, prm)
    return (y_prompt, y_sample, gla_p, gdn_p, conv_p, rwkv_p, shift_p, gla_s, gdn_s, conv_s, rwkv_s, shift_s)
```

```python
import numpy as np
from contextlib import ExitStack
import concourse.bass as bass
import concourse.mybir as mybir
from concourse.bass_utils import run_bass_kernel_spmd

F32 = mybir.dt.float32
BF16 = mybir.dt.bfloat16
AF = mybir.ActivationFunctionType
ALU = mybir.AluOpType

D = 2048
KT = 16
DFF = 5632
PROJ = 7196
GLA0, GDN0, RW0 = 0, 1552, 4636
NEG = -30000.0
K0 = float(np.exp(-0.5))


class Eng:
    def __init__(s, name):
        s.name = name; s.sem = None; s.seq = 0; s.q = []; s.seen = {}
        s.dsems = []; s.dcount = 0


class V:
    __slots__ = ('b', 'ap')

    def __init__(s, b, ap):
        s.b = b; s.ap = ap

    def __getitem__(s, i):
        return V(s.b, s.ap[i])

    def bc(s, shape):
        return V(s.b, s.ap.broadcast_to(list(shape)))

    def un(s, d):
        return V(s.b, s.ap.unsqueeze(d))

    def cast(s, dt):
        return V(s.b, s.ap.bitcast(dt))

    def re(s, pat, **kw):
        return V(s.b, s.ap.rearrange(pat, **kw))


class Buf:
    excl = False

    def __init__(s, t):
        s.t = t; s.w = None; s.r = {}

    def __getitem__(s, i):
        return V(s, s.t[i])

    def v(s):
        return V(s, s.t[:])


class KB:
    def __init__(s, nc):
        s.nc = nc
        s.pe, s.act, s.dve, s.pool, s.sp = Eng('pe'), Eng('act'), Eng('dve'), Eng('pool'), Eng('sp')
        s.engs = [s.pe, s.act, s.dve, s.pool, s.sp]
        s.semid = {}

    def _need(s, eng, ev):
        sem, val = ev
        k = id(sem)
        if eng.seen.get(k, 0) < val:
            eng.q.append(lambda e, sem=sem, val=val: e.wait_ge(sem, val))
            eng.seen[k] = val

    def _deps(s, eng, reads, writes):
        for b in reads:
            if b.w is not None:
                s._need(eng, b.w)
            if b.excl:
                for k, ev in b.r.items():
                    if ev[0] is not eng.sem:
                        s._need(eng, ev)
        pe = eng is s.pe
        for b in writes:
            if b.w is not None and not (pe and b.w[0] is eng.sem):
                s._need(eng, b.w)
            for k, ev in b.r.items():
                if not (pe and ev[0] is eng.sem):
                    s._need(eng, ev)

    def _mark(s, ev, reads, writes):
        for b in writes:
            b.w = ev; b.r = {}
        for b in reads:
            if b not in writes:
                b.r[id(ev[0])] = ev

    enabled = True

    def I(s, eng, fn, reads, writes):
        if not s.enabled:
            return
        reads = [x for x in reads if x is not None]
        s._deps(eng, reads, writes)
        eng.seq += 1
        ev = (eng.sem, eng.seq)
        eng.q.append(lambda e, fn=fn, sem=eng.sem: fn(e).then_inc(sem, 1))
        s._mark(ev, reads, writes)

    def dma(s, q, out_ap, in_ap, reads, writes, slow=False):
        if not s.enabled:
            return
        K = len(q.dsems)
        i = q.dcount; q.dcount += 1
        sem = q.dsems[i % K]
        if i >= K:
            s._need(q, (sem, 16 * (i // K)))
        for b in reads:
            if b.w is not None:
                s._need(q, b.w)
        for b in writes:
            if b.w is not None:
                s._need(q, b.w)
            for k, ev in b.r.items():
                s._need(q, ev)
        ev = (sem, 16 * (i // K + 1))
        if slow:
            q.q.append(lambda e, o=out_ap, a=in_ap, sem=sem: e.dma_start(out=o, in_=a, allow_slow_non_contiguous=True).then_inc(sem, 16))
        else:
            q.q.append(lambda e, o=out_ap, a=in_ap, sem=sem: e.dma_start(out=o, in_=a).then_inc(sem, 16))
        s._mark(ev, reads, writes)
        return ev

    def mm(s, out, lhsT, rhs, start=True, stop=True):
        s.I(s.pe, lambda e, o=out.ap, l=lhsT.ap, r=rhs.ap, st=start, sp=stop: e.matmul(o, l, r, start=st, stop=sp),
            [lhsT.b, rhs.b], [out.b])

    def tr(s, out, in_, ident):
        s.I(s.pe, lambda e, o=out.ap, i=in_.ap, d=ident.ap: e.transpose(o, i, d), [in_.b, ident.b], [out.b])

    def actf(s, out, in_, func, scale=1.0, bias=0.0, eng=None):
        rd = [in_.b]
        sc = scale; bi = bias
        if isinstance(scale, V):
            rd.append(scale.b); sc = scale.ap
        if isinstance(bias, V):
            rd.append(bias.b); bi = bias.ap
        s.I(s.act, lambda e, o=out.ap, i=in_.ap, f=func, sc=sc, bi=bi: e.activation(out=o, in_=i, func=f, bias=bi, scale=sc),
            rd, [out.b])

    def tt(s, out, in0, in1, op, eng=None):
        eng = eng or s.dve
        s.I(eng, lambda e, o=out.ap, a=in0.ap, b=in1.ap, op=op: e.tensor_tensor(o, a, b, op), [in0.b, in1.b], [out.b])

    def ts(s, out, in0, s1, op0, s2=None, op1=None, eng=None):
        eng = eng or s.dve
        rd = [in0.b]
        a1 = s1; a2 = s2
        if isinstance(s1, V):
            rd.append(s1.b); a1 = s1.ap
        if isinstance(s2, V):
            rd.append(s2.b); a2 = s2.ap
        if op1 is None:
            s.I(eng, lambda e, o=out.ap, a=in0.ap, a1=a1, op0=op0: e.tensor_scalar(o, a, a1, None, op0), rd, [out.b])
        else:
            s.I(eng, lambda e, o=out.ap, a=in0.ap, a1=a1, a2=a2, op0=op0, op1=op1: e.tensor_scalar(o, a, a1, a2, op0, op1), rd, [out.b])

    def stt(s, out, in0, sc, in1, op0, op1):
        rd = [in0.b, in1.b]
        a = sc
        if isinstance(sc, V):
            rd.append(sc.b); a = sc.ap
        s.I(s.dve, lambda e, o=out.ap, i0=in0.ap, a=a, i1=in1.ap, op0=op0, op1=op1: e.scalar_tensor_tensor(o, i0, a, i1, op0, op1),
            rd, [out.b])

    def scan(s, out, d0, d1):
        s.I(s.dve, lambda e, o=out.ap, a=d0.ap, b=d1.ap: e.tensor_tensor_scan(o, a, b, 0.0, ALU.mult, ALU.add),
            [d0.b, d1.b], [out.b])

    def cp(s, out, in_, eng=None):
        eng = eng or s.dve
        if eng is s.act:
            s.I(eng, lambda e, o=out.ap, i=in_.ap: e.activation(out=o, in_=i, func=AF.Copy), [in_.b], [out.b])
        else:
            s.I(eng, lambda e, o=out.ap, i=in_.ap: e.tensor_copy(o, i), [in_.b], [out.b])

    def recip(s, out, in_):
        s.I(s.dve, lambda e, o=out.ap, i=in_.ap: e.reciprocal(o, i), [in_.b], [out.b])

    def memset(s, out, val, eng=None):
        eng = eng or s.dve
        s.I(eng, lambda e, o=out.ap, v=val: e.memset(o, v), [], [out.b])


def _chunks(n, m):
    return [(i, min(m, n - i)) for i in range(0, n, m)]


def build(TP, TS, L, NTMAX=128):
    import os
    STOP = float(os.environ.get('KSTOP', '99'))
    nc = bass.Bass("TRN2", target_bir_lowering=False)
    kb = KB(nc)

    def phase(n):
        if n > STOP:
            kb.enabled = False
    dt = nc.dram_tensor
    es = ExitStack()
    for e_ in kb.engs:
        e_.sem = es.enter_context(nc.semaphore("s_%s" % e_.name))
    kb.sp.dsems = [es.enter_context(nc.semaphore("dsp%d" % i)) for i in range(8)]
    kb.pool.dsems = [es.enter_context(nc.semaphore("dpl%d" % i)) for i in range(8)]

    def din(name, shape):
        return dt(name, list(shape), F32, kind="ExternalInput").ap()

    def dout(name, shape):
        return dt(name, list(shape), F32, kind="ExternalOutput").ap()

    X = {'p': din("x_p", [TP, D]), 's': din("x_s", [TS, D])}
    st_gla = din("st_gla", [L, 4, 64, 128]); st_gdn = din("st_gdn", [L, 6, 128, 128])
    st_conv = din("st_conv", [L, 3, 2304]); st_rw = din("st_rw", [L, 12, 64, 64]); st_shift = din("st_shift", [L, 1, 2560])
    norm1_g = din("norm1_g", [L, D]); w_in = din("w_in", [L, D, PROJ])
    gla_a_up = din("gla_a_up", [L, 16, 256]); gla_a_bias = din("gla_a_bias", [L, 256]); gla_norm_g = din("gla_norm_g", [L, 128])
    gdn_conv_w = din("gdn_conv_w", [L, 4, 2304]); gdn_A_log = din("gdn_A_log", [L, 6]); gdn_dt_bias = din("gdn_dt_bias", [L, 6])
    gdn_norm_g = din("gdn_norm_g", [L, 128])
    rw_mu = din("rw_mu", [L, 2560]); rw_w0 = din("rw_w0", [L, 768]); rw_w_up = din("rw_w_up", [L, 64, 768])
    rw_a0 = din("rw_a0", [L, 768]); rw_a_up = din("rw_a_up", [L, 64, 768])
    LV = max(L - 1, 1)
    rw_v0 = din("rw_v0", [LV, 768]); rw_v_down = din("rw_v_down", [LV, 768, 32]); rw_v_up = din("rw_v_up", [LV, 32, 768])
    rw_g_up = din("rw_g_up", [L, 128, 768]); rw_k_k = din("rw_k_k", [L, 768]); rw_k_a = din("rw_k_a", [L, 768])
    rw_r_k = din("rw_r_k", [L, 12, 64]); rw_ln_g = din("rw_ln_g", [L, 768]); rw_ln_b = din("rw_ln_b", [L, 768])
    w_out = din("w_out", [L, D, D]); norm2_g = din("norm2_g", [L, D])
    w_gate = din("w_ffn_gate", [L, D, DFF]); w_up = din("w_ffn_up", [L, D, DFF]); w_down = din("w_ffn_down", [L, DFF, D])
    final_g = din("final_norm_g", [D])
    cst = din("cst", [128, 448]); selc = din("selc", [6, 768])
    Y = {'p': dout("y_p", [TP, D]), 's': dout("y_s", [TS, D])}
    O_gla = {k: dout("gla_" + k, [L, 4, 64, 128]) for k in 'ps'}
    O_gdn = {k: dout("gdn_" + k, [L, 6, 128, 128]) for k in 'ps'}
    O_conv = {k: dout("conv_" + k, [L, 3, 2304]) for k in 'ps'}
    O_rw = {k: dout("rwkv_" + k, [L, 12, 64, 64]) for k in 'ps'}
    O_shift = {k: dout("shift_" + k, [L, 1, 2560]) for k in 'ps'}

    NT0 = min(NTMAX, TP)
    NTW = NT0 + 4

    def sb(name, shape, dtype=F32):
        return Buf(nc.alloc_sbuf_tensor(name, list(shape), dtype))

    xT = [sb("xT%d" % k, [128, NT0]) for k in range(KT)]
    hT = [sb("hT%d" % k, [128, NT0], BF16) for k in range(KT)]
    mixT = [sb("mx%d" % k, [128, NT0], BF16) for k in range(KT)]
    xio = sb("xio", [128, D])
    NF, NB, NW = 22, 76, 18
    FW = max(NTW, 388)
    fpool = [sb("fp%d" % i, [128, FW]) for i in range(NF)]
    bpool = [sb("bp%d" % i, [128, NT0], BF16) for i in range(NB)]
    wpool = [sb("wp%d" % i, [128, 512], BF16) for i in range(NW)]
    ffree, bfree, wfree = list(range(NF)), list(range(NB)), list(range(NW))

    def wa():
        return wpool[wfree.pop(0)]

    def wfr(*bs):
        for b in bs:
            wfree.append(wpool.index(b))

    def fa():
        return fpool[ffree.pop(0)]

    def ba():
        return bpool[bfree.pop(0)]

    def ffr(*bs):
        for b in bs:
            ffree.append(fpool.index(b))

    def bfr(*bs):
        for b in bs:
            bfree.append(bpool.index(b))

    NSLOT = 2
    wslots = [[sb("w%d_%d" % (i, q), [128, 4, 512], BF16) for q in range(4)] for i in range(NSLOT)]
    wctr = [0]
    Sgla = [sb("Sgla%d" % l, [128, 2, 128]) for l in range(L)]
    Sgdn = [sb("Sgdn%d" % l, [128, 6, 128]) for l in range(L)]
    Srw = [sb("Srw%d" % l, [128, 6, 64]) for l in range(L)]
    chist = [sb("chist%d" % l, [128, 18, 3]) for l in range(L)]
    shist = [sb("shist%d" % l, [128, 21]) for l in range(L)]
    Sbf_gla = sb("Sbf_gla", [128, 2, 128], BF16); Sbf_gdn = sb("Sbf_gdn", [128, 6, 128], BF16); Sbf_rw = sb("Sbf_rw", [128, 6, 64], BF16)
    oT = sb("oT", [128, 6, NT0])
    browall = sb("browall", [128, 6, NT0]); ebrow = sb("ebrow", [128, 6, NT0], BF16); betarow = sb("betarow", [128, 6, NT0], BF16)
    Eplus = sb("Eplus", [128, 6, NT0], BF16)
    vfirst = sb("vfirst", [128, 6, NT0], BF16)
    NCHM = max(NT0 // 64, 1)
    ebC = sb("ebC", [128, 6, NCHM])
    cstb = sb("cstb", [128, 448])
    ident_f = cstb[:, 0:128]; bd64_f = cstb[:, 128:256]
    mUi = cstb[0:64, 256:320]; mUs = cstb[0:64, 320:384]; mLs = cstb[0:64, 384:448]
    sel6 = sb("sel6", [6, 768])
    ident_b = sb("ident_b", [128, 128], BF16); ones_b = sb("ones_b", [128, 128], BF16)
    bd64_b = sb("bd64_b", [128, 128], BF16); bd64s_b = sb("bd64s_b", [128, 128], BF16)
    nUi = sb("nUi", [64, 64]); nLs = sb("nLs", [64, 64])
    notstart = sb("notstart", [128, NT0])
    g1 = sb("g1", [128, L, 16]); g2 = sb("g2", [128, L, 16]); gf = sb("gf", [128, 16])
    a_up_b = sb("a_up_b", [16, L, 256], BF16); nabias = sb("nabias", [128, L, 2]); glan = sb("glan", [128, L])
    cw = sb("cw", [128, L, 4, 18]); negA = sb("negA", [6, L]); dtb = sb("dtb", [6, L]); gdnn = sb("gdnn", [128, L])
    mu_rkv = sb("mu_rkv", [128, L, 18]); mu_w = sb("mu_w", [64, L]); mu_a = sb("mu_a", [64, L]); mu_g = sb("mu_g", [128, L])
    w0 = sb("w0", [128, L, 6]); a0 = sb("a0", [128, L, 6]); v0 = sb("v0", [128, LV, 6])
    kk_c = sb("kk_c", [128, L, 6]); ka_c = sb("ka_c", [128, L, 6]); omka = sb("omka", [128, L, 6]); rk_c = sb("rk_c", [128, L, 6])
    lng = sb("lng", [128, L, 6]); lnb = sb("lnb", [128, L, 6])
    w_up_b = sb("w_up_b", [64, L, 768], BF16); a_upr_b = sb("a_upr_b", [64, L, 768], BF16)
    g_up_b = sb("g_up_b", [128, L, 768], BF16); v_up_b = sb("v_up_b", [32, LV, 768], BF16); v_dn_b = sb("v_dn_b", [128, LV, 6, 32], BF16)
    psb = [Buf(nc.alloc_psum_tensor("ps%d" % i, [128, 512], F32)) for i in range(8)]
    for b_ in psb:
        b_.excl = True
    pctr = [0]

    def psn():
        b = psb[pctr[0] % 6]; pctr[0] += 1
        return b

    act, dve, pool, sp = kb.act, kb.dve, kb.pool, kb.sp
    mm, tr, actf, tt, ts, stt, scan, cp, recip, memset = kb.mm, kb.tr, kb.actf, kb.tt, kb.ts, kb.stt, kb.scan, kb.cp, kb.recip, kb.memset

    def ld(dst, src_ap, q=sp, slow=True):
        kb.dma(q, dst.ap, src_ap, [], [dst.b], slow=slow)

    ld(cstb.v(), cst, slow=False)
    ld(sel6.v(), selc, slow=False)
    cp(ident_b.v(), ident_f); memset(ones_b.v(), 1.0); cp(bd64_b.v(), bd64_f)
    ts(bd64s_b.v(), bd64_f, 1.0 / 64, ALU.mult)
    ts(nUi.v(), mUi, -1.0, ALU.add, -NEG, ALU.mult)
    ts(nLs.v(), mLs, -1.0, ALU.add, -NEG, ALU.mult)

    def pcol(dst, src, pat, **kw):
        ld(dst, src.rearrange(pat, **kw))

    for l in range(L):
        pcol(g1[:, l, :], norm1_g[l], "(k p) -> p k", p=128)
        pcol(g2[:, l, :], norm2_g[l], "(k p) -> p k", p=128)
        pcol(nabias[:, l, :], gla_a_bias[l], "(j p) -> p j", p=128)
        pcol(glan[:, l:l + 1], gla_norm_g[l], "(p o) -> p o", o=1)
        pcol(gdnn[:, l:l + 1], gdn_norm_g[l], "(p o) -> p o", o=1)
        for j in range(4):
            pcol(cw[:, l, j, :], gdn_conv_w[l, j], "(i p) -> p i", p=128)
        pcol(negA[:, l:l + 1], gdn_A_log[l], "(p o) -> p o", o=1)
        pcol(dtb[:, l:l + 1], gdn_dt_bias[l], "(p o) -> p o", o=1)
        pcol(mu_rkv[:, l, :], rw_mu[l, 0:2304], "(i p) -> p i", p=128)
        pcol(mu_w[:, l:l + 1], rw_mu[l, 2304:2368], "(p o) -> p o", o=1)
        pcol(mu_a[:, l:l + 1], rw_mu[l, 2368:2432], "(p o) -> p o", o=1)
        pcol(mu_g[:, l:l + 1], rw_mu[l, 2432:2560], "(p o) -> p o", o=1)
        pcol(w0[:, l, :], rw_w0[l], "(i p) -> p i", p=128)
        pcol(a0[:, l, :], rw_a0[l], "(i p) -> p i", p=128)
        pcol(kk_c[:, l, :], rw_k_k[l], "(i p) -> p i", p=128)
        pcol(ka_c[:, l, :], rw_k_a[l], "(i p) -> p i", p=128)
        pcol(rk_c[:, l, :], rw_r_k[l], "(j i) d -> (i d) j", i=2)
        pcol(lng[:, l, :], rw_ln_g[l], "(i p) -> p i", p=128)
        pcol(lnb[:, l, :], rw_ln_b[l], "(i p) -> p i", p=128)
        ld(a_up_b[:, l, :], gla_a_up[l], q=pool, slow=False)
        ld(w_up_b[:, l, :], rw_w_up[l], q=pool, slow=False)
        ld(a_upr_b[:, l, :], rw_a_up[l], q=pool, slow=False)
        ld(g_up_b[:, l, :], rw_g_up[l], q=pool, slow=False)
    for l in range(L - 1):
        pcol(v0[:, l, :], rw_v0[l], "(i p) -> p i", p=128)
        ld(v_up_b[:, l, :], rw_v_up[l], q=pool, slow=False)
        ld(v_dn_b[:, l, :, :], rw_v_down[l].rearrange("(k p) r -> p k r", p=128), q=pool, slow=False)
    pcol(gf.v(), final_g, "(k p) -> p k", p=128)
    ts(nabias.v(), nabias.v(), -1.0, ALU.mult)
    actf(negA.v(), negA.v(), AF.Exp)
    ts(negA.v(), negA.v(), -1.0, ALU.mult)
    ts(omka.v(), ka_c.v(), -1.0, ALU.mult, 1.0, ALU.add)

    def wload(src2d, r0, kt, c0, ncols):
        slot = wslots[wctr[0] % NSLOT]; wctr[0] += 1
        for q, (k0, kn) in enumerate(_chunks(kt, 4)):
            src = src2d[r0 + k0 * 128: r0 + (k0 + kn) * 128, c0:c0 + ncols].rearrange("(k p) n -> p k n", p=128)
            kb.dma(pool, slot[q].t[:, 0:kn, 0:ncols], src, [], [slot[q]])
        return slot

    def wk(slot, k, c0, c1):
        return slot[k // 4][:, k % 4, c0:c1]

    def run_seq(sk, T):
        NT = min(NT0, T)
        C = min(64, T)
        NCH = NT // C
        NLEV = int(np.log2(C))
        Xd, Yd = X[sk], Y[sk]
        cs = slice(0, C)
        memset(notstart.v(), 1.0)
        memset(notstart[:, 0:NT].re("p (n c) -> p n c", c=C)[:, :, 0:1], 0.0)
        for l in range(L):
            if sk == 'p':
                for b in (Sgla[l], Sgdn[l], Srw[l], chist[l], shist[l]):
                    memset(b.v(), 0.0)
            else:
                for h in range(4):
                    ld(Sgla[l][(h % 2) * 64:(h % 2) * 64 + 64, h // 2, :], st_gla[l, h], slow=False)
                ld(Sgdn[l].v(), st_gdn[l].rearrange("h d e -> d h e"), slow=False)
                for h in range(12):
                    ld(Srw[l][(h % 2) * 64:(h % 2) * 64 + 64, h // 2, :], st_rw[l, h], slow=False)
                for t_ in range(3):
                    ld(chist[l][:, :, t_], st_conv[l, t_].rearrange("(i p) -> p i", p=128))
                ld(shist[l][:, 0:18], st_shift[l, 0, 0:2304].rearrange("(i p) -> p i", p=128))
                ld(shist[l][0:64, 18:19], st_shift[l, 0, 2304:2368].rearrange("(p o) -> p o", o=1))
                ld(shist[l][0:64, 19:20], st_shift[l, 0, 2368:2432].rearrange("(p o) -> p o", o=1))
                ld(shist[l][:, 20:21], st_shift[l, 0, 2432:2560].rearrange("(p o) -> p o", o=1))

        tok = slice(0, NT)

        def rmsnorm(gcol, outs):
            ps = psn()
            for k in range(KT):
                sq = ba()
                actf(sq[:, tok], xT[k][:, tok], AF.Square)
                mm(ps[:, tok], ones_b.v(), sq[:, tok], start=(k == 0), stop=(k == KT - 1))
                bfr(sq)
            rstd = fa()
            actf(rstd[:, tok], ps[:, tok], AF.Sqrt, scale=1.0 / D, bias=1e-6)
            recip(rstd[:, tok], rstd[:, tok])
            for k in range(KT):
                stt(outs[k][:, tok], xT[k][:, tok], gcol(k), rstd[:, tok], ALU.mult, ALU.mult)
            ffr(rstd)

        def dense_fm(slot, c0, width, kt, rhs, ps, prow=128):
            for k in range(kt):
                mm(ps[0:width, tok], wk(slot, k, c0, c0 + width), rhs(k), start=(k == 0), stop=(k == kt - 1))

        def pnorm(src_v, ones_v, scale, bias, dst):
            sq = ba()
            actf(sq[:, tok], src_v, AF.Square)
            ps = psn()
            mm(ps[:, tok], ones_v, sq[:, tok])
            actf(dst, ps[:, tok], AF.Sqrt, scale=scale, bias=bias)
            recip(dst, dst)
            bfr(sq)

        def tform(Nn, Aa):
            P = fa()
            Pv = P[cs, 0:6 * C].re("p (h c) -> p h c", h=6)
            tt(Pv, Aa, ident_f[cs, cs].un(1).bc([C, 6, C]), ALU.add)
            curN, curA = Nn, Aa
            prev = []
            for lev in range(1, NLEV):
                psN = psn()
                pv = psN[cs, 0:6 * C].re("p (h c) -> p h c", h=6)
                for h in range(6):
                    mm(pv[:, h, :], curA[:, h, :], curN[:, h, :])
                nN = fa()
                nNv = nN[cs, 0:6 * C].re("p (h c) -> p h c", h=6)
                cp(nNv, pv, eng=act)
                if lev < NLEV - 1:
                    psA = psn()
                    pa = psA[cs, 0:6 * C].re("p (h c) -> p h c", h=6)
                    for h in range(6):
                        mm(pa[:, h, :], curN[:, h, :], curA[:, h, :])
                    nA = fa()
                    nAv = nA[cs, 0:6 * C].re("p (h c) -> p h c", h=6)
                    cp(nAv, pa, eng=act)
                else:
                    nA = None; nAv = None
                psP = psn()
                pp = psP[cs, 0:6 * C].re("p (h c) -> p h c", h=6)
                for h in range(6):
                    mm(pp[:, h, :], nNv[:, h, :], Pv[:, h, :])
                tt(Pv, Pv, pp, ALU.add)
                curN, curA = nNv, nAv
                if prev:
                    ffr(*prev)
                prev = [x for x in (nN, nA) if x is not None]
            if prev:
                ffr(*prev)
            return P, Pv

        def headnorm_out(o_v, ones_v, nrm_scale, gcol, gate_b, dst):
            rstd = fa()
            pnorm(o_v, ones_v, nrm_scale, 1e-6, rstd[:, tok])
            t1 = fa()
            stt(t1[:, tok], o_v, gcol, rstd[:, tok], ALU.mult, ALU.mult)
            tt(dst, t1[:, tok], gate_b, ALU.mult)
            ffr(rstd, t1)

        def mixer(l):
            Wl = w_in[l]
            hr = lambda k: hT[k][:, tok]
            phase(2)
            s1 = wload(Wl, 0, KT, GLA0 + 1024, 272)
            ps = psn(); dense_fm(s1, 0, 16, KT, hr, ps)
            aT = ba(); cp(aT[0:16, tok], ps[0:16, tok], eng=act)
            sg = []
            s2 = None
            for i in range(4):
                if i == 2:
                    s2 = wload(Wl, 0, KT, GLA0 + 1296, 256)
                ps = psn()
                dense_fm(s1 if i < 2 else s2, (16 + i * 128) if i < 2 else (i - 2) * 128, 128, KT, hr, ps)
                g = ba(); actf(g[:, tok], ps[:, tok], AF.Silu); sg.append(g)
            phase(2.1)
            EP, EM, ED = [], [], []
            for j in range(2):
                ps = psn()
                mm(ps[:, tok], a_up_b[0:16, l, j * 128:(j + 1) * 128], aT[0:16, tok])
                e = fa()
                actf(e[:, tok], ps[:, tok], AF.Exp, scale=-1.0, bias=nabias[:, l, j:j + 1])
                actf(e[:, tok], e[:, tok], AF.Ln, bias=1.0)
                csm = fa()
                scan(csm[:, tok], notstart[:, tok], e[:, tok])
                ep = fa(); em = fa(); ed = fa()
                actf(ep[:, tok], csm[:, tok], AF.Exp, scale=-1.0 / 16, bias=float(np.log(0.125)))
                actf(em[:, tok], csm[:, tok], AF.Exp, scale=1.0 / 16)
                c3 = csm[:, tok].re("p (n c) -> p n c", c=C)
                tt(e[:, tok].re("p (n c) -> p n c", c=C), c3[:, :, C - 1:C].bc([128, NCH, C]), c3, ALU.subtract)
                actf(ed[:, tok], e[:, tok], AF.Exp, scale=-1.0 / 16)
                actf(ebC[:, j, 0:NCH], c3[:, :, C - 1], AF.Exp, scale=-1.0 / 16)
                EP.append(ep); EM.append(em); ED.append(ed)
                ffr(e, csm)
            bfr(aT)
            phase(2.2)
            qe, ke, kd = [], [], []
            s3 = wload(Wl, 0, KT, GLA0 + 0, 512)
            for j in range(2):
                ps = psn(); dense_fm(s3, j * 128, 128, KT, hr, ps)
                for i in range(2):
                    r0 = i * 64
                    q = ba()
                    tt(q[r0:r0 + 64, tok], ps[r0:r0 + 64, tok], EP[j][r0:r0 + 64, tok], ALU.mult)
                    memset(q[64 - r0:128 - r0, tok], 0.0)
                    qe.append(q)
            for j in range(2):
                ps = psn(); dense_fm(s3, 256 + j * 128, 128, KT, hr, ps)
                k1 = ba(); tt(k1[:, tok], ps[:, tok], EM[j][:, tok], ALU.mult); ke.append(k1)
                k2 = ba(); tt(k2[:, tok], ps[:, tok], ED[j][:, tok], ALU.mult); kd.append(k2)
            ffr(*EP, *EM, *ED)
            phase(2.3)
            s4 = wload(Wl, 0, KT, GLA0 + 512, 512)
            cp(Sbf_gla.v(), Sgla[l].v(), eng=act)
            for c in range(NCH):
                cc = slice(c * C, (c + 1) * C)
                ps = psn()
                for k in range(KT):
                    mm(ps[cs, 0:512], hT[k][:, cc], wk(s4, k, 0, 512), start=(k == 0), stop=(k == KT - 1))
                vt = wa()
                cp(vt[cs, 0:512], ps[cs, 0:512], eng=act)
                vtv = lambda h: vt[cs, h * 128:(h + 1) * 128]
                pst = psn()
                ptb = pst.v()
                for j in range(2):
                    mm(ptb[cs, j * 128:(j + 1) * 128], kd[j][:, cc], ident_b.v())
                kdt = wa()
                cp(kdt[cs, 0:256], ptb[cs, 0:256])
                pss = psn()
                for h in range(4):
                    r0 = (h % 2) * 64
                    mm(pss[cs, h * C:(h + 1) * C], ke[h // 2][:, cc], qe[h][:, cc])
                scm = wa()
                tt(scm[cs, 0:4 * C].re("p (h c) -> p h c", h=4), pss[cs, 0:4 * C].re("p (h c) -> p h c", h=4),
                   mUi[cs, cs].un(1).bc([C, 4, C]), ALU.mult)
                po = psn()
                for h in range(4):
                    r0 = (h % 2) * 64
                    mm(po[:, h * C:(h + 1) * C], Sbf_gla[:, h // 2, :], qe[h][:, cc], start=True, stop=False)
                    mm(po[:, h * C:(h + 1) * C], vtv(h), scm[cs, h * C:(h + 1) * C], start=False, stop=True)
                cp(oT[:, 0:4, cc], po[:, 0:4 * C].re("p (h c) -> p h c", h=4), eng=act)
                pS = psn()
                for h in range(4):
                    r0 = (h % 2) * 64
                    mm(pS[r0:r0 + 64, (h // 2) * 128:(h // 2) * 128 + 128], kdt[cs, h * 64:(h + 1) * 64], vtv(h))
                for j in range(2):
                    stt(Sgla[l][:, j, :], Sgla[l][:, j, :], ebC[:, j, c:c + 1], pS[:, j * 128:(j + 1) * 128], ALU.mult, ALU.add)
                cp(Sbf_gla.v(), Sgla[l].v(), eng=act)
                wfr(vt, kdt, scm)
            bfr(*qe, *ke, *kd)
            for h in range(4):
                headnorm_out(oT[:, h, tok], ones_b.v(), 1.0 / 128, glan[:, l:l + 1], sg[h][:, tok], mixT[h][:, tok])
            bfr(*sg)

            phase(3)
            sA = wload(Wl, 0, KT, GDN0 + 2304, 396)
            sB = wload(Wl, 0, KT, GDN0 + 2304 + 396, 384)
            ps = psn(); dense_fm(sA, 0, 6, KT, hr, ps)
            bT6 = fa(); actf(bT6[0:6, tok], ps[0:6, tok], AF.Sigmoid)
            ps = psn(); dense_fm(sA, 6, 6, KT, hr, ps)
            b6 = fa(); e6 = fa()
            actf(e6[0:6, tok], ps[0:6, tok], AF.Exp, bias=dtb[:, l:l + 1])
            actf(e6[0:6, tok], e6[0:6, tok], AF.Ln, bias=1.0)
            ts(e6[0:6, tok], e6[0:6, tok], negA[:, l:l + 1], ALU.mult)
            scan(b6[0:6, tok], notstart[0:6, tok], e6[0:6, tok])
            ffr(e6)
            for h in range(6):
                ps = psn()
                mm(ps[:, tok], sel6[0:6, h * 128:(h + 1) * 128], b6[0:6, tok])
                cp(browall[:, h, tok], ps[:, tok], eng=act)
                ps = psn()
                mm(ps[:, tok], sel6[0:6, h * 128:(h + 1) * 128], bT6[0:6, tok])
                cp(betarow[:, h, tok], ps[:, tok], eng=act)
            actf(ebrow[:, :, tok], browall[:, :, tok], AF.Exp)
            b4 = browall[:, :, tok].re("p h (n c) -> p h n c", c=C)
            actf(ebC[:, :, 0:NCH], b4[:, :, :, C - 1], AF.Exp)
            sgd = []
            for i in range(6):
                ps = psn()
                if i < 3:
                    dense_fm(sA, 12 + i * 128, 128, KT, hr, ps)
                else:
                    dense_fm(sB, (i - 3) * 128, 128, KT, hr, ps)
                g = ba(); actf(g[:, tok], ps[:, tok], AF.Silu); sgd.append(g)
            qn, qe, kn, kbt, kbe, kd, vb = [], [], [], [], [], [], []
            slot = None
            for i in range(18):
                if i % 4 == 0:
                    slot = wload(Wl, 0, KT, GDN0 + i * 128, min(512, 2304 - i * 128))
                ps = psn(); dense_fm(slot, (i % 4) * 128, 128, KT, hr, ps)
                xc = fa()
                cp(xc[:, 0:3], chist[l][:, i, :])
                cp(xc[:, 3:3 + NT], ps[:, tok], eng=act)
                acc = fa()
                ts(acc[:, tok], xc[:, 0:NT], cw[:, l, 0, i:i + 1], ALU.mult)
                for j in range(1, 4):
                    stt(acc[:, tok], xc[:, j:j + NT], cw[:, l, j, i:i + 1], acc[:, tok], ALU.mult, ALU.add)
                cp(chist[l][:, i, :], xc[:, NT:NT + 3])
                h = i % 6
                if i < 12:
                    sl = fa()
                    actf(sl[:, tok], acc[:, tok], AF.Silu)
                    rn = fa()
                    if i < 6:
                        pnorm(sl[:, tok], ones_b.v(), 128.0, 128e-6, rn[:, tok])
                        a = ba(); tt(a[:, tok], sl[:, tok], rn[:, tok], ALU.mult); qn.append(a)
                        b = ba(); tt(b[:, tok], a[:, tok], ebrow[:, h, tok], ALU.mult); qe.append(b)
                    else:
                        pnorm(sl[:, tok], ones_b.v(), 1.0, 1e-6, rn[:, tok])
                        a = ba(); tt(a[:, tok], sl[:, tok], rn[:, tok], ALU.mult); kn.append(a)
                        b = ba(); tt(b[:, tok], a[:, tok], betarow[:, h, tok], ALU.mult); kbt.append(b)
                        b2 = ba(); tt(b2[:, tok], b[:, tok], ebrow[:, h, tok], ALU.mult); kbe.append(b2)
                        dd = rn
                        tt(dd[:, tok].re("p (n c) -> p n c", c=C), b4[:, h, :, C - 1:C].bc([128, NCH, C]), b4[:, h, :, :], ALU.subtract)
                        actf(dd[:, tok], dd[:, tok], AF.Exp)
                        b3 = ba(); tt(b3[:, tok], a[:, tok], dd[:, tok], ALU.mult); kd.append(b3)
                    ffr(sl, rn)
                else:
                    sl = fa()
                    actf(sl[:, tok], acc[:, tok], AF.Silu)
                    a = ba(); tt(a[:, tok], sl[:, tok], betarow[:, h, tok], ALU.mult); vb.append(a)
                    ffr(sl)
                ffr(xc, acc)
            cp(Sbf_gdn.v(), Sgdn[l].v(), eng=act)
            for c in range(NCH):
                cc = slice(c * C, (c + 1) * C)
                def trans6(srcs):
                    outs = []
                    for g in range(2):
                        p = psn(); pb = p.v()
                        for hh in range(3):
                            mm(pb[cs, hh * 128:(hh + 1) * 128], srcs[g * 3 + hh][:, cc], ident_b.v())
                        o = wa(); cp(o[cs, 0:384], pb[cs, 0:384], eng=(act if g else dve)); outs.append(o)
                    return outs
                vbt = trans6(vb); kdt = trans6(kd)
                hv = lambda lst, h: lst[h // 3][cs, (h % 3) * 128:(h % 3) * 128 + 128]
                p = psn()
                mm(p[cs, 0:6], b6[0:6, cc], ident_f[0:6, 0:6])
                btok = fa(); cp(btok[cs, 0:6], p[cs, 0:6])
                Dm = fa(); Dv = Dm[cs, 0:6 * C].re("p (h c) -> p h c", h=6)
                tt(Dv, browall[cs, :, cc], btok[cs, 0:6].un(2).bc([C, 6, C]), ALU.subtract)
                aT_ = fa(); aTv = aT_[cs, 0:6 * C].re("p (h c) -> p h c", h=6)
                tt(aTv, Dv, nUi[cs, cs].un(1).bc([C, 6, C]), ALU.add)
                actf(aTv, aTv, AF.Exp)
                a2 = fa(); a2v = a2[cs, 0:6 * C].re("p (h c) -> p h c", h=6)
                tt(a2v, nLs[cs, cs].un(1).bc([C, 6, C]), Dv, ALU.subtract)
                actf(a2v, a2v, AF.Exp)
                dTs = Dm
                tt(Dv, aTv, mUs[cs, cs].un(1).bc([C, 6, C]), ALU.mult)
                pk1 = psn(); pk2 = psn(); pq = psn()
                v1 = pk1[cs, 0:6 * C].re("p (h c) -> p h c", h=6)
                v2 = pk2[cs, 0:6 * C].re("p (h c) -> p h c", h=6)
                v3 = pq[cs, 0:6 * C].re("p (h c) -> p h c", h=6)
                for h in range(6):
                    mm(v1[:, h, :], kn[h][:, cc], kbt[h][:, cc])
                    mm(v2[:, h, :], kbt[h][:, cc], kn[h][:, cc])
                    mm(v3[:, h, :], kn[h][:, cc], qn[h][:, cc])
                Aa = fa(); Aav = Aa[cs, 0:6 * C].re("p (h c) -> p h c", h=6)
                stt(Aav, v1, -1.0, Dv, ALU.mult, ALU.mult)
                Nn = fa(); Nnv = Nn[cs, 0:6 * C].re("p (h c) -> p h c", h=6)
                stt(Nnv, v2, -1.0, a2v, ALU.mult, ALU.mult)
                PT = wa(); PTv = PT[cs, 0:6 * C].re("p (h c) -> p h c", h=6)
                tt(PTv, v3, aTv, ALU.mult)
                ptv = lambda h: PT[cs, h * C:(h + 1) * C]
                ffr(Dm, aT_, a2, btok)
                P, Pv = tform(Nnv, Aav)
                ffr(Aa, Nn)
                X0 = [fa(), fa()]; U = [wa(), wa()]
                for g in range(2):
                    p = psn()
                    for hh in range(3):
                        h = g * 3 + hh
                        mm(p[cs, hh * 128:(hh + 1) * 128], kbe[h][:, cc], Sbf_gdn[:, h, :])
                    tt(X0[g][cs, 0:384], vbt[g][cs, 0:384], p[cs, 0:384], ALU.subtract)
                for g in range(2):
                    p = psn()
                    for hh in range(3):
                        h = g * 3 + hh
                        mm(p[cs, hh * 128:(hh + 1) * 128], Pv[:, h, :], X0[g][cs, hh * 128:(hh + 1) * 128])
                    cp(U[g][cs, 0:384], p[cs, 0:384], eng=act)
                po = psn()
                for h in range(6):
                    mm(po[:, h * C:(h + 1) * C], Sbf_gdn[:, h, :], qe[h][:, cc], start=True, stop=False)
                    mm(po[:, h * C:(h + 1) * C], hv(U, h), ptv(h), start=False, stop=True)
                cp(oT[:, :, cc], po[:, 0:6 * C].re("p (h c) -> p h c", h=6), eng=act)
                pS = [psn(), psn()]
                for h in range(6):
                    mm(pS[h // 3][:, (h % 3) * 128:(h % 3) * 128 + 128], hv(kdt, h), hv(U, h))
                tt(Sgdn[l].v(), Sgdn[l].v(), ebC[:, :, c:c + 1].bc([128, 6, 128]), ALU.mult)
                for g in range(2):
                    tt(Sgdn[l][:, g * 3:g * 3 + 3, :], Sgdn[l][:, g * 3:g * 3 + 3, :],
                       pS[g][:, 0:384].re("p (h e) -> p h e", h=3), ALU.add)
                cp(Sbf_gdn.v(), Sgdn[l].v(), eng=act)
                ffr(P, *X0); wfr(PT, *U, *vbt, *kdt)
            bfr(*qn, *qe, *kn, *kbt, *kbe, *kd, *vb)
            ffr(bT6, b6)
            for h in range(6):
                headnorm_out(oT[:, h, tok], ones_b.v(), 1.0 / 128, gdnn[:, l:l + 1], sgd[h][:, tok], mixT[4 + h][:, tok])
            bfr(*sgd)

            phase(4)
            def shiftmix(ps_v, rows, hcol, mucol, dst_v):
                zb = fa()
                cp(zb[0:rows, 0:1], shist[l][0:rows, hcol:hcol + 1])
                cp(zb[0:rows, 1:1 + NT], ps_v, eng=act)
                d = fa()
                tt(d[0:rows, tok], zb[0:rows, 0:NT], zb[0:rows, 1:1 + NT], ALU.subtract)
                stt(dst_v, d[0:rows, tok], mucol, zb[0:rows, 1:1 + NT], ALU.mult, ALU.add)
                cp(shist[l][0:rows, hcol:hcol + 1], zb[0:rows, NT:NT + 1])
                ffr(zb, d)

            sL = wload(Wl, 0, KT, RW0 + 2304, 256)
            tmpf = fa()
            ps = psn(); dense_fm(sL, 0, 64, KT, hr, ps)
            shiftmix(ps[0:64, tok], 64, 18, mu_w[:, l:l + 1], tmpf[0:64, tok])
            twT = ba(); actf(twT[0:64, tok], tmpf[0:64, tok], AF.Tanh)
            ps = psn(); dense_fm(sL, 64, 64, KT, hr, ps)
            xaT = ba(); shiftmix(ps[0:64, tok], 64, 19, mu_a[:, l:l + 1], xaT[0:64, tok])
            ps = psn(); dense_fm(sL, 128, 128, KT, hr, ps)
            shiftmix(ps[:, tok], 128, 20, mu_g[:, l:l + 1], tmpf[:, tok])
            sgT = ba(); actf(sgT[:, tok], tmpf[:, tok], AF.Sigmoid)
            ffr(tmpf)
            rT, kT_, vT = [], [], []
            slot = None
            for i in range(18):
                if i % 4 == 0:
                    slot = wload(Wl, 0, KT, RW0 + i * 128, min(512, 2304 - i * 128))
                ps = psn(); dense_fm(slot, (i % 4) * 128, 128, KT, hr, ps)
                z = ba()
                shiftmix(ps[:, tok], 128, i, mu_rkv[:, l, i:i + 1], z[:, tok])
                (rT if i < 6 else kT_ if i < 12 else vT).append(z)
            if l == 0:
                for j in range(6):
                    cp(vfirst[:, j, tok], vT[j][:, tok])
            else:
                ps = psn()
                for j in range(6):
                    mm(ps[0:32, tok], v_dn_b[:, l - 1, j, :], vT[j][:, tok], start=(j == 0), stop=(j == 5))
                t1 = ba(); cp(t1[0:32, tok], ps[0:32, tok], eng=act)
                for j in range(6):
                    ps = psn(); mm(ps[:, tok], v_up_b[0:32, l - 1, j * 128:(j + 1) * 128], t1[0:32, tok])
                    nu = fa(); actf(nu[:, tok], ps[:, tok], AF.Sigmoid, bias=v0[:, l - 1, j:j + 1])
                    d = fa()
                    tt(d[:, tok], vfirst[:, j, tok], vT[j][:, tok], ALU.subtract)
                    tt(d[:, tok], d[:, tok], nu[:, tok], ALU.mult)
                    tt(vT[j][:, tok], vT[j][:, tok], d[:, tok], ALU.add)
                    ffr(nu, d)
                bfr(t1)
            rt, kt_, at, kpt, kdl, adl, bonus, gate = [], [], [], [], [], [], [], []
            for j in range(6):
                ps = psn(); mm(ps[:, tok], w_up_b[0:64, l, j * 128:(j + 1) * 128], twT[0:64, tok])
                sig = fa(); actf(sig[:, tok], ps[:, tok], AF.Sigmoid, bias=w0[:, l, j:j + 1])
                csm = fa(); scan(csm[:, tok], notstart[:, tok], sig[:, tok])
                ps = psn(); mm(ps[:, tok], a_upr_b[0:64, l, j * 128:(j + 1) * 128], xaT[0:64, tok])
                aa = fa(); actf(aa[:, tok], ps[:, tok], AF.Sigmoid, bias=a0[:, l, j:j + 1])
                ps = psn(); mm(ps[:, tok], g_up_b[:, l, j * 128:(j + 1) * 128], sgT[:, tok])
                g_ = ba(); cp(g_[:, tok], ps[:, tok], eng=act); gate.append(g_)
                kr = fa()
                ts(kr[:, tok], kT_[j][:, tok], kk_c[:, l, j:j + 1], ALU.mult)
                rn = fa()
                pnorm(kr[:, tok], bd64_b.v(), 1.0, 1e-6, rn[:, tok])
                kk = kr
                tt(kk[:, tok], kr[:, tok], rn[:, tok], ALU.mult)
                ka = rn
                tt(ka[:, tok], kk[:, tok], aa[:, tok], ALU.mult)
                k2 = fa()
                ts(k2[:, tok], aa[:, tok], ka_c[:, l, j:j + 1], ALU.mult, omka[:, l, j:j + 1], ALU.add)
                tt(k2[:, tok], k2[:, tok], kT_[j][:, tok], ALU.mult)
                rk = ba()
                stt(rk[:, tok], rT[j][:, tok], rk_c[:, l, j:j + 1], k2[:, tok], ALU.mult, ALU.mult)
                ps = psn(); mm(ps[:, tok], bd64_b.v(), rk[:, tok])
                bo = ba(); tt(bo[:, tok], ps[:, tok], vT[j][:, tok], ALU.mult); bonus.append(bo)
                bfr(rk)
                e = fa()
                actf(Eplus[:, j, tok], csm[:, tok], AF.Exp, scale=-K0)
                c3 = csm[:, tok].re("p (n c) -> p n c", c=C)
                actf(ebC[:, j, 0:NCH], c3[:, :, C - 1], AF.Exp, scale=-K0)
                for i_ in range(2):
                    q0 = i_ * 64
                    a_ = ba()
                    tt(a_[q0:q0 + 64, tok], rT[j][q0:q0 + 64, tok], Eplus[q0:q0 + 64, j, tok], ALU.mult)
                    memset(a_[64 - q0:128 - q0, tok], 0.0)
                    rt.append(a_)
                actf(e[:, tok], csm[:, tok], AF.Exp, scale=K0)
                a_ = ba(); tt(a_[:, tok], k2[:, tok], e[:, tok], ALU.mult); kt_.append(a_)
                a_ = ba(); tt(a_[:, tok], ka[:, tok], e[:, tok], ALU.mult); at.append(a_)
                tt(e[:, tok], csm[:, tok], sig[:, tok], ALU.subtract)
                actf(e[:, tok], e[:, tok], AF.Exp, scale=-K0)
                for i_ in range(2):
                    q0 = i_ * 64
                    a_ = ba()
                    tt(a_[q0:q0 + 64, tok], kk[q0:q0 + 64, tok], e[q0:q0 + 64, tok], ALU.mult)
                    memset(a_[64 - q0:128 - q0, tok], 0.0)
                    kpt.append(a_)
                tt(e[:, tok].re("p (n c) -> p n c", c=C), c3[:, :, C - 1:C].bc([128, NCH, C]), c3, ALU.subtract)
                actf(e[:, tok], e[:, tok], AF.Exp, scale=-K0)
                a_ = ba(); tt(a_[:, tok], k2[:, tok], e[:, tok], ALU.mult); kdl.append(a_)
                a_ = ba(); tt(a_[:, tok], ka[:, tok], e[:, tok], ALU.mult); adl.append(a_)
                ffr(kr, rn, k2, e, sig, csm, aa)
                bfr(rT[j], kT_[j])
            bfr(twT, xaT, sgT)
            vbl = vT
            cp(Sbf_rw.v(), Srw[l].v(), eng=act)
            for c in range(NCH):
                cc = slice(c * C, (c + 1) * C)
                po = psb[6]; pS = psb[7]
                for g in range(2):
                    def trans3(srcs):
                        p = psn(); pb = p.v()
                        for jj in range(3):
                            mm(pb[cs, jj * 128:(jj + 1) * 128], srcs[g * 3 + jj][:, cc], ident_b.v())
                        o = wa(); cp(o[cs, 0:384], pb[cs, 0:384]); return o
                    adt = trans3(adl); kdt = trans3(kdl); vtk = trans3(vbl)
                    pmat = [psn() for _ in range(5)]
                    pv = [p[cs, 0:6 * C].re("p (h c) -> p h c", h=6) for p in pmat]
                    for hh in range(6):
                        j = g * 3 + hh // 2; r0 = (hh % 2) * 64
                        hg = g * 6 + hh
                        mm(pv[0][:, hh, :], at[j][:, cc], kpt[hg][:, cc])
                        mm(pv[1][:, hh, :], kpt[hg][:, cc], at[j][:, cc])
                        mm(pv[2][:, hh, :], kt_[j][:, cc], kpt[hg][:, cc])
                        mm(pv[3][:, hh, :], at[j][:, cc], rt[hg][:, cc])
                        mm(pv[4][:, hh, :], kt_[j][:, cc], rt[hg][:, cc])
                    Aa = fa(); Aav = Aa[cs, 0:6 * C].re("p (h c) -> p h c", h=6)
                    stt(Aav, pv[0], -1.0, mUs[cs, cs].un(1).bc([C, 6, C]), ALU.mult, ALU.mult)
                    Nn = fa(); Nnv = Nn[cs, 0:6 * C].re("p (h c) -> p h c", h=6)
                    stt(Nnv, pv[1], -1.0, mLs[cs, cs].un(1).bc([C, 6, C]), ALU.mult, ALU.mult)
                    m3 = []
                    for idx, msk in ((2, mUs), (3, mUi), (4, mUi)):
                        o = wa()
                        tt(o[cs, 0:6 * C].re("p (h c) -> p h c", h=6), pv[idx], msk[cs, cs].un(1).bc([C, 6, C]), ALU.mult)
                        m3.append(o)
                    mv = lambda k, hh: m3[k][cs, hh * C:(hh + 1) * C]
                    P, Pv = tform(Nnv, Aav)
                    ffr(Aa, Nn)
                    p = psn()
                    for hh in range(6):
                        j = g * 3 + hh // 2; r0 = (hh % 2) * 64
                        mm(p[cs, hh * 64:(hh + 1) * 64], kpt[g * 6 + hh][:, cc], Sbf_rw[:, j, :], start=True, stop=False)
                        mm(p[cs, hh * 64:(hh + 1) * 64], mv(0, hh), vtk[cs, hh * 64:(hh + 1) * 64], start=False, stop=True)
                    X0 = fa()
                    ts(X0[cs, 0:384], p[cs, 0:384], -1.0, ALU.mult)
                    p = psn()
                    for hh in range(6):
                        mm(p[cs, hh * 64:(hh + 1) * 64], Pv[:, hh, :], X0[cs, hh * 64:(hh + 1) * 64])
                    U = wa(); cp(U[cs, 0:384], p[cs, 0:384], eng=act)
                    for hh in range(6):
                        j = g * 3 + hh // 2; r0 = (hh % 2) * 64
                        ov = po[r0:r0 + 64, j * C:(j + 1) * C]
                        mm(ov, Sbf_rw[:, j, :], rt[g * 6 + hh][:, cc], start=True, stop=False)
                        mm(ov, U[cs, hh * 64:(hh + 1) * 64], mv(1, hh), start=False, stop=False)
                        mm(ov, vtk[cs, hh * 64:(hh + 1) * 64], mv(2, hh), start=False, stop=True)
                    for hh in range(6):
                        j = g * 3 + hh // 2; r0 = (hh % 2) * 64
                        sv = pS[r0:r0 + 64, j * 64:(j + 1) * 64]
                        mm(sv, adt[cs, hh * 64:(hh + 1) * 64], U[cs, hh * 64:(hh + 1) * 64], start=True, stop=False)
                        mm(sv, kdt[cs, hh * 64:(hh + 1) * 64], vtk[cs, hh * 64:(hh + 1) * 64], start=False, stop=True)
                    ffr(P, X0); wfr(U, adt, kdt, vtk, *m3)
                cp(oT[:, :, cc], po[:, 0:6 * C].re("p (h c) -> p h c", h=6), eng=act)
                tt(Srw[l].v(), Srw[l].v(), ebC[:, :, c:c + 1].bc([128, 6, 64]), ALU.mult)
                tt(Srw[l].v(), Srw[l].v(), pS[:, 0:384].re("p (h e) -> p h e", h=6), ALU.add)
                cp(Sbf_rw.v(), Srw[l].v(), eng=act)
            bfr(*rt, *kt_, *at, *kpt, *kdl, *adl, *vbl)
            for j in range(6):
                ob = ba(); cp(ob[:, tok], oT[:, j, tok], eng=act)
                ps = psn(); mm(ps[:, tok], bd64s_b.v(), ob[:, tok])
                cen = fa(); tt(cen[:, tok], oT[:, j, tok], ps[:, tok], ALU.subtract)
                rstd = fa()
                pnorm(cen[:, tok], bd64s_b.v(), 1.0, 64e-5, rstd[:, tok])
                tt(cen[:, tok], cen[:, tok], rstd[:, tok], ALU.mult)
                ts(cen[:, tok], cen[:, tok], lng[:, l, j:j + 1], ALU.mult, lnb[:, l, j:j + 1], ALU.add)
                tt(cen[:, tok], cen[:, tok], bonus[j][:, tok], ALU.add)
                tt(mixT[10 + j][:, tok], cen[:, tok], gate[j][:, tok], ALU.mult)
                ffr(cen, rstd); bfr(ob)
            bfr(*bonus, *gate)

            phase(5)
            for cg in range(4):
                slot = wload(w_out[l], 0, KT, cg * 512, 512)
                for m in range(4):
                    ps = psn(); dense_fm(slot, m * 128, 128, KT, lambda k: mixT[k][:, tok], ps)
                    o = cg * 4 + m
                    tt(xT[o][:, tok], xT[o][:, tok], ps[:, tok], ALU.add)

        def ffn(l):
            phase(6)
            rmsnorm(lambda k: g2[:, l, k:k + 1], hT)
            hm = []
            for cg in range(DFF // 512):
                sg_ = wload(w_gate[l], 0, KT, cg * 512, 512)
                su_ = wload(w_up[l], 0, KT, cg * 512, 512)
                for m in range(4):
                    pg = psn(); dense_fm(sg_, m * 128, 128, KT, lambda k: hT[k][:, tok], pg)
                    pu = psn(); dense_fm(su_, m * 128, 128, KT, lambda k: hT[k][:, tok], pu)
                    s_ = ba(); actf(s_[:, tok], pg[:, tok], AF.Silu)
                    o = ba(); tt(o[:, tok], s_[:, tok], pu[:, tok], ALU.mult)
                    bfr(s_); hm.append(o)
            for cg in range(4):
                pd = [psn() for _ in range(4)]
                for kg in range(4):
                    slot = wload(w_down[l], kg * 1408, 11, cg * 512, 512)
                    for m in range(4):
                        for k in range(11):
                            mm(pd[m][:, tok], wk(slot, k, m * 128, (m + 1) * 128), hm[kg * 11 + k][:, tok],
                               start=(kg == 0 and k == 0), stop=(kg == 3 and k == 10))
                for m in range(4):
                    o = cg * 4 + m
                    tt(xT[o][:, tok], xT[o][:, tok], pd[m][:, tok], ALU.add)
            bfr(*hm)

        for t0 in range(0, T, NT):
            phase(1)
            for s0, rows in _chunks(NT, 128):
                phase(0.2)
                kb.dma(sp, xio.t[0:rows, :], Xd[t0 + s0:t0 + s0 + rows, :], [], [xio])
                for q in range(4):
                    ps = psn()
                    for kk_ in range(4):
                        k = q * 4 + kk_
                        phase(0.5)
                        mm(ps[:, kk_ * rows:(kk_ + 1) * rows], xio[0:rows, k * 128:(k + 1) * 128], ident_f[0:rows, 0:rows])
                    for kk_ in range(4):
                        k = q * 4 + kk_
                        phase(0.8)
                        cp(xT[k][:, s0:s0 + rows], ps[:, kk_ * rows:(kk_ + 1) * rows], eng=(act if kk_ % 2 else dve))
            for l in range(L):
                phase(1.5)
                rmsnorm(lambda k: g1[:, l, k:k + 1], hT)
                mixer(l)
                ffn(l)
            phase(7)
            yT = [fa() for _ in range(4)]
            for q in range(4):
                pass
            ps = psn()
            for k in range(KT):
                sq = ba()
                actf(sq[:, tok], xT[k][:, tok], AF.Square)
                mm(ps[:, tok], ones_b.v(), sq[:, tok], start=(k == 0), stop=(k == KT - 1))
                bfr(sq)
            rstd = fa()
            actf(rstd[:, tok], ps[:, tok], AF.Sqrt, scale=1.0 / D, bias=1e-6)
            recip(rstd[:, tok], rstd[:, tok])
            for s0, rows in _chunks(NT, 128):
                for q in range(4):
                    ps = psn()
                    for kk_ in range(4):
                        k = q * 4 + kk_
                        y = yT[kk_]
                        stt(y[:, 0:rows], xT[k][:, s0:s0 + rows], gf[:, k:k + 1], rstd[:, s0:s0 + rows], ALU.mult, ALU.mult)
                        mm(ps[0:rows, kk_ * 128:(kk_ + 1) * 128], y[:, 0:rows], ident_f)
                    cp(xio[0:rows, q * 512:(q + 1) * 512], ps[0:rows, 0:512], eng=(act if q % 2 else dve))
                kb.dma(sp, Yd[t0 + s0:t0 + s0 + rows, :], xio.t[0:rows, :], [xio], [])
            ffr(rstd, *yT)
        kb.enabled = True
        for l in range(L):
            for h in range(4):
                kb.dma(sp, O_gla[sk][l, h], Sgla[l].t[(h % 2) * 64:(h % 2) * 64 + 64, h // 2, :], [Sgla[l]], [])
            kb.dma(sp, O_gdn[sk][l].rearrange("h d e -> d h e"), Sgdn[l].t[:], [Sgdn[l]], [])
            for h in range(12):
                kb.dma(sp, O_rw[sk][l, h], Srw[l].t[(h % 2) * 64:(h % 2) * 64 + 64, h // 2, :], [Srw[l]], [])
            for t_ in range(3):
                kb.dma(sp, O_conv[sk][l, t_].rearrange("(i p) -> p i", p=128), chist[l].t[:, :, t_], [chist[l]], [], slow=True)
            kb.dma(sp, O_shift[sk][l, 0, 0:2304].rearrange("(i p) -> p i", p=128), shist[l].t[:, 0:18], [shist[l]], [], slow=True)
            kb.dma(sp, O_shift[sk][l, 0, 2304:2368].rearrange("(p o) -> p o", o=1), shist[l].t[0:64, 18:19], [shist[l]], [], slow=True)
            kb.dma(sp, O_shift[sk][l, 0, 2368:2432].rearrange("(p o) -> p o", o=1), shist[l].t[0:64, 19:20], [shist[l]], [], slow=True)
            kb.dma(sp, O_shift[sk][l, 0, 2432:2560].rearrange("(p o) -> p o", o=1), shist[l].t[:, 20:21], [shist[l]], [], slow=True)

    if os.environ.get('KSEQ', 'ps').find('p') >= 0:
        run_seq('p', TP)
    if os.environ.get('KSEQ', 'ps').find('s') >= 0:
        run_seq('s', TS)

    for i, sem in enumerate(sp.dsems):
        n = (sp.dcount - i + len(sp.dsems) - 1) // len(sp.dsems)
        if n > 0:
            kb._need(sp, (sem, 16 * n))

    with nc.Block() as block:
        @block.tensor
        def _(e):
            for f in kb.pe.q:
                f(e)

        @block.scalar
        def _(e):
            for f in kb.act.q:
                f(e)

        @block.vector
        def _(e):
            for f in kb.dve.q:
                f(e)

        @block.gpsimd
        def _(e):
            for f in kb.pool.q:
                f(e)

        @block.sync
        def _(e):
            for f in kb.sp.q:
                f(e)
    es.close()
    return nc, kb


def make_consts():
    c = np.zeros((128, 448), np.float32)
    c[:, 0:128] = np.eye(128)
    c[0:64, 128:192] = 1.0; c[64:128, 192:256] = 1.0
    s = np.arange(64)[:, None]; t = np.arange(64)[None, :]
    c[0:64, 256:320] = (s <= t); c[0:64, 320:384] = (s < t); c[0:64, 384:448] = (t < s)
    sel = np.zeros((6, 6, 128), np.float32)
    for h in range(6):
        sel[h, h, :] = 1.0
    return c, sel.reshape(6, 768)


_CACHE = {}


def run(inputs, TP, TS, L, NTMAX=128, ncores=8, trace=False):
    key = (TP, TS, L, NTMAX)
    if key not in _CACHE:
        _CACHE[key] = build(TP, TS, L, NTMAX)
    nc, kb = _CACHE[key]
    cst, selc = make_consts()
    f = lambda a: np.ascontiguousarray(a, dtype=np.float32)
    shared = {k: f(inputs[k]) for k in (
        'norm1_g', 'w_in', 'gla_a_up', 'gla_a_bias', 'gla_norm_g', 'gdn_conv_w', 'gdn_A_log', 'gdn_dt_bias', 'gdn_norm_g',
        'rw_mu', 'rw_w0', 'rw_w_up', 'rw_a0', 'rw_a_up', 'rw_g_up', 'rw_k_k', 'rw_k_a', 'rw_r_k', 'rw_ln_g', 'rw_ln_b',
        'w_out', 'norm2_g', 'w_ffn_gate', 'w_ffn_up', 'w_ffn_down', 'final_norm_g')}
    for k in ('rw_v0', 'rw_v_down', 'rw_v_up'):
        a = f(inputs[k])
        if a.shape[0] == 0:
            a = np.zeros((1,) + a.shape[1:], np.float32)
        shared[k] = a
    shared['cst'] = cst; shared['selc'] = selc
    in_maps = []
    for c in range(ncores):
        m = dict(shared)
        m['x_p'] = f(inputs['x_prompt'][c]); m['x_s'] = f(inputs['x_sample'][c])
        m['st_gla'] = f(inputs['state_gla'][:, c]); m['st_gdn'] = f(inputs['state_gdn'][:, c])
        m['st_conv'] = f(inputs['cache_gdn_conv'][:, c]); m['st_rw'] = f(inputs['state_rwkv'][:, c])
        m['st_shift'] = f(inputs['cache_rwkv_shift'][:, c])
        in_maps.append(m)
    res = run_bass_kernel_spmd(nc, in_maps, core_ids=list(range(ncores)), trace=trace)
    R = res.results
    st = lambda name: np.stack([R[c][name] for c in range(ncores)], axis=0)
    st1 = lambda name: np.stack([R[c][name] for c in range(ncores)], axis=1)
    outs = (st('y_p'), st('y_s'),
            st1('gla_p'), st1('gdn_p'), st1('conv_p'), st1('rwkv_p'), st1('shift_p'),
            st1('gla_s'), st1('gdn_s'), st1('conv_s'), st1('rwkv_s'), st1('shift_s'))
    return tuple(np.ascontiguousarray(o, dtype=np.float32) for o in outs), res


def kernel(**inputs):
    TP = inputs['x_prompt'].shape[1]; TS = inputs['x_sample'].shape[1]; L = inputs['norm1_g'].shape[0]
    outs, _ = run(inputs, TP, TS, L)
    return outs
```

```python
import numpy as np
from contextlib import ExitStack
import concourse.bass as bass
import concourse.mybir as mybir
from concourse.bass_utils import run_bass_kernel_spmd

F32 = mybir.dt.float32
BF16 = mybir.dt.bfloat16
AF = mybir.ActivationFunctionType
ALU = mybir.AluOpType

D = 2048
KT = 16
DFF = 5632
PROJ = 7196
GLA0, GDN0, RW0 = 0, 1552, 4636
NEG = -30000.0
K0 = float(np.exp(-0.5))


class Eng:
    def __init__(s, name):
        s.name = name; s.sem = None; s.seq = 0; s.q = []; s.seen = {}
        s.dsems = []; s.dcount = 0


class V:
    __slots__ = ('b', 'ap')

    def __init__(s, b, ap):
        s.b = b; s.ap = ap

    def __getitem__(s, i):
        return V(s.b, s.ap[i])

    def bc(s, shape):
        return V(s.b, s.ap.broadcast_to(list(shape)))

    def un(s, d):
        return V(s.b, s.ap.unsqueeze(d))

    def cast(s, dt):
        return V(s.b, s.ap.bitcast(dt))

    def re(s, pat, **kw):
        return V(s.b, s.ap.rearrange(pat, **kw))


class Buf:
    excl = False

    def __init__(s, t):
        s.t = t; s.w = None; s.r = {}

    def __getitem__(s, i):
        return V(s, s.t[i])

    def v(s):
        return V(s, s.t[:])


class KB:
    def __init__(s, nc):
        s.nc = nc
        s.pe, s.act, s.dve, s.pool, s.sp = Eng('pe'), Eng('act'), Eng('dve'), Eng('pool'), Eng('sp')
        s.engs = [s.pe, s.act, s.dve, s.pool, s.sp]
        s.semid = {}

    def _need(s, eng, ev):
        sem, val = ev
        k = id(sem)
        if eng.seen.get(k, 0) < val:
            eng.q.append(lambda e, sem=sem, val=val: e.wait_ge(sem, val))
            eng.seen[k] = val

    def _deps(s, eng, reads, writes):
        for b in reads:
            if b.w is not None:
                s._need(eng, b.w)
            if b.excl:
                for k, ev in b.r.items():
                    if ev[0] is not eng.sem:
                        s._need(eng, ev)
        pe = eng is s.pe
        for b in writes:
            if b.w is not None and not (pe and b.w[0] is eng.sem):
                s._need(eng, b.w)
            for k, ev in b.r.items():
                if not (pe and ev[0] is eng.sem):
                    s._need(eng, ev)

    def _mark(s, ev, reads, writes):
        for b in writes:
            b.w = ev; b.r = {}
        for b in reads:
            if b not in writes:
                b.r[id(ev[0])] = ev

    enabled = True

    def I(s, eng, fn, reads, writes):
        if not s.enabled:
            return
        reads = [x for x in reads if x is not None]
        s._deps(eng, reads, writes)
        eng.seq += 1
        ev = (eng.sem, eng.seq)
        eng.q.append(lambda e, fn=fn, sem=eng.sem: fn(e).then_inc(sem, 1))
        s._mark(ev, reads, writes)

    def dma(s, q, out_ap, in_ap, reads, writes, slow=False):
        if not s.enabled:
            return
        K = len(q.dsems)
        i = q.dcount; q.dcount += 1
        sem = q.dsems[i % K]
        if i >= K:
            s._need(q, (sem, 16 * (i // K)))
        for b in reads:
            if b.w is not None:
                s._need(q, b.w)
        for b in writes:
            if b.w is not None:
                s._need(q, b.w)
            for k, ev in b.r.items():
                s._need(q, ev)
        ev = (sem, 16 * (i // K + 1))
        if slow:
            q.q.append(lambda e, o=out_ap, a=in_ap, sem=sem: e.dma_start(out=o, in_=a, allow_slow_non_contiguous=True).then_inc(sem, 16))
        else:
            q.q.append(lambda e, o=out_ap, a=in_ap, sem=sem: e.dma_start(out=o, in_=a).then_inc(sem, 16))
        s._mark(ev, reads, writes)
        return ev

    def mm(s, out, lhsT, rhs, start=True, stop=True):
        s.I(s.pe, lambda e, o=out.ap, l=lhsT.ap, r=rhs.ap, st=start, sp=stop: e.matmul(o, l, r, start=st, stop=sp),
            [lhsT.b, rhs.b], [out.b])

    def tr(s, out, in_, ident):
        s.I(s.pe, lambda e, o=out.ap, i=in_.ap, d=ident.ap: e.transpose(o, i, d), [in_.b, ident.b], [out.b])

    def actf(s, out, in_, func, scale=1.0, bias=0.0, eng=None):
        rd = [in_.b]
        sc = scale; bi = bias
        if isinstance(scale, V):
            rd.append(scale.b); sc = scale.ap
        if isinstance(bias, V):
            rd.append(bias.b); bi = bias.ap
        s.I(s.act, lambda e, o=out.ap, i=in_.ap, f=func, sc=sc, bi=bi: e.activation(out=o, in_=i, func=f, bias=bi, scale=sc),
            rd, [out.b])

    def tt(s, out, in0, in1, op, eng=None):
        eng = eng or s.dve
        s.I(eng, lambda e, o=out.ap, a=in0.ap, b=in1.ap, op=op: e.tensor_tensor(o, a, b, op), [in0.b, in1.b], [out.b])

    def ts(s, out, in0, s1, op0, s2=None, op1=None, eng=None):
        eng = eng or s.dve
        rd = [in0.b]
        a1 = s1; a2 = s2
        if isinstance(s1, V):
            rd.append(s1.b); a1 = s1.ap
        if isinstance(s2, V):
            rd.append(s2.b); a2 = s2.ap
        if op1 is None:
            s.I(eng, lambda e, o=out.ap, a=in0.ap, a1=a1, op0=op0: e.tensor_scalar(o, a, a1, None, op0), rd, [out.b])
        else:
            s.I(eng, lambda e, o=out.ap, a=in0.ap, a1=a1, a2=a2, op0=op0, op1=op1: e.tensor_scalar(o, a, a1, a2, op0, op1), rd, [out.b])

    def stt(s, out, in0, sc, in1, op0, op1):
        rd = [in0.b, in1.b]
        a = sc
        if isinstance(sc, V):
            rd.append(sc.b); a = sc.ap
        s.I(s.dve, lambda e, o=out.ap, i0=in0.ap, a=a, i1=in1.ap, op0=op0, op1=op1: e.scalar_tensor_tensor(o, i0, a, i1, op0, op1),
            rd, [out.b])

    def scan(s, out, d0, d1):
        s.I(s.dve, lambda e, o=out.ap, a=d0.ap, b=d1.ap: e.tensor_tensor_scan(o, a, b, 0.0, ALU.mult, ALU.add),
            [d0.b, d1.b], [out.b])

    def cp(s, out, in_, eng=None):
        eng = eng or s.dve
        if eng is s.act:
            s.I(eng, lambda e, o=out.ap, i=in_.ap: e.activation(out=o, in_=i, func=AF.Copy), [in_.b], [out.b])
        else:
            s.I(eng, lambda e, o=out.ap, i=in_.ap: e.tensor_copy(o, i), [in_.b], [out.b])

    def recip(s, out, in_):
        s.I(s.dve, lambda e, o=out.ap, i=in_.ap: e.reciprocal(o, i), [in_.b], [out.b])

    def memset(s, out, val, eng=None):
        eng = eng or s.dve
        s.I(eng, lambda e, o=out.ap, v=val: e.memset(o, v), [], [out.b])


def _chunks(n, m):
    return [(i, min(m, n - i)) for i in range(0, n, m)]


def build(TP, TS, L, NTMAX=256):
    import os
    STOP = float(os.environ.get('KSTOP', '99'))
    nc = bass.Bass("TRN2", target_bir_lowering=False)
    kb = KB(nc)

    def phase(n):
        if n > STOP:
            kb.enabled = False
    dt = nc.dram_tensor
    es = ExitStack()
    for e_ in kb.engs:
        e_.sem = es.enter_context(nc.semaphore("s_%s" % e_.name))
    kb.sp.dsems = [es.enter_context(nc.semaphore("dsp%d" % i)) for i in range(8)]
    kb.pool.dsems = [es.enter_context(nc.semaphore("dpl%d" % i)) for i in range(8)]

    def din(name, shape):
        return dt(name, list(shape), F32, kind="ExternalInput").ap()

    def dout(name, shape):
        return dt(name, list(shape), F32, kind="ExternalOutput").ap()

    X = {'p': din("x_p", [TP, D]), 's': din("x_s", [TS, D])}
    st_gla = din("st_gla", [L, 4, 64, 128]); st_gdn = din("st_gdn", [L, 6, 128, 128])
    st_conv = din("st_conv", [L, 3, 2304]); st_rw = din("st_rw", [L, 12, 64, 64]); st_shift = din("st_shift", [L, 1, 2560])
    norm1_g = din("norm1_g", [L, D]); w_in = din("w_in", [L, D, PROJ])
    gla_a_up = din("gla_a_up", [L, 16, 256]); gla_a_bias = din("gla_a_bias", [L, 256]); gla_norm_g = din("gla_norm_g", [L, 128])
    gdn_conv_w = din("gdn_conv_w", [L, 4, 2304]); gdn_A_log = din("gdn_A_log", [L, 6]); gdn_dt_bias = din("gdn_dt_bias", [L, 6])
    gdn_norm_g = din("gdn_norm_g", [L, 128])
    rw_mu = din("rw_mu", [L, 2560]); rw_w0 = din("rw_w0", [L, 768]); rw_w_up = din("rw_w_up", [L, 64, 768])
    rw_a0 = din("rw_a0", [L, 768]); rw_a_up = din("rw_a_up", [L, 64, 768])
    LV = max(L - 1, 1)
    rw_v0 = din("rw_v0", [LV, 768]); rw_v_down = din("rw_v_down", [LV, 768, 32]); rw_v_up = din("rw_v_up", [LV, 32, 768])
    rw_g_up = din("rw_g_up", [L, 128, 768]); rw_k_k = din("rw_k_k", [L, 768]); rw_k_a = din("rw_k_a", [L, 768])
    rw_r_k = din("rw_r_k", [L, 12, 64]); rw_ln_g = din("rw_ln_g", [L, 768]); rw_ln_b = din("rw_ln_b", [L, 768])
    w_out = din("w_out", [L, D, D]); norm2_g = din("norm2_g", [L, D])
    w_gate = din("w_ffn_gate", [L, D, DFF]); w_up = din("w_ffn_up", [L, D, DFF]); w_down = din("w_ffn_down", [L, DFF, D])
    final_g = din("final_norm_g", [D])
    cst = din("cst", [128, 448])
    Y = {'p': dout("y_p", [TP, D]), 's': dout("y_s", [TS, D])}
    O_gla = {k: dout("gla_" + k, [L, 4, 64, 128]) for k in 'ps'}
    O_gdn = {k: dout("gdn_" + k, [L, 6, 128, 128]) for k in 'ps'}
    O_conv = {k: dout("conv_" + k, [L, 3, 2304]) for k in 'ps'}
    O_rw = {k: dout("rwkv_" + k, [L, 12, 64, 64]) for k in 'ps'}
    O_shift = {k: dout("shift_" + k, [L, 1, 2560]) for k in 'ps'}

    NT0 = min(NTMAX, TP)
    NTW = NT0 + 4

    def sb(name, shape, dtype=F32):
        return Buf(nc.alloc_sbuf_tensor(name, list(shape), dtype))

    xT = [sb("xT%d" % k, [128, NT0]) for k in range(KT)]
    hT = [sb("hT%d" % k, [128, NT0], BF16) for k in range(KT)]
    mixT = [sb("mx%d" % k, [128, NT0], BF16) for k in range(KT)]
    xio = sb("xio", [128, 512])
    NF, NB, NW = 18, 74, 14
    FW = max(NTW, 384)
    fpool = [sb("fp%d" % i, [128, FW]) for i in range(NF)]
    bpool = [sb("bp%d" % i, [128, NT0], BF16) for i in range(NB)]
    wpool = [sb("wp%d" % i, [128, 512], BF16) for i in range(NW)]
    ffree, bfree, wfree = list(range(NF)), list(range(NB)), list(range(NW))

    def wa():
        return wpool[wfree.pop(0)]

    def wfr(*bs):
        for b in bs:
            wfree.append(wpool.index(b))

    def fa():
        return fpool[ffree.pop(0)]

    def ba():
        return bpool[bfree.pop(0)]

    def ffr(*bs):
        for b in bs:
            ffree.append(fpool.index(b))

    def bfr(*bs):
        for b in bs:
            bfree.append(bpool.index(b))

    NSLOT = 2
    wslots = [[sb("w%d_%d" % (i, q), [128, 4, 512], BF16) for q in range(4)] for i in range(NSLOT)]
    wctr = [0]
    Sgla = [sb("Sgla%d" % l, [128, 2, 128]) for l in range(L)]
    Sgdn = [sb("Sgdn%d" % l, [128, 6, 128]) for l in range(L)]
    Srw = [sb("Srw%d" % l, [128, 6, 64]) for l in range(L)]
    chist = [sb("chist%d" % l, [128, 18, 3]) for l in range(L)]
    shist = [sb("shist%d" % l, [128, 21]) for l in range(L)]
    Sbf_gla = sb("Sbf_gla", [128, 2, 128], BF16); Sbf_gdn = sb("Sbf_gdn", [128, 6, 128], BF16); Sbf_rw = sb("Sbf_rw", [128, 6, 64], BF16)
    oT = sb("oT", [128, 6, NT0])
    browall = sb("browall", [128, 6, NT0]); ebrow = sb("ebrow", [128, 6, NT0], BF16); betarow = sb("betarow", [128, 6, NT0], BF16)
    vfirst = sb("vfirst", [128, 6, NT0], BF16)
    NCHM = max(NT0 // 64, 1)
    ebC = sb("ebC", [128, 6, NCHM])
    cstb = sb("cstb", [128, 448])
    ident_f = cstb[:, 0:128]; bd64_f = cstb[:, 128:256]
    mUi = cstb[0:64, 256:320]; mUs = cstb[0:64, 320:384]; mLs = cstb[0:64, 384:448]
    ones6 = sb("ones6", [6, 128])
    ident_b = sb("ident_b", [128, 128], BF16); ones_b = sb("ones_b", [128, 128], BF16)
    bd64_b = sb("bd64_b", [128, 128], BF16); bd64s_b = sb("bd64s_b", [128, 128], BF16)
    nUi = sb("nUi", [64, 64]); nLs = sb("nLs", [64, 64])
    notstart = sb("notstart", [128, NT0])
    g1 = sb("g1", [128, L, 16]); g2 = sb("g2", [128, L, 16]); gf = sb("gf", [128, 16])
    a_up_b = sb("a_up_b", [16, 256], BF16); nabias = sb("nabias", [128, L, 2]); glan = sb("glan", [128, L])
    cw = sb("cw", [128, L, 4, 18]); negA = sb("negA", [6, L]); dtb = sb("dtb", [6, L]); gdnn = sb("gdnn", [128, L])
    mu_rkv = sb("mu_rkv", [128, L, 18]); mu_w = sb("mu_w", [64, L]); mu_a = sb("mu_a", [64, L]); mu_g = sb("mu_g", [128, L])
    w0 = sb("w0", [128, L, 6]); a0 = sb("a0", [128, L, 6]); v0 = sb("v0", [128, LV, 6])
    kk_c = sb("kk_c", [128, L, 6]); ka_c = sb("ka_c", [128, L, 6]); omka = sb("omka", [128, L, 6]); rk_c = sb("rk_c", [128, L, 6])
    lng = sb("lng", [128, L, 6]); lnb = sb("lnb", [128, L, 6])
    w_up_b = sb("w_up_b", [64, 768], BF16); a_upr_b = sb("a_upr_b", [64, 768], BF16)
    g_up_b = sb("g_up_b", [128, 768], BF16); v_up_b = sb("v_up_b", [32, 768], BF16); v_dn_b = sb("v_dn_b", [128, 6, 32], BF16)
    psb = [Buf(nc.alloc_psum_tensor("ps%d" % i, [128, 512], F32)) for i in range(8)]
    for b_ in psb:
        b_.excl = True
    pctr = [0]

    def psn():
        b = psb[pctr[0] % 6]; pctr[0] += 1
        return b

    act, dve, pool, sp = kb.act, kb.dve, kb.pool, kb.sp
    mm, tr, actf, tt, ts, stt, scan, cp, recip, memset = kb.mm, kb.tr, kb.actf, kb.tt, kb.ts, kb.stt, kb.scan, kb.cp, kb.recip, kb.memset

    def ld(dst, src_ap, q=sp, slow=True):
        kb.dma(q, dst.ap, src_ap, [], [dst.b], slow=slow)

    ld(cstb.v(), cst, slow=False)
    memset(ones6.v(), 1.0)
    cp(ident_b.v(), ident_f); memset(ones_b.v(), 1.0); cp(bd64_b.v(), bd64_f)
    ts(bd64s_b.v(), bd64_f, 1.0 / 64, ALU.mult)
    ts(nUi.v(), mUi, -1.0, ALU.add, -NEG, ALU.mult)
    ts(nLs.v(), mLs, -1.0, ALU.add, -NEG, ALU.mult)

    def pcol(dst, src, pat, **kw):
        ld(dst, src.rearrange(pat, **kw))

    for l in range(L):
        pcol(g1[:, l, :], norm1_g[l], "(k p) -> p k", p=128)
        pcol(g2[:, l, :], norm2_g[l], "(k p) -> p k", p=128)
        pcol(nabias[:, l, :], gla_a_bias[l], "(j p) -> p j", p=128)
        pcol(glan[:, l:l + 1], gla_norm_g[l], "(p o) -> p o", o=1)
        pcol(gdnn[:, l:l + 1], gdn_norm_g[l], "(p o) -> p o", o=1)
        for j in range(4):
            pcol(cw[:, l, j, :], gdn_conv_w[l, j], "(i p) -> p i", p=128)
        pcol(negA[:, l:l + 1], gdn_A_log[l], "(p o) -> p o", o=1)
        pcol(dtb[:, l:l + 1], gdn_dt_bias[l], "(p o) -> p o", o=1)
        pcol(mu_rkv[:, l, :], rw_mu[l, 0:2304], "(i p) -> p i", p=128)
        pcol(mu_w[:, l:l + 1], rw_mu[l, 2304:2368], "(p o) -> p o", o=1)
        pcol(mu_a[:, l:l + 1], rw_mu[l, 2368:2432], "(p o) -> p o", o=1)
        pcol(mu_g[:, l:l + 1], rw_mu[l, 2432:2560], "(p o) -> p o", o=1)
        pcol(w0[:, l, :], rw_w0[l], "(i p) -> p i", p=128)
        pcol(a0[:, l, :], rw_a0[l], "(i p) -> p i", p=128)
        pcol(kk_c[:, l, :], rw_k_k[l], "(i p) -> p i", p=128)
        pcol(ka_c[:, l, :], rw_k_a[l], "(i p) -> p i", p=128)
        pcol(rk_c[:, l, :], rw_r_k[l], "(j i) d -> (i d) j", i=2)
        pcol(lng[:, l, :], rw_ln_g[l], "(i p) -> p i", p=128)
        pcol(lnb[:, l, :], rw_ln_b[l], "(i p) -> p i", p=128)
    for l in range(L - 1):
        pcol(v0[:, l, :], rw_v0[l], "(i p) -> p i", p=128)
    pcol(gf.v(), final_g, "(k p) -> p k", p=128)
    ts(nabias.v(), nabias.v(), -1.0, ALU.mult)
    actf(negA.v(), negA.v(), AF.Exp)
    ts(negA.v(), negA.v(), -1.0, ALU.mult)
    ts(omka.v(), ka_c.v(), -1.0, ALU.mult, 1.0, ALU.add)

    def wload(src2d, r0, kt, c0, ncols):
        slot = wslots[wctr[0] % NSLOT]; wctr[0] += 1
        for q, (k0, kn) in enumerate(_chunks(kt, 4)):
            src = src2d[r0 + k0 * 128: r0 + (k0 + kn) * 128, c0:c0 + ncols].rearrange("(k p) n -> p k n", p=128)
            kb.dma(pool, slot[q].t[:, 0:kn, 0:ncols], src, [], [slot[q]])
        return slot

    def wk(slot, k, c0, c1):
        return slot[k // 4][:, k % 4, c0:c1]

    def run_seq(sk, T):
        NT = min(NT0, T)
        C = min(64, T)
        NCH = NT // C
        NLEV = int(np.log2(C))
        Xd, Yd = X[sk], Y[sk]
        cs = slice(0, C)
        memset(notstart.v(), 1.0)
        memset(notstart[:, 0:NT].re("p (n c) -> p n c", c=C)[:, :, 0:1], 0.0)
        for l in range(L):
            if sk == 'p':
                for b in (Sgla[l], Sgdn[l], Srw[l], chist[l], shist[l]):
                    memset(b.v(), 0.0)
            else:
                for h in range(4):
                    ld(Sgla[l][(h % 2) * 64:(h % 2) * 64 + 64, h // 2, :], st_gla[l, h], slow=False)
                ld(Sgdn[l].v(), st_gdn[l].rearrange("h d e -> d h e"), slow=False)
                for h in range(12):
                    ld(Srw[l][(h % 2) * 64:(h % 2) * 64 + 64, h // 2, :], st_rw[l, h], slow=False)
                for t_ in range(3):
                    ld(chist[l][:, :, t_], st_conv[l, t_].rearrange("(i p) -> p i", p=128))
                ld(shist[l][:, 0:18], st_shift[l, 0, 0:2304].rearrange("(i p) -> p i", p=128))
                ld(shist[l][0:64, 18:19], st_shift[l, 0, 2304:2368].rearrange("(p o) -> p o", o=1))
                ld(shist[l][0:64, 19:20], st_shift[l, 0, 2368:2432].rearrange("(p o) -> p o", o=1))
                ld(shist[l][:, 20:21], st_shift[l, 0, 2432:2560].rearrange("(p o) -> p o", o=1))

        tok = slice(0, NT)

        def rmsnorm(gcol, outs):
            ps = psn()
            for k in range(KT):
                sq = ba()
                actf(sq[:, tok], xT[k][:, tok], AF.Square)
                mm(ps[:, tok], ones_b.v(), sq[:, tok], start=(k == 0), stop=(k == KT - 1))
                bfr(sq)
            rstd = fa()
            actf(rstd[:, tok], ps[:, tok], AF.Sqrt, scale=1.0 / D, bias=1e-6)
            recip(rstd[:, tok], rstd[:, tok])
            for k in range(KT):
                stt(outs[k][:, tok], xT[k][:, tok], gcol(k), rstd[:, tok], ALU.mult, ALU.mult)
            ffr(rstd)

        def dense_fm(slot, c0, width, kt, rhs, ps, prow=128):
            for k in range(kt):
                mm(ps[0:width, tok], wk(slot, k, c0, c0 + width), rhs(k), start=(k == 0), stop=(k == kt - 1))

        def pnorm(src_v, ones_v, scale, bias, dst):
            sq = ba()
            actf(sq[:, tok], src_v, AF.Square)
            ps = psn()
            mm(ps[:, tok], ones_v, sq[:, tok])
            actf(dst, ps[:, tok], AF.Sqrt, scale=scale, bias=bias)
            recip(dst, dst)
            bfr(sq)

        def tform_gen(Nn, Aa):
            P = fa()
            Pv = P[cs, 0:6 * C].re("p (h c) -> p h c", h=6)
            tt(Pv, Aa, ident_f[cs, cs].un(1).bc([C, 6, C]), ALU.add)
            curN, curA = Nn, Aa
            prev = []
            v6 = lambda b_: b_[cs, 0:6 * C].re("p (h c) -> p h c", h=6)
            for lev in range(1, NLEV):
                psN = psn(); pv = v6(psN)
                for h in range(6):
                    mm(pv[:, h, :], curA[:, h, :], curN[:, h, :])
                nA = None; nAv = None
                if lev < NLEV - 1:
                    psA = psn(); pa = v6(psA)
                    for h in range(6):
                        mm(pa[:, h, :], curN[:, h, :], curA[:, h, :])
                nN = fa(); nNv = v6(nN)
                cp(nNv, pv, eng=act)
                if lev < NLEV - 1:
                    nA = fa(); nAv = v6(nA)
                    cp(nAv, pa, eng=dve)
                yield
                psP = psn(); pp = v6(psP)
                for h in range(6):
                    mm(pp[:, h, :], nNv[:, h, :], Pv[:, h, :])
                tt(Pv, Pv, pp, ALU.add)
                curN, curA = nNv, nAv
                if prev:
                    ffr(*prev)
                prev = [x for x in (nN, nA) if x is not None]
                yield
            if prev:
                ffr(*prev)
            return P, Pv

        def run_gen(g_):
            try:
                while True:
                    next(g_)
            except StopIteration as e_:
                return e_.value

        def tform(Nn, Aa):
            return run_gen(tform_gen(Nn, Aa))

        def interleave(gens):
            gens = list(gens)
            while gens:
                for g_ in list(gens):
                    try:
                        next(g_)
                    except StopIteration:
                        gens.remove(g_)

        def headnorm_out(o_v, ones_v, nrm_scale, gcol, gate_b, dst):
            rstd = fa()
            pnorm(o_v, ones_v, nrm_scale, 1e-6, rstd[:, tok])
            t1 = fa()
            stt(t1[:, tok], o_v, gcol, rstd[:, tok], ALU.mult, ALU.mult)
            tt(dst, t1[:, tok], gate_b, ALU.mult)
            ffr(rstd, t1)

        def mixer(l):
            Wl = w_in[l]
            hr = lambda k: hT[k][:, tok]
            ld(a_up_b.v(), gla_a_up[l], q=pool, slow=False)
            ld(w_up_b.v(), rw_w_up[l], q=pool, slow=False)
            ld(a_upr_b.v(), rw_a_up[l], q=pool, slow=False)
            ld(g_up_b.v(), rw_g_up[l], q=pool, slow=False)
            if l > 0:
                ld(v_up_b.v(), rw_v_up[l - 1], q=pool, slow=False)
                ld(v_dn_b.v(), rw_v_down[l - 1].rearrange("(k p) r -> p k r", p=128), q=pool, slow=False)
            phase(2)
            s1 = wload(Wl, 0, KT, GLA0 + 1024, 272)
            ps = psn(); dense_fm(s1, 0, 16, KT, hr, ps)
            aT = ba(); cp(aT[0:16, tok], ps[0:16, tok], eng=act)
            sg = []
            s2 = None
            for i in range(4):
                if i == 2:
                    s2 = wload(Wl, 0, KT, GLA0 + 1296, 256)
                ps = psn()
                dense_fm(s1 if i < 2 else s2, (16 + i * 128) if i < 2 else (i - 2) * 128, 128, KT, hr, ps)
                g = ba(); actf(g[:, tok], ps[:, tok], AF.Silu); sg.append(g)
            phase(2.1)
            EP, EM, ED = [], [], []
            for j in range(2):
                ps = psn()
                mm(ps[:, tok], a_up_b[0:16, j * 128:(j + 1) * 128], aT[0:16, tok])
                e = fa()
                actf(e[:, tok], ps[:, tok], AF.Exp, scale=-1.0, bias=nabias[:, l, j:j + 1])
                actf(e[:, tok], e[:, tok], AF.Ln, bias=1.0)
                csm = fa()
                scan(csm[:, tok], notstart[:, tok], e[:, tok])
                ep = fa(); em = fa(); ed = fa()
                actf(ep[:, tok], csm[:, tok], AF.Exp, scale=-1.0 / 16, bias=float(np.log(0.125)))
                actf(em[:, tok], csm[:, tok], AF.Exp, scale=1.0 / 16)
                c3 = csm[:, tok].re("p (n c) -> p n c", c=C)
                tt(e[:, tok].re("p (n c) -> p n c", c=C), c3[:, :, C - 1:C].bc([128, NCH, C]), c3, ALU.subtract)
                actf(ed[:, tok], e[:, tok], AF.Exp, scale=-1.0 / 16)
                actf(ebC[:, j, 0:NCH], c3[:, :, C - 1], AF.Exp, scale=-1.0 / 16)
                EP.append(ep); EM.append(em); ED.append(ed)
                ffr(e, csm)
            bfr(aT)
            phase(2.2)
            qe, ke, kd = [], [], []
            s3 = wload(Wl, 0, KT, GLA0 + 0, 512)
            for j in range(2):
                ps = psn(); dense_fm(s3, j * 128, 128, KT, hr, ps)
                for i in range(2):
                    r0 = i * 64
                    q = ba()
                    tt(q[r0:r0 + 64, tok], ps[r0:r0 + 64, tok], EP[j][r0:r0 + 64, tok], ALU.mult)
                    memset(q[64 - r0:128 - r0, tok], 0.0)
                    qe.append(q)
            for j in range(2):
                ps = psn(); dense_fm(s3, 256 + j * 128, 128, KT, hr, ps)
                k1 = ba(); tt(k1[:, tok], ps[:, tok], EM[j][:, tok], ALU.mult); ke.append(k1)
                k2 = ba(); tt(k2[:, tok], ps[:, tok], ED[j][:, tok], ALU.mult); kd.append(k2)
            ffr(*EP, *EM, *ED)
            phase(2.3)
            s4 = wload(Wl, 0, KT, GLA0 + 512, 512)
            cp(Sbf_gla.v(), Sgla[l].v(), eng=act)
            for c in range(NCH):
                cc = slice(c * C, (c + 1) * C)
                ps = psn()
                for k in range(KT):
                    mm(ps[cs, 0:512], hT[k][:, cc], wk(s4, k, 0, 512), start=(k == 0), stop=(k == KT - 1))
                vt = wa()
                cp(vt[cs, 0:512], ps[cs, 0:512], eng=act)
                vtv = lambda h: vt[cs, h * 128:(h + 1) * 128]
                pst = psn()
                ptb = pst.v()
                for j in range(2):
                    mm(ptb[cs, j * 128:(j + 1) * 128], kd[j][:, cc], ident_b.v())
                kdt = wa()
                cp(kdt[cs, 0:256], ptb[cs, 0:256])
                pss = psn()
                for h in range(4):
                    r0 = (h % 2) * 64
                    mm(pss[cs, h * C:(h + 1) * C], ke[h // 2][:, cc], qe[h][:, cc])
                scm = wa()
                tt(scm[cs, 0:4 * C].re("p (h c) -> p h c", h=4), pss[cs, 0:4 * C].re("p (h c) -> p h c", h=4),
                   mUi[cs, cs].un(1).bc([C, 4, C]), ALU.mult)
                po = psn()
                for h in range(4):
                    r0 = (h % 2) * 64
                    mm(po[:, h * C:(h + 1) * C], Sbf_gla[:, h // 2, :], qe[h][:, cc], start=True, stop=False)
                    mm(po[:, h * C:(h + 1) * C], vtv(h), scm[cs, h * C:(h + 1) * C], start=False, stop=True)
                cp(oT[:, 0:4, cc], po[:, 0:4 * C].re("p (h c) -> p h c", h=4), eng=act)
                pS = psn()
                for h in range(4):
                    r0 = (h % 2) * 64
                    mm(pS[r0:r0 + 64, (h // 2) * 128:(h // 2) * 128 + 128], kdt[cs, h * 64:(h + 1) * 64], vtv(h))
                for j in range(2):
                    stt(Sgla[l][:, j, :], Sgla[l][:, j, :], ebC[:, j, c:c + 1], pS[:, j * 128:(j + 1) * 128], ALU.mult, ALU.add)
                cp(Sbf_gla.v(), Sgla[l].v(), eng=act)
                wfr(vt, kdt, scm)
            bfr(*qe, *ke, *kd)
            for h in range(4):
                headnorm_out(oT[:, h, tok], ones_b.v(), 1.0 / 128, glan[:, l:l + 1], sg[h][:, tok], mixT[h][:, tok])
            bfr(*sg)

            phase(3)
            sA = wload(Wl, 0, KT, GDN0 + 2304, 396)
            sB = wload(Wl, 0, KT, GDN0 + 2304 + 396, 384)
            ps = psn(); dense_fm(sA, 0, 6, KT, hr, ps)
            bT6 = fa(); actf(bT6[0:6, tok], ps[0:6, tok], AF.Sigmoid)
            ps = psn(); dense_fm(sA, 6, 6, KT, hr, ps)
            b6 = fa(); e6 = fa()
            actf(e6[0:6, tok], ps[0:6, tok], AF.Exp, bias=dtb[:, l:l + 1])
            actf(e6[0:6, tok], e6[0:6, tok], AF.Ln, bias=1.0)
            ts(e6[0:6, tok], e6[0:6, tok], negA[:, l:l + 1], ALU.mult)
            scan(b6[0:6, tok], notstart[0:6, tok], e6[0:6, tok])
            ffr(e6)
            for h in range(6):
                msk_ = fa()
                ts(msk_[0:6, tok], b6[0:6, tok], ident_f[0:6, h:h + 1], ALU.mult)
                ps = psn()
                mm(ps[:, tok], ones6.v(), msk_[0:6, tok])
                cp(browall[:, h, tok], ps[:, tok], eng=act)
                ts(msk_[0:6, tok], bT6[0:6, tok], ident_f[0:6, h:h + 1], ALU.mult)
                ps = psn()
                mm(ps[:, tok], ones6.v(), msk_[0:6, tok])
                cp(betarow[:, h, tok], ps[:, tok], eng=act)
                ffr(msk_)
            actf(ebrow[:, :, tok], browall[:, :, tok], AF.Exp)
            b4 = browall[:, :, tok].re("p h (n c) -> p h n c", c=C)
            actf(ebC[:, :, 0:NCH], b4[:, :, :, C - 1], AF.Exp)
            sgd = []
            for i in range(6):
                ps = psn()
                if i < 3:
                    dense_fm(sA, 12 + i * 128, 128, KT, hr, ps)
                else:
                    dense_fm(sB, (i - 3) * 128, 128, KT, hr, ps)
                g = ba(); actf(g[:, tok], ps[:, tok], AF.Silu); sgd.append(g)
            qn, qe, kn, kbt, kbe, kd, vb = [], [], [], [], [], [], []
            slot = None
            for i in range(18):
                if i % 4 == 0:
                    slot = wload(Wl, 0, KT, GDN0 + i * 128, min(512, 2304 - i * 128))
                ps = psn(); dense_fm(slot, (i % 4) * 128, 128, KT, hr, ps)
                xc = fa()
                cp(xc[:, 0:3], chist[l][:, i, :])
                cp(xc[:, 3:3 + NT], ps[:, tok], eng=act)
                acc = fa()
                ts(acc[:, tok], xc[:, 0:NT], cw[:, l, 0, i:i + 1], ALU.mult)
                for j in range(1, 4):
                    stt(acc[:, tok], xc[:, j:j + NT], cw[:, l, j, i:i + 1], acc[:, tok], ALU.mult, ALU.add)
                cp(chist[l][:, i, :], xc[:, NT:NT + 3])
                h = i % 6
                if i < 12:
                    sl = fa()
                    actf(sl[:, tok], acc[:, tok], AF.Silu)
                    rn = fa()
                    if i < 6:
                        pnorm(sl[:, tok], ones_b.v(), 128.0, 128e-6, rn[:, tok])
                        a = ba(); tt(a[:, tok], sl[:, tok], rn[:, tok], ALU.mult); qn.append(a)
                        b = ba(); tt(b[:, tok], a[:, tok], ebrow[:, h, tok], ALU.mult); qe.append(b)
                    else:
                        pnorm(sl[:, tok], ones_b.v(), 1.0, 1e-6, rn[:, tok])
                        a = ba(); tt(a[:, tok], sl[:, tok], rn[:, tok], ALU.mult); kn.append(a)
                        b = ba(); tt(b[:, tok], a[:, tok], betarow[:, h, tok], ALU.mult); kbt.append(b)
                        b2 = ba(); tt(b2[:, tok], b[:, tok], ebrow[:, h, tok], ALU.mult); kbe.append(b2)
                        dd = rn
                        tt(dd[:, tok].re("p (n c) -> p n c", c=C), b4[:, h, :, C - 1:C].bc([128, NCH, C]), b4[:, h, :, :], ALU.subtract)
                        actf(dd[:, tok], dd[:, tok], AF.Exp)
                        b3 = ba(); tt(b3[:, tok], a[:, tok], dd[:, tok], ALU.mult); kd.append(b3)
                    ffr(sl, rn)
                else:
                    sl = fa()
                    actf(sl[:, tok], acc[:, tok], AF.Silu)
                    a = ba(); tt(a[:, tok], sl[:, tok], betarow[:, h, tok], ALU.mult); vb.append(a)
                    ffr(sl)
                ffr(xc, acc)
            cp(Sbf_gdn.v(), Sgdn[l].v(), eng=act)
            for c in range(NCH):
                cc = slice(c * C, (c + 1) * C)
                def trans6(srcs):
                    outs = []
                    for g in range(2):
                        p = psn(); pb = p.v()
                        for hh in range(3):
                            mm(pb[cs, hh * 128:(hh + 1) * 128], srcs[g * 3 + hh][:, cc], ident_b.v())
                        o = wa(); cp(o[cs, 0:384], pb[cs, 0:384], eng=(act if g else dve)); outs.append(o)
                    return outs
                vbt = trans6(vb); kdt = trans6(kd)
                hv = lambda lst, h: lst[h // 3][cs, (h % 3) * 128:(h % 3) * 128 + 128]
                p = psn()
                mm(p[cs, 0:6], b6[0:6, cc], ident_f[0:6, 0:6])
                btok = fa(); cp(btok[cs, 0:6], p[cs, 0:6])
                Dm = fa(); Dv = Dm[cs, 0:6 * C].re("p (h c) -> p h c", h=6)
                tt(Dv, browall[cs, :, cc], btok[cs, 0:6].un(2).bc([C, 6, C]), ALU.subtract)
                aT_ = fa(); aTv = aT_[cs, 0:6 * C].re("p (h c) -> p h c", h=6)
                tt(aTv, Dv, nUi[cs, cs].un(1).bc([C, 6, C]), ALU.add)
                actf(aTv, aTv, AF.Exp)
                a2 = fa(); a2v = a2[cs, 0:6 * C].re("p (h c) -> p h c", h=6)
                tt(a2v, nLs[cs, cs].un(1).bc([C, 6, C]), Dv, ALU.subtract)
                actf(a2v, a2v, AF.Exp)
                dTs = Dm
                tt(Dv, aTv, mUs[cs, cs].un(1).bc([C, 6, C]), ALU.mult)
                pk1 = psn(); pk2 = psn(); pq = psn()
                v1 = pk1[cs, 0:6 * C].re("p (h c) -> p h c", h=6)
                v2 = pk2[cs, 0:6 * C].re("p (h c) -> p h c", h=6)
                v3 = pq[cs, 0:6 * C].re("p (h c) -> p h c", h=6)
                for h in range(6):
                    mm(v1[:, h, :], kn[h][:, cc], kbt[h][:, cc])
                    mm(v2[:, h, :], kbt[h][:, cc], kn[h][:, cc])
                    mm(v3[:, h, :], kn[h][:, cc], qn[h][:, cc])
                Aa = fa(); Aav = Aa[cs, 0:6 * C].re("p (h c) -> p h c", h=6)
                stt(Aav, v1, -1.0, Dv, ALU.mult, ALU.mult)
                Nn = fa(); Nnv = Nn[cs, 0:6 * C].re("p (h c) -> p h c", h=6)
                stt(Nnv, v2, -1.0, a2v, ALU.mult, ALU.mult)
                PT = wa(); PTv = PT[cs, 0:6 * C].re("p (h c) -> p h c", h=6)
                tt(PTv, v3, aTv, ALU.mult)
                ptv = lambda h: PT[cs, h * C:(h + 1) * C]
                ffr(Dm, aT_, a2, btok)
                P, Pv = tform(Nnv, Aav)
                ffr(Aa, Nn)
                X0 = [fa(), fa()]; U = [wa(), wa()]
                for g in range(2):
                    p = psn()
                    for hh in range(3):
                        h = g * 3 + hh
                        mm(p[cs, hh * 128:(hh + 1) * 128], kbe[h][:, cc], Sbf_gdn[:, h, :])
                    tt(X0[g][cs, 0:384], vbt[g][cs, 0:384], p[cs, 0:384], ALU.subtract)
                for g in range(2):
                    p = psn()
                    for hh in range(3):
                        h = g * 3 + hh
                        mm(p[cs, hh * 128:(hh + 1) * 128], Pv[:, h, :], X0[g][cs, hh * 128:(hh + 1) * 128])
                    cp(U[g][cs, 0:384], p[cs, 0:384], eng=act)
                po = psn()
                for h in range(6):
                    mm(po[:, h * C:(h + 1) * C], Sbf_gdn[:, h, :], qe[h][:, cc], start=True, stop=False)
                    mm(po[:, h * C:(h + 1) * C], hv(U, h), ptv(h), start=False, stop=True)
                cp(oT[:, :, cc], po[:, 0:6 * C].re("p (h c) -> p h c", h=6), eng=act)
                pS = [psn(), psn()]
                for h in range(6):
                    mm(pS[h // 3][:, (h % 3) * 128:(h % 3) * 128 + 128], hv(kdt, h), hv(U, h))
                tt(Sgdn[l].v(), Sgdn[l].v(), ebC[:, :, c:c + 1].bc([128, 6, 128]), ALU.mult)
                for g in range(2):
                    tt(Sgdn[l][:, g * 3:g * 3 + 3, :], Sgdn[l][:, g * 3:g * 3 + 3, :],
                       pS[g][:, 0:384].re("p (h e) -> p h e", h=3), ALU.add)
                cp(Sbf_gdn.v(), Sgdn[l].v(), eng=act)
                ffr(P, *X0); wfr(PT, *U, *vbt, *kdt)
            bfr(*qn, *qe, *kn, *kbt, *kbe, *kd, *vb)
            ffr(bT6, b6)
            for h in range(6):
                headnorm_out(oT[:, h, tok], ones_b.v(), 1.0 / 128, gdnn[:, l:l + 1], sgd[h][:, tok], mixT[4 + h][:, tok])
            bfr(*sgd)

            phase(4)
            def shiftmix(ps_v, rows, hcol, mucol, dst_v):
                zb = fa()
                cp(zb[0:rows, 0:1], shist[l][0:rows, hcol:hcol + 1])
                cp(zb[0:rows, 1:1 + NT], ps_v, eng=act)
                d = fa()
                tt(d[0:rows, tok], zb[0:rows, 0:NT], zb[0:rows, 1:1 + NT], ALU.subtract)
                stt(dst_v, d[0:rows, tok], mucol, zb[0:rows, 1:1 + NT], ALU.mult, ALU.add)
                cp(shist[l][0:rows, hcol:hcol + 1], zb[0:rows, NT:NT + 1])
                ffr(zb, d)

            sL = wload(Wl, 0, KT, RW0 + 2304, 256)
            tmpf = fa()
            ps = psn(); dense_fm(sL, 0, 64, KT, hr, ps)
            shiftmix(ps[0:64, tok], 64, 18, mu_w[:, l:l + 1], tmpf[0:64, tok])
            twT = ba(); actf(twT[0:64, tok], tmpf[0:64, tok], AF.Tanh)
            ps = psn(); dense_fm(sL, 64, 64, KT, hr, ps)
            xaT = ba(); shiftmix(ps[0:64, tok], 64, 19, mu_a[:, l:l + 1], xaT[0:64, tok])
            ps = psn(); dense_fm(sL, 128, 128, KT, hr, ps)
            shiftmix(ps[:, tok], 128, 20, mu_g[:, l:l + 1], tmpf[:, tok])
            sgT = ba(); actf(sgT[:, tok], tmpf[:, tok], AF.Sigmoid)
            ffr(tmpf)
            rT, kT_, vT = [], [], []
            slot = None
            for i in range(18):
                if i % 4 == 0:
                    slot = wload(Wl, 0, KT, RW0 + i * 128, min(512, 2304 - i * 128))
                ps = psn(); dense_fm(slot, (i % 4) * 128, 128, KT, hr, ps)
                z = ba()
                shiftmix(ps[:, tok], 128, i, mu_rkv[:, l, i:i + 1], z[:, tok])
                (rT if i < 6 else kT_ if i < 12 else vT).append(z)
            if l == 0:
                for j in range(6):
                    cp(vfirst[:, j, tok], vT[j][:, tok])
            else:
                ps = psn()
                for j in range(6):
                    mm(ps[0:32, tok], v_dn_b[:, j, :], vT[j][:, tok], start=(j == 0), stop=(j == 5))
                t1 = ba(); cp(t1[0:32, tok], ps[0:32, tok], eng=act)
                for j in range(6):
                    ps = psn(); mm(ps[:, tok], v_up_b[0:32, j * 128:(j + 1) * 128], t1[0:32, tok])
                    nu = fa(); actf(nu[:, tok], ps[:, tok], AF.Sigmoid, bias=v0[:, l - 1, j:j + 1])
                    d = fa()
                    tt(d[:, tok], vfirst[:, j, tok], vT[j][:, tok], ALU.subtract)
                    tt(d[:, tok], d[:, tok], nu[:, tok], ALU.mult)
                    tt(vT[j][:, tok], vT[j][:, tok], d[:, tok], ALU.add)
                    ffr(nu, d)
                bfr(t1)
            rt, kt_, at, kpt, kdl, adl, bonus, gate = [], [], [], [], [], [], [], []
            for j in range(6):
                ps = psn(); mm(ps[:, tok], w_up_b[0:64, j * 128:(j + 1) * 128], twT[0:64, tok])
                sig = fa(); actf(sig[:, tok], ps[:, tok], AF.Sigmoid, bias=w0[:, l, j:j + 1])
                csm = fa(); scan(csm[:, tok], notstart[:, tok], sig[:, tok])
                ps = psn(); mm(ps[:, tok], a_upr_b[0:64, j * 128:(j + 1) * 128], xaT[0:64, tok])
                aa = fa(); actf(aa[:, tok], ps[:, tok], AF.Sigmoid, bias=a0[:, l, j:j + 1])
                ps = psn(); mm(ps[:, tok], g_up_b[:, j * 128:(j + 1) * 128], sgT[:, tok])
                g_ = ba(); cp(g_[:, tok], ps[:, tok], eng=act); gate.append(g_)
                kr = fa()
                ts(kr[:, tok], kT_[j][:, tok], kk_c[:, l, j:j + 1], ALU.mult)
                rn = fa()
                pnorm(kr[:, tok], bd64_b.v(), 1.0, 1e-6, rn[:, tok])
                kk = kr
                tt(kk[:, tok], kr[:, tok], rn[:, tok], ALU.mult)
                ka = rn
                tt(ka[:, tok], kk[:, tok], aa[:, tok], ALU.mult)
                k2 = fa()
                ts(k2[:, tok], aa[:, tok], ka_c[:, l, j:j + 1], ALU.mult, omka[:, l, j:j + 1], ALU.add)
                tt(k2[:, tok], k2[:, tok], kT_[j][:, tok], ALU.mult)
                rk = ba()
                stt(rk[:, tok], rT[j][:, tok], rk_c[:, l, j:j + 1], k2[:, tok], ALU.mult, ALU.mult)
                ps = psn(); mm(ps[:, tok], bd64_b.v(), rk[:, tok])
                bo = ba(); tt(bo[:, tok], ps[:, tok], vT[j][:, tok], ALU.mult); bonus.append(bo)
                bfr(rk)
                e = fa()
                Epl = ba()
                actf(Epl[:, tok], csm[:, tok], AF.Exp, scale=-K0)
                c3 = csm[:, tok].re("p (n c) -> p n c", c=C)
                actf(ebC[:, j, 0:NCH], c3[:, :, C - 1], AF.Exp, scale=-K0)
                for i_ in range(2):
                    q0 = i_ * 64
                    a_ = ba()
                    tt(a_[q0:q0 + 64, tok], rT[j][q0:q0 + 64, tok], Epl[q0:q0 + 64, tok], ALU.mult)
                    memset(a_[64 - q0:128 - q0, tok], 0.0)
                    rt.append(a_)
                bfr(Epl)
                actf(e[:, tok], csm[:, tok], AF.Exp, scale=K0)
                a_ = ba(); tt(a_[:, tok], k2[:, tok], e[:, tok], ALU.mult); kt_.append(a_)
                a_ = ba(); tt(a_[:, tok], ka[:, tok], e[:, tok], ALU.mult); at.append(a_)
                tt(e[:, tok], csm[:, tok], sig[:, tok], ALU.subtract)
                actf(e[:, tok], e[:, tok], AF.Exp, scale=-K0)
                for i_ in range(2):
                    q0 = i_ * 64
                    a_ = ba()
                    tt(a_[q0:q0 + 64, tok], kk[q0:q0 + 64, tok], e[q0:q0 + 64, tok], ALU.mult)
                    memset(a_[64 - q0:128 - q0, tok], 0.0)
                    kpt.append(a_)
                tt(e[:, tok].re("p (n c) -> p n c", c=C), c3[:, :, C - 1:C].bc([128, NCH, C]), c3, ALU.subtract)
                actf(e[:, tok], e[:, tok], AF.Exp, scale=-K0)
                a_ = ba(); tt(a_[:, tok], k2[:, tok], e[:, tok], ALU.mult); kdl.append(a_)
                a_ = ba(); tt(a_[:, tok], ka[:, tok], e[:, tok], ALU.mult); adl.append(a_)
                ffr(kr, rn, k2, e, sig, csm, aa)
                bfr(rT[j], kT_[j])
            bfr(twT, xaT, sgT)
            vbl = vT
            cp(Sbf_rw.v(), Srw[l].v(), eng=act)

            def rw_group(c, g, po, pS):
                cc = slice(c * C, (c + 1) * C)

                def trans3(srcs, e_):
                    p = psn(); pb = p.v()
                    for jj in range(3):
                        mm(pb[cs, jj * 128:(jj + 1) * 128], srcs[g * 3 + jj][:, cc], ident_b.v())
                    o = wa(); cp(o[cs, 0:384], pb[cs, 0:384], eng=e_); return o
                adt = trans3(adl, dve); kdt = trans3(kdl, act); vtk = trans3(vbl, dve)
                yield
                v6 = lambda b_: b_[cs, 0:6 * C].re("p (h c) -> p h c", h=6)
                Aa = fa(); Nn = fa(); m3 = []
                kinds = ((at, kpt, 0), (kpt, at, 1), (kt_, kpt, 2), (at, rt, 3), (kt_, rt, 4))
                for (la, lb, idx) in kinds:
                    p = psn(); pv = v6(p)
                    for hh in range(6):
                        j = g * 3 + hh // 2; hg = g * 6 + hh
                        x_ = (la[hg] if la in (kpt, rt) else la[j])[:, cc]
                        y_ = (lb[hg] if lb in (kpt, rt) else lb[j])[:, cc]
                        mm(pv[:, hh, :], x_, y_)
                    if idx == 0:
                        stt(v6(Aa), pv, -1.0, mUs[cs, cs].un(1).bc([C, 6, C]), ALU.mult, ALU.mult)
                    elif idx == 1:
                        stt(v6(Nn), pv, -1.0, mLs[cs, cs].un(1).bc([C, 6, C]), ALU.mult, ALU.mult)
                    else:
                        o = wa()
                        msk = mUs if idx == 2 else mUi
                        tt(v6(o), pv, msk[cs, cs].un(1).bc([C, 6, C]), ALU.mult)
                        m3.append(o)
                    if idx == 1:
                        yield
                yield
                mv = lambda k, hh: m3[k][cs, hh * C:(hh + 1) * C]
                P, Pv = yield from tform_gen(v6(Nn), v6(Aa))
                ffr(Aa, Nn)
                p = psn()
                for hh in range(6):
                    j = g * 3 + hh // 2
                    mm(p[cs, hh * 64:(hh + 1) * 64], kpt[g * 6 + hh][:, cc], Sbf_rw[:, j, :], start=True, stop=False)
                    mm(p[cs, hh * 64:(hh + 1) * 64], mv(0, hh), vtk[cs, hh * 64:(hh + 1) * 64], start=False, stop=True)
                X0 = fa()
                ts(X0[cs, 0:384], p[cs, 0:384], -1.0, ALU.mult)
                yield
                p = psn()
                for hh in range(6):
                    mm(p[cs, hh * 64:(hh + 1) * 64], Pv[:, hh, :], X0[cs, hh * 64:(hh + 1) * 64])
                U = wa(); cp(U[cs, 0:384], p[cs, 0:384], eng=act)
                yield
                for hh in range(6):
                    j = g * 3 + hh // 2; r0 = (hh % 2) * 64
                    ov = po[r0:r0 + 64, j * C:(j + 1) * C]
                    mm(ov, Sbf_rw[:, j, :], rt[g * 6 + hh][:, cc], start=True, stop=False)
                    mm(ov, U[cs, hh * 64:(hh + 1) * 64], mv(1, hh), start=False, stop=False)
                    mm(ov, vtk[cs, hh * 64:(hh + 1) * 64], mv(2, hh), start=False, stop=True)
                for hh in range(6):
                    j = g * 3 + hh // 2; r0 = (hh % 2) * 64
                    sv = pS[r0:r0 + 64, j * 64:(j + 1) * 64]
                    mm(sv, adt[cs, hh * 64:(hh + 1) * 64], U[cs, hh * 64:(hh + 1) * 64], start=True, stop=False)
                    mm(sv, kdt[cs, hh * 64:(hh + 1) * 64], vtk[cs, hh * 64:(hh + 1) * 64], start=False, stop=True)
                ffr(P, X0); wfr(U, adt, kdt, vtk, *m3)

            for c in range(NCH):
                cc = slice(c * C, (c + 1) * C)
                po = psb[6]; pS = psb[7]
                interleave([rw_group(c, 0, po, pS), rw_group(c, 1, po, pS)])
                cp(oT[:, :, cc], po[:, 0:6 * C].re("p (h c) -> p h c", h=6), eng=act)
                tt(Srw[l].v(), Srw[l].v(), ebC[:, :, c:c + 1].bc([128, 6, 64]), ALU.mult)
                tt(Srw[l].v(), Srw[l].v(), pS[:, 0:384].re("p (h e) -> p h e", h=6), ALU.add)
                cp(Sbf_rw.v(), Srw[l].v(), eng=act)
            bfr(*rt, *kt_, *at, *kpt, *kdl, *adl, *vbl)
            for j in range(6):
                ob = ba(); cp(ob[:, tok], oT[:, j, tok], eng=act)
                ps = psn(); mm(ps[:, tok], bd64s_b.v(), ob[:, tok])
                cen = fa(); tt(cen[:, tok], oT[:, j, tok], ps[:, tok], ALU.subtract)
                rstd = fa()
                pnorm(cen[:, tok], bd64s_b.v(), 1.0, 64e-5, rstd[:, tok])
                tt(cen[:, tok], cen[:, tok], rstd[:, tok], ALU.mult)
                ts(cen[:, tok], cen[:, tok], lng[:, l, j:j + 1], ALU.mult, lnb[:, l, j:j + 1], ALU.add)
                tt(cen[:, tok], cen[:, tok], bonus[j][:, tok], ALU.add)
                tt(mixT[10 + j][:, tok], cen[:, tok], gate[j][:, tok], ALU.mult)
                ffr(cen, rstd); bfr(ob)
            bfr(*bonus, *gate)

            phase(5)
            for cg in range(4):
                slot = wload(w_out[l], 0, KT, cg * 512, 512)
                for m in range(4):
                    ps = psn(); dense_fm(slot, m * 128, 128, KT, lambda k: mixT[k][:, tok], ps)
                    o = cg * 4 + m
                    tt(xT[o][:, tok], xT[o][:, tok], ps[:, tok], ALU.add)

        def ffn(l):
            phase(6)
            rmsnorm(lambda k: g2[:, l, k:k + 1], hT)
            hm = []
            for cg in range(DFF // 512):
                sg_ = wload(w_gate[l], 0, KT, cg * 512, 512)
                su_ = wload(w_up[l], 0, KT, cg * 512, 512)
                bg = [psn(), psn()]; bu = [psn(), psn()]
                reg = lambda banks, m: banks[m // 2][:, (m % 2) * NT:(m % 2) * NT + NT]
                for m in range(4):
                    for k in range(KT):
                        mm(reg(bg, m), wk(sg_, k, m * 128, (m + 1) * 128), hT[k][:, tok], start=(k == 0), stop=(k == KT - 1))
                for m in range(4):
                    for k in range(KT):
                        mm(reg(bu, m), wk(su_, k, m * 128, (m + 1) * 128), hT[k][:, tok], start=(k == 0), stop=(k == KT - 1))
                for m in range(4):
                    s_ = ba(); actf(s_[:, tok], reg(bg, m), AF.Silu)
                    o = ba(); tt(o[:, tok], s_[:, tok], reg(bu, m), ALU.mult)
                    bfr(s_); hm.append(o)
            for cg in range(4):
                pd = [psn() for _ in range(4)]
                for kg in range(4):
                    slot = wload(w_down[l], kg * 1408, 11, cg * 512, 512)
                    for m in range(4):
                        for k in range(11):
                            mm(pd[m][:, tok], wk(slot, k, m * 128, (m + 1) * 128), hm[kg * 11 + k][:, tok],
                               start=(kg == 0 and k == 0), stop=(kg == 3 and k == 10))
                for m in range(4):
                    o = cg * 4 + m
                    tt(xT[o][:, tok], xT[o][:, tok], pd[m][:, tok], ALU.add)
            bfr(*hm)

        for t0 in range(0, T, NT):
            phase(1)
            for s0, rows in _chunks(NT, 128):
                for q in range(4):
                    phase(0.2)
                    kb.dma(sp, xio.t[0:rows, :], Xd[t0 + s0:t0 + s0 + rows, q * 512:(q + 1) * 512], [], [xio])
                    ps = psn()
                    for kk_ in range(4):
                        k = q * 4 + kk_
                        phase(0.5)
                        mm(ps[:, kk_ * rows:(kk_ + 1) * rows], xio[0:rows, kk_ * 128:(kk_ + 1) * 128], ident_f[0:rows, 0:rows])
                    for kk_ in range(4):
                        k = q * 4 + kk_
                        phase(0.8)
                        cp(xT[k][:, s0:s0 + rows], ps[:, kk_ * rows:(kk_ + 1) * rows], eng=(act if kk_ % 2 else dve))
            for l in range(L):
                phase(1.5)
                rmsnorm(lambda k: g1[:, l, k:k + 1], hT)
                mixer(l)
                ffn(l)
            phase(7)
            yT = [fa() for _ in range(4)]
            for q in range(4):
                pass
            ps = psn()
            for k in range(KT):
                sq = ba()
                actf(sq[:, tok], xT[k][:, tok], AF.Square)
                mm(ps[:, tok], ones_b.v(), sq[:, tok], start=(k == 0), stop=(k == KT - 1))
                bfr(sq)
            rstd = fa()
            actf(rstd[:, tok], ps[:, tok], AF.Sqrt, scale=1.0 / D, bias=1e-6)
            recip(rstd[:, tok], rstd[:, tok])
            for s0, rows in _chunks(NT, 128):
                for q in range(4):
                    ps = psn()
                    for kk_ in range(4):
                        k = q * 4 + kk_
                        y = yT[kk_]
                        stt(y[:, 0:rows], xT[k][:, s0:s0 + rows], gf[:, k:k + 1], rstd[:, s0:s0 + rows], ALU.mult, ALU.mult)
                        mm(ps[0:rows, kk_ * 128:(kk_ + 1) * 128], y[:, 0:rows], ident_f)
                    cp(xio[0:rows, 0:512], ps[0:rows, 0:512], eng=(act if q % 2 else dve))
                    kb.dma(sp, Yd[t0 + s0:t0 + s0 + rows, q * 512:(q + 1) * 512], xio.t[0:rows, :], [xio], [])
            ffr(rstd, *yT)
        kb.enabled = True
        for l in range(L):
            for h in range(4):
                kb.dma(sp, O_gla[sk][l, h], Sgla[l].t[(h % 2) * 64:(h % 2) * 64 + 64, h // 2, :], [Sgla[l]], [])
            kb.dma(sp, O_gdn[sk][l].rearrange("h d e -> d h e"), Sgdn[l].t[:], [Sgdn[l]], [])
            for h in range(12):
                kb.dma(sp, O_rw[sk][l, h], Srw[l].t[(h % 2) * 64:(h % 2) * 64 + 64, h // 2, :], [Srw[l]], [])
            for t_ in range(3):
                kb.dma(sp, O_conv[sk][l, t_].rearrange("(i p) -> p i", p=128), chist[l].t[:, :, t_], [chist[l]], [], slow=True)
            kb.dma(sp, O_shift[sk][l, 0, 0:2304].rearrange("(i p) -> p i", p=128), shist[l].t[:, 0:18], [shist[l]], [], slow=True)
            kb.dma(sp, O_shift[sk][l, 0, 2304:2368].rearrange("(p o) -> p o", o=1), shist[l].t[0:64, 18:19], [shist[l]], [], slow=True)
            kb.dma(sp, O_shift[sk][l, 0, 2368:2432].rearrange("(p o) -> p o", o=1), shist[l].t[0:64, 19:20], [shist[l]], [], slow=True)
            kb.dma(sp, O_shift[sk][l, 0, 2432:2560].rearrange("(p o) -> p o", o=1), shist[l].t[:, 20:21], [shist[l]], [], slow=True)

    if os.environ.get('KSEQ', 'ps').find('p') >= 0:
        run_seq('p', TP)
    if os.environ.get('KSEQ', 'ps').find('s') >= 0:
        run_seq('s', TS)

    for i, sem in enumerate(sp.dsems):
        n = (sp.dcount - i + len(sp.dsems) - 1) // len(sp.dsems)
        if n > 0:
            kb._need(sp, (sem, 16 * n))

    with nc.Block() as block:
        @block.tensor
        def _(e):
            for f in kb.pe.q:
                f(e)

        @block.scalar
        def _(e):
            for f in kb.act.q:
                f(e)

        @block.vector
        def _(e):
            for f in kb.dve.q:
                f(e)

        @block.gpsimd
        def _(e):
            for f in kb.pool.q:
                f(e)

        @block.sync
        def _(e):
            for f in kb.sp.q:
                f(e)
    es.close()
    return nc, kb


def make_consts():
    c = np.zeros((128, 448), np.float32)
    c[:, 0:128] = np.eye(128)
    c[0:64, 128:192] = 1.0; c[64:128, 192:256] = 1.0
    s = np.arange(64)[:, None]; t = np.arange(64)[None, :]
    c[0:64, 256:320] = (s <= t); c[0:64, 320:384] = (s < t); c[0:64, 384:448] = (t < s)
    sel = np.zeros((6, 6, 128), np.float32)
    for h in range(6):
        sel[h, h, :] = 1.0
    return c, sel.reshape(6, 768)


_CACHE = {}


def run(inputs, TP, TS, L, NTMAX=256, ncores=8, trace=False):
    key = (TP, TS, L, NTMAX)
    if key not in _CACHE:
        _CACHE[key] = build(TP, TS, L, NTMAX)
    nc, kb = _CACHE[key]
    cst, selc = make_consts()
    f = lambda a: np.ascontiguousarray(a, dtype=np.float32)
    shared = {k: f(inputs[k]) for k in (
        'norm1_g', 'w_in', 'gla_a_up', 'gla_a_bias', 'gla_norm_g', 'gdn_conv_w', 'gdn_A_log', 'gdn_dt_bias', 'gdn_norm_g',
        'rw_mu', 'rw_w0', 'rw_w_up', 'rw_a0', 'rw_a_up', 'rw_g_up', 'rw_k_k', 'rw_k_a', 'rw_r_k', 'rw_ln_g', 'rw_ln_b',
        'w_out', 'norm2_g', 'w_ffn_gate', 'w_ffn_up', 'w_ffn_down', 'final_norm_g')}
    for k in ('rw_v0', 'rw_v_down', 'rw_v_up'):
        a = f(inputs[k])
        if a.shape[0] == 0:
            a = np.zeros((1,) + a.shape[1:], np.float32)
        shared[k] = a
    shared['cst'] = cst
    in_maps = []
    for c in range(ncores):
        m = dict(shared)
        m['x_p'] = f(inputs['x_prompt'][c]); m['x_s'] = f(inputs['x_sample'][c])
        m['st_gla'] = f(inputs['state_gla'][:, c]); m['st_gdn'] = f(inputs['state_gdn'][:, c])
        m['st_conv'] = f(inputs['cache_gdn_conv'][:, c]); m['st_rw'] = f(inputs['state_rwkv'][:, c])
        m['st_shift'] = f(inputs['cache_rwkv_shift'][:, c])
        in_maps.append(m)
    res = run_bass_kernel_spmd(nc, in_maps, core_ids=list(range(ncores)), trace=trace)
    R = res.results
    st = lambda name: np.stack([R[c][name] for c in range(ncores)], axis=0)
    st1 = lambda name: np.stack([R[c][name] for c in range(ncores)], axis=1)
    outs = (st('y_p'), st('y_s'),
            st1('gla_p'), st1('gdn_p'), st1('conv_p'), st1('rwkv_p'), st1('shift_p'),
            st1('gla_s'), st1('gdn_s'), st1('conv_s'), st1('rwkv_s'), st1('shift_s'))
    return tuple(np.ascontiguousarray(o, dtype=np.float32) for o in outs), res


def kernel(**inputs):
    TP = inputs['x_prompt'].shape[1]; TS = inputs['x_sample'].shape[1]; L = inputs['norm1_g'].shape[0]
    outs, _ = run(inputs, TP, TS, L)
    return outs
```

```python
import numpy as np
from contextlib import ExitStack
import concourse.bass as bass
import concourse.mybir as mybir
from concourse.bass_utils import run_bass_kernel_spmd

F32 = mybir.dt.float32
BF16 = mybir.dt.bfloat16
AF = mybir.ActivationFunctionType
ALU = mybir.AluOpType

D = 2048
KT = 16
DFF = 5632
PROJ = 7196
GLA0, GDN0, RW0 = 0, 1552, 4636
NEG = -30000.0
K0 = float(np.exp(-0.5))


class Eng:
    def __init__(s, name):
        s.name = name; s.sem = None; s.seq = 0; s.q = []; s.seen = {}
        s.dsems = []; s.dcount = 0


class V:
    __slots__ = ('b', 'ap')

    def __init__(s, b, ap):
        s.b = b; s.ap = ap

    def __getitem__(s, i):
        return V(s.b, s.ap[i])

    def bc(s, shape):
        return V(s.b, s.ap.broadcast_to(list(shape)))

    def un(s, d):
        return V(s.b, s.ap.unsqueeze(d))

    def cast(s, dt):
        return V(s.b, s.ap.bitcast(dt))

    def re(s, pat, **kw):
        return V(s.b, s.ap.rearrange(pat, **kw))


class Buf:
    excl = False

    def __init__(s, t):
        s.t = t; s.w = None; s.r = {}

    def __getitem__(s, i):
        return V(s, s.t[i])

    def v(s):
        return V(s, s.t[:])


class KB:
    def __init__(s, nc):
        s.nc = nc
        s.pe, s.act, s.dve, s.pool, s.sp = Eng('pe'), Eng('act'), Eng('dve'), Eng('pool'), Eng('sp')
        s.engs = [s.pe, s.act, s.dve, s.pool, s.sp]
        s.semid = {}

    def _need(s, eng, ev):
        sem, val = ev
        k = id(sem)
        if eng.seen.get(k, 0) < val:
            eng.q.append(lambda e, sem=sem, val=val: e.wait_ge(sem, val))
            eng.seen[k] = val

    def _deps(s, eng, reads, writes):
        for b in reads:
            if b.w is not None:
                s._need(eng, b.w)
            if b.excl:
                for k, ev in b.r.items():
                    if ev[0] is not eng.sem:
                        s._need(eng, ev)
        pe = eng is s.pe
        for b in writes:
            if b.w is not None and not (pe and b.w[0] is eng.sem):
                s._need(eng, b.w)
            for k, ev in b.r.items():
                if not (pe and ev[0] is eng.sem):
                    s._need(eng, ev)

    def _mark(s, ev, reads, writes):
        for b in writes:
            b.w = ev; b.r = {}
        for b in reads:
            if b not in writes:
                b.r[id(ev[0])] = ev

    enabled = True

    def I(s, eng, fn, reads, writes):
        if not s.enabled:
            return
        reads = [x for x in reads if x is not None]
        s._deps(eng, reads, writes)
        eng.seq += 1
        ev = (eng.sem, eng.seq)
        eng.q.append(lambda e, fn=fn, sem=eng.sem: fn(e).then_inc(sem, 1))
        s._mark(ev, reads, writes)

    def dma(s, q, out_ap, in_ap, reads, writes, slow=False):
        if not s.enabled:
            return
        K = len(q.dsems)
        i = q.dcount; q.dcount += 1
        sem = q.dsems[i % K]
        if i >= K:
            s._need(q, (sem, 16 * (i // K)))
        for b in reads:
            if b.w is not None:
                s._need(q, b.w)
        for b in writes:
            if b.w is not None:
                s._need(q, b.w)
            for k, ev in b.r.items():
                s._need(q, ev)
        ev = (sem, 16 * (i // K + 1))
        if slow:
            q.q.append(lambda e, o=out_ap, a=in_ap, sem=sem: e.dma_start(out=o, in_=a, allow_slow_non_contiguous=True).then_inc(sem, 16))
        else:
            q.q.append(lambda e, o=out_ap, a=in_ap, sem=sem: e.dma_start(out=o, in_=a).then_inc(sem, 16))
        s._mark(ev, reads, writes)
        return ev

    def mm(s, out, lhsT, rhs, start=True, stop=True):
        s.I(s.pe, lambda e, o=out.ap, l=lhsT.ap, r=rhs.ap, st=start, sp=stop: e.matmul(o, l, r, start=st, stop=sp),
            [lhsT.b, rhs.b], [out.b])

    def tr(s, out, in_, ident):
        s.I(s.pe, lambda e, o=out.ap, i=in_.ap, d=ident.ap: e.transpose(o, i, d), [in_.b, ident.b], [out.b])

    def actf(s, out, in_, func, scale=1.0, bias=0.0, eng=None):
        rd = [in_.b]
        sc = scale; bi = bias
        if isinstance(scale, V):
            rd.append(scale.b); sc = scale.ap
        if isinstance(bias, V):
            rd.append(bias.b); bi = bias.ap
        s.I(s.act, lambda e, o=out.ap, i=in_.ap, f=func, sc=sc, bi=bi: e.activation(out=o, in_=i, func=f, bias=bi, scale=sc),
            rd, [out.b])

    def tt(s, out, in0, in1, op, eng=None):
        eng = eng or s.dve
        s.I(eng, lambda e, o=out.ap, a=in0.ap, b=in1.ap, op=op: e.tensor_tensor(o, a, b, op), [in0.b, in1.b], [out.b])

    def ts(s, out, in0, s1, op0, s2=None, op1=None, eng=None):
        eng = eng or s.dve
        rd = [in0.b]
        a1 = s1; a2 = s2
        if isinstance(s1, V):
            rd.append(s1.b); a1 = s1.ap
        if isinstance(s2, V):
            rd.append(s2.b); a2 = s2.ap
        if op1 is None:
            s.I(eng, lambda e, o=out.ap, a=in0.ap, a1=a1, op0=op0: e.tensor_scalar(o, a, a1, None, op0), rd, [out.b])
        else:
            s.I(eng, lambda e, o=out.ap, a=in0.ap, a1=a1, a2=a2, op0=op0, op1=op1: e.tensor_scalar(o, a, a1, a2, op0, op1), rd, [out.b])

    def stt(s, out, in0, sc, in1, op0, op1):
        rd = [in0.b, in1.b]
        a = sc
        if isinstance(sc, V):
            rd.append(sc.b); a = sc.ap
        s.I(s.dve, lambda e, o=out.ap, i0=in0.ap, a=a, i1=in1.ap, op0=op0, op1=op1: e.scalar_tensor_tensor(o, i0, a, i1, op0, op1),
            rd, [out.b])

    def scan(s, out, d0, d1):
        s.I(s.dve, lambda e, o=out.ap, a=d0.ap, b=d1.ap: e.tensor_tensor_scan(o, a, b, 0.0, ALU.mult, ALU.add),
            [d0.b, d1.b], [out.b])

    def cp(s, out, in_, eng=None):
        eng = eng or s.dve
        if eng is s.act:
            s.I(eng, lambda e, o=out.ap, i=in_.ap: e.activation(out=o, in_=i, func=AF.Copy), [in_.b], [out.b])
        else:
            s.I(eng, lambda e, o=out.ap, i=in_.ap: e.tensor_copy(o, i), [in_.b], [out.b])

    def recip(s, out, in_):
        s.I(s.dve, lambda e, o=out.ap, i=in_.ap: e.reciprocal(o, i), [in_.b], [out.b])

    def memset(s, out, val, eng=None):
        eng = eng or s.dve
        s.I(eng, lambda e, o=out.ap, v=val: e.memset(o, v), [], [out.b])


def _chunks(n, m):
    return [(i, min(m, n - i)) for i in range(0, n, m)]


def build(TP, TS, L, NTMAX=256):
    import os
    STOP = float(os.environ.get('KSTOP', '99'))
    nc = bass.Bass("TRN2", target_bir_lowering=False)
    kb = KB(nc)

    def phase(n):
        if n > STOP:
            kb.enabled = False
    dt = nc.dram_tensor
    es = ExitStack()
    for e_ in kb.engs:
        e_.sem = es.enter_context(nc.semaphore("s_%s" % e_.name))
    kb.sp.dsems = [es.enter_context(nc.semaphore("dsp%d" % i)) for i in range(8)]
    kb.pool.dsems = [es.enter_context(nc.semaphore("dpl%d" % i)) for i in range(8)]

    def din(name, shape):
        return dt(name, list(shape), F32, kind="ExternalInput").ap()

    def dout(name, shape):
        return dt(name, list(shape), F32, kind="ExternalOutput").ap()

    X = {'p': din("x_p", [TP, D]), 's': din("x_s", [TS, D])}
    st_gla = din("st_gla", [L, 4, 64, 128]); st_gdn = din("st_gdn", [L, 6, 128, 128])
    st_conv = din("st_conv", [L, 3, 2304]); st_rw = din("st_rw", [L, 12, 64, 64]); st_shift = din("st_shift", [L, 1, 2560])
    norm1_g = din("norm1_g", [L, D]); w_in = din("w_in", [L, D, PROJ])
    gla_a_up = din("gla_a_up", [L, 16, 256]); gla_a_bias = din("gla_a_bias", [L, 256]); gla_norm_g = din("gla_norm_g", [L, 128])
    gdn_conv_w = din("gdn_conv_w", [L, 4, 2304]); gdn_A_log = din("gdn_A_log", [L, 6]); gdn_dt_bias = din("gdn_dt_bias", [L, 6])
    gdn_norm_g = din("gdn_norm_g", [L, 128])
    rw_mu = din("rw_mu", [L, 2560]); rw_w0 = din("rw_w0", [L, 768]); rw_w_up = din("rw_w_up", [L, 64, 768])
    rw_a0 = din("rw_a0", [L, 768]); rw_a_up = din("rw_a_up", [L, 64, 768])
    LV = max(L - 1, 1)
    rw_v0 = din("rw_v0", [LV, 768]); rw_v_down = din("rw_v_down", [LV, 768, 32]); rw_v_up = din("rw_v_up", [LV, 32, 768])
    rw_g_up = din("rw_g_up", [L, 128, 768]); rw_k_k = din("rw_k_k", [L, 768]); rw_k_a = din("rw_k_a", [L, 768])
    rw_r_k = din("rw_r_k", [L, 12, 64]); rw_ln_g = din("rw_ln_g", [L, 768]); rw_ln_b = din("rw_ln_b", [L, 768])
    w_out = din("w_out", [L, D, D]); norm2_g = din("norm2_g", [L, D])
    w_gate = din("w_ffn_gate", [L, D, DFF]); w_up = din("w_ffn_up", [L, D, DFF]); w_down = din("w_ffn_down", [L, DFF, D])
    final_g = din("final_norm_g", [D])
    cst = din("cst", [128, 448])
    Y = {'p': dout("y_p", [TP, D]), 's': dout("y_s", [TS, D])}
    O_gla = {k: dout("gla_" + k, [L, 4, 64, 128]) for k in 'ps'}
    O_gdn = {k: dout("gdn_" + k, [L, 6, 128, 128]) for k in 'ps'}
    O_conv = {k: dout("conv_" + k, [L, 3, 2304]) for k in 'ps'}
    O_rw = {k: dout("rwkv_" + k, [L, 12, 64, 64]) for k in 'ps'}
    O_shift = {k: dout("shift_" + k, [L, 1, 2560]) for k in 'ps'}

    NT0 = min(NTMAX, TP)
    NTW = NT0 + 4

    def sb(name, shape, dtype=F32):
        return Buf(nc.alloc_sbuf_tensor(name, list(shape), dtype))

    xT = [sb("xT%d" % k, [128, NT0]) for k in range(KT)]
    hT = [sb("hT%d" % k, [128, NT0], BF16) for k in range(KT)]
    mixT = [sb("mx%d" % k, [128, NT0], BF16) for k in range(KT)]
    xio = sb("xio", [128, 512])
    NF, NB, NW = 19, 74, 20
    FW = max(NTW, 384)
    fpool = [sb("fp%d" % i, [128, FW]) for i in range(NF)]
    bpool = [sb("bp%d" % i, [128, NT0], BF16) for i in range(NB)]
    wpool = [sb("wp%d" % i, [128, 512], BF16) for i in range(NW)]
    ffree, bfree, wfree = list(range(NF)), list(range(NB)), list(range(NW))

    def wa():
        return wpool[wfree.pop(0)]

    def wfr(*bs):
        for b in bs:
            wfree.append(wpool.index(b))

    def fa():
        return fpool[ffree.pop(0)]

    def ba():
        return bpool[bfree.pop(0)]

    def ffr(*bs):
        for b in bs:
            ffree.append(fpool.index(b))

    def bfr(*bs):
        for b in bs:
            bfree.append(bpool.index(b))

    NSLOT = 2
    wslots = [[sb("w%d_%d" % (i, q), [128, 4, 512], BF16) for q in range(4)] for i in range(NSLOT)]
    wctr = [0]
    Sgla = [sb("Sgla%d" % l, [128, 2, 128]) for l in range(L)]
    Sgdn = [sb("Sgdn%d" % l, [128, 6, 128]) for l in range(L)]
    Srw = [sb("Srw%d" % l, [128, 6, 64]) for l in range(L)]
    chist = [sb("chist%d" % l, [128, 18, 3]) for l in range(L)]
    shist = [sb("shist%d" % l, [128, 21]) for l in range(L)]
    Sbf_gla = sb("Sbf_gla", [128, 2, 128], BF16); Sbf_gdn = sb("Sbf_gdn", [128, 6, 128], BF16); Sbf_rw = sb("Sbf_rw", [128, 6, 64], BF16)
    oT = sb("oT", [128, 6, NT0])
    browall = sb("browall", [128, 6, NT0])
    vfirst = sb("vfirst", [128, 6, NT0], BF16)
    NCHM = max(NT0 // 64, 1)
    ebC = sb("ebC", [128, 6, NCHM])
    cstb = sb("cstb", [128, 448])
    ident_f = cstb[:, 0:128]; bd64_f = cstb[:, 128:256]
    mUi = cstb[0:64, 256:320]; mUs = cstb[0:64, 320:384]; mLs = cstb[0:64, 384:448]
    ones6 = sb("ones6", [6, 128])
    ident_b = sb("ident_b", [128, 128], BF16); ones_b = sb("ones_b", [128, 128], BF16)
    bd64_b = sb("bd64_b", [128, 128], BF16); bd64s_b = sb("bd64s_b", [128, 128], BF16)
    nUi = sb("nUi", [64, 64]); nLs = sb("nLs", [64, 64])
    notstart = sb("notstart", [128, NT0])
    g1 = sb("g1", [128, L, 16]); g2 = sb("g2", [128, L, 16]); gf = sb("gf", [128, 16])
    a_up_b = sb("a_up_b", [16, 256], BF16); nabias = sb("nabias", [128, L, 2]); glan = sb("glan", [128, L])
    cw = sb("cw", [128, L, 4, 18]); negA = sb("negA", [6, L]); dtb = sb("dtb", [6, L]); gdnn = sb("gdnn", [128, L])
    mu_rkv = sb("mu_rkv", [128, L, 18]); mu_w = sb("mu_w", [64, L]); mu_a = sb("mu_a", [64, L]); mu_g = sb("mu_g", [128, L])
    w0 = sb("w0", [128, L, 6]); a0 = sb("a0", [128, L, 6]); v0 = sb("v0", [128, LV, 6])
    kk_c = sb("kk_c", [128, L, 6]); ka_c = sb("ka_c", [128, L, 6]); omka = sb("omka", [128, L, 6]); rk_c = sb("rk_c", [128, L, 6])
    lng = sb("lng", [128, L, 6]); lnb = sb("lnb", [128, L, 6])
    w_up_b = sb("w_up_b", [64, 768], BF16); a_upr_b = sb("a_upr_b", [64, 768], BF16)
    g_up_b = sb("g_up_b", [128, 768], BF16); v_up_b = sb("v_up_b", [32, 768], BF16); v_dn_b = sb("v_dn_b", [128, 6, 32], BF16)
    psb = [Buf(nc.alloc_psum_tensor("ps%d" % i, [128, 512], F32)) for i in range(8)]
    for b_ in psb:
        b_.excl = True
    pctr = [0]

    def psn():
        b = psb[pctr[0] % 6]; pctr[0] += 1
        return b

    act, dve, pool, sp = kb.act, kb.dve, kb.pool, kb.sp
    mm, tr, actf, tt, ts, stt, scan, cp, recip, memset = kb.mm, kb.tr, kb.actf, kb.tt, kb.ts, kb.stt, kb.scan, kb.cp, kb.recip, kb.memset

    def ld(dst, src_ap, q=sp, slow=True):
        kb.dma(q, dst.ap, src_ap, [], [dst.b], slow=slow)

    ld(cstb.v(), cst, slow=False)
    memset(ones6.v(), 1.0)
    cp(ident_b.v(), ident_f); memset(ones_b.v(), 1.0); cp(bd64_b.v(), bd64_f)
    ts(bd64s_b.v(), bd64_f, 1.0 / 64, ALU.mult)
    ts(nUi.v(), mUi, -1.0, ALU.add, -NEG, ALU.mult)
    ts(nLs.v(), mLs, -1.0, ALU.add, -NEG, ALU.mult)

    def pcol(dst, src, pat, **kw):
        ld(dst, src.rearrange(pat, **kw))

    for l in range(L):
        pcol(g1[:, l, :], norm1_g[l], "(k p) -> p k", p=128)
        pcol(g2[:, l, :], norm2_g[l], "(k p) -> p k", p=128)
        pcol(nabias[:, l, :], gla_a_bias[l], "(j p) -> p j", p=128)
        pcol(glan[:, l:l + 1], gla_norm_g[l], "(p o) -> p o", o=1)
        pcol(gdnn[:, l:l + 1], gdn_norm_g[l], "(p o) -> p o", o=1)
        for j in range(4):
            pcol(cw[:, l, j, :], gdn_conv_w[l, j], "(i p) -> p i", p=128)
        pcol(negA[:, l:l + 1], gdn_A_log[l], "(p o) -> p o", o=1)
        pcol(dtb[:, l:l + 1], gdn_dt_bias[l], "(p o) -> p o", o=1)
        pcol(mu_rkv[:, l, :], rw_mu[l, 0:2304], "(i p) -> p i", p=128)
        pcol(mu_w[:, l:l + 1], rw_mu[l, 2304:2368], "(p o) -> p o", o=1)
        pcol(mu_a[:, l:l + 1], rw_mu[l, 2368:2432], "(p o) -> p o", o=1)
        pcol(mu_g[:, l:l + 1], rw_mu[l, 2432:2560], "(p o) -> p o", o=1)
        pcol(w0[:, l, :], rw_w0[l], "(i p) -> p i", p=128)
        pcol(a0[:, l, :], rw_a0[l], "(i p) -> p i", p=128)
        pcol(kk_c[:, l, :], rw_k_k[l], "(i p) -> p i", p=128)
        pcol(ka_c[:, l, :], rw_k_a[l], "(i p) -> p i", p=128)
        pcol(rk_c[:, l, :], rw_r_k[l], "(j i) d -> (i d) j", i=2)
        pcol(lng[:, l, :], rw_ln_g[l], "(i p) -> p i", p=128)
        pcol(lnb[:, l, :], rw_ln_b[l], "(i p) -> p i", p=128)
    for l in range(L - 1):
        pcol(v0[:, l, :], rw_v0[l], "(i p) -> p i", p=128)
    pcol(gf.v(), final_g, "(k p) -> p k", p=128)
    ts(nabias.v(), nabias.v(), -1.0, ALU.mult)
    actf(negA.v(), negA.v(), AF.Exp)
    ts(negA.v(), negA.v(), -1.0, ALU.mult)
    ts(omka.v(), ka_c.v(), -1.0, ALU.mult, 1.0, ALU.add)

    def wload(src2d, r0, kt, c0, ncols):
        slot = wslots[wctr[0] % NSLOT]; wctr[0] += 1
        for q, (k0, kn) in enumerate(_chunks(kt, 4)):
            src = src2d[r0 + k0 * 128: r0 + (k0 + kn) * 128, c0:c0 + ncols].rearrange("(k p) n -> p k n", p=128)
            kb.dma(pool, slot[q].t[:, 0:kn, 0:ncols], src, [], [slot[q]])
        return slot

    def wk(slot, k, c0, c1):
        return slot[k // 4][:, k % 4, c0:c1]

    def run_seq(sk, T):
        NT = min(NT0, T)
        C = min(64, T)
        NCH = NT // C
        NLEV = int(np.log2(C))
        Xd, Yd = X[sk], Y[sk]
        cs = slice(0, C)
        memset(notstart.v(), 1.0)
        memset(notstart[:, 0:NT].re("p (n c) -> p n c", c=C)[:, :, 0:1], 0.0)
        for l in range(L):
            if sk == 'p':
                for b in (Sgla[l], Sgdn[l], Srw[l], chist[l], shist[l]):
                    memset(b.v(), 0.0)
            else:
                for h in range(4):
                    ld(Sgla[l][(h % 2) * 64:(h % 2) * 64 + 64, h // 2, :], st_gla[l, h], slow=False)
                ld(Sgdn[l].v(), st_gdn[l].rearrange("h d e -> d h e"), slow=False)
                for h in range(12):
                    ld(Srw[l][(h % 2) * 64:(h % 2) * 64 + 64, h // 2, :], st_rw[l, h], slow=False)
                for t_ in range(3):
                    ld(chist[l][:, :, t_], st_conv[l, t_].rearrange("(i p) -> p i", p=128))
                ld(shist[l][:, 0:18], st_shift[l, 0, 0:2304].rearrange("(i p) -> p i", p=128))
                ld(shist[l][0:64, 18:19], st_shift[l, 0, 2304:2368].rearrange("(p o) -> p o", o=1))
                ld(shist[l][0:64, 19:20], st_shift[l, 0, 2368:2432].rearrange("(p o) -> p o", o=1))
                ld(shist[l][:, 20:21], st_shift[l, 0, 2432:2560].rearrange("(p o) -> p o", o=1))

        tok = slice(0, NT)

        def rmsnorm(gcol, outs):
            ps = psn()
            for k in range(KT):
                sq = ba()
                actf(sq[:, tok], xT[k][:, tok], AF.Square)
                mm(ps[:, tok], ones_b.v(), sq[:, tok], start=(k == 0), stop=(k == KT - 1))
                bfr(sq)
            rstd = fa()
            actf(rstd[:, tok], ps[:, tok], AF.Sqrt, scale=1.0 / D, bias=1e-6)
            recip(rstd[:, tok], rstd[:, tok])
            for k in range(KT):
                stt(outs[k][:, tok], xT[k][:, tok], gcol(k), rstd[:, tok], ALU.mult, ALU.mult)
            ffr(rstd)

        def dense_fm(slot, c0, width, kt, rhs, ps, prow=128):
            for k in range(kt):
                mm(ps[0:width, tok], wk(slot, k, c0, c0 + width), rhs(k), start=(k == 0), stop=(k == kt - 1))

        def pnorm(src_v, ones_v, scale, bias, dst):
            sq = ba()
            actf(sq[:, tok], src_v, AF.Square)
            ps = psn()
            mm(ps[:, tok], ones_v, sq[:, tok])
            actf(dst, ps[:, tok], AF.Sqrt, scale=scale, bias=bias)
            recip(dst, dst)
            bfr(sq)

        def tform_gen(Nn, Aa):
            v6 = lambda b_: b_[cs, 0:6 * C].re("p (h c) -> p h c", h=6)
            v6b = lambda b_: b_.v().cast(BF16)[cs, 0:6 * C].re("p (h c) -> p h c", h=6)
            P = fa()
            Pv = v6b(P)
            tt(Pv, Aa, ident_f[cs, cs].un(1).bc([C, 6, C]), ALU.add)
            curN, curA = Nn, Aa
            prev = []
            for lev in range(1, NLEV):
                psN = psn(); pv = v6(psN)
                for h in range(6):
                    mm(pv[:, h, :], curA[:, h, :], curN[:, h, :])
                nA = None; nAv = None
                if lev < NLEV - 1:
                    psA = psn(); pa = v6(psA)
                    for h in range(6):
                        mm(pa[:, h, :], curN[:, h, :], curA[:, h, :])
                nN = fa(); nNv = v6b(nN)
                cp(nNv, pv, eng=act)
                if lev < NLEV - 1:
                    nA = fa(); nAv = v6b(nA)
                    cp(nAv, pa, eng=dve)
                yield
                psP = psn(); pp = v6(psP)
                for h in range(6):
                    mm(pp[:, h, :], nNv[:, h, :], Pv[:, h, :])
                tt(Pv, Pv, pp, ALU.add)
                curN, curA = nNv, nAv
                if prev:
                    ffr(*prev)
                prev = [x for x in (nN, nA) if x is not None]
                yield
            if prev:
                ffr(*prev)
            return P, Pv

        def run_gen(g_):
            try:
                while True:
                    next(g_)
            except StopIteration as e_:
                return e_.value

        def tform(Nn, Aa):
            return run_gen(tform_gen(Nn, Aa))

        def interleave(gens):
            gens = list(gens)
            res = {}
            live = list(gens)
            while live:
                for g_ in list(live):
                    try:
                        next(g_)
                    except StopIteration as e_:
                        res[g_] = e_.value
                        live.remove(g_)
            return [res[g_] for g_ in gens]

        def headnorm_out(o_v, ones_v, nrm_scale, gcol, gate_b, dst):
            rstd = fa()
            pnorm(o_v, ones_v, nrm_scale, 1e-6, rstd[:, tok])
            t1 = fa()
            stt(t1[:, tok], o_v, gcol, rstd[:, tok], ALU.mult, ALU.mult)
            tt(dst, t1[:, tok], gate_b, ALU.mult)
            ffr(rstd, t1)

        def mixer(l):
            Wl = w_in[l]
            hr = lambda k: hT[k][:, tok]
            ld(a_up_b.v(), gla_a_up[l], q=pool, slow=False)
            ld(w_up_b.v(), rw_w_up[l], q=pool, slow=False)
            ld(a_upr_b.v(), rw_a_up[l], q=pool, slow=False)
            ld(g_up_b.v(), rw_g_up[l], q=pool, slow=False)
            if l > 0:
                ld(v_up_b.v(), rw_v_up[l - 1], q=pool, slow=False)
                ld(v_dn_b.v(), rw_v_down[l - 1].rearrange("(k p) r -> p k r", p=128), q=pool, slow=False)
            phase(2)
            s1 = wload(Wl, 0, KT, GLA0 + 1024, 272)
            ps = psn(); dense_fm(s1, 0, 16, KT, hr, ps)
            aT = ba(); cp(aT[0:16, tok], ps[0:16, tok], eng=act)
            sg = []
            s2 = None
            for i in range(4):
                if i == 2:
                    s2 = wload(Wl, 0, KT, GLA0 + 1296, 256)
                ps = psn()
                dense_fm(s1 if i < 2 else s2, (16 + i * 128) if i < 2 else (i - 2) * 128, 128, KT, hr, ps)
                g = ba(); actf(g[:, tok], ps[:, tok], AF.Silu); sg.append(g)
            phase(2.1)
            EP, EM, ED = [], [], []
            for j in range(2):
                ps = psn()
                mm(ps[:, tok], a_up_b[0:16, j * 128:(j + 1) * 128], aT[0:16, tok])
                e = fa()
                actf(e[:, tok], ps[:, tok], AF.Exp, scale=-1.0, bias=nabias[:, l, j:j + 1])
                actf(e[:, tok], e[:, tok], AF.Ln, bias=1.0)
                csm = fa()
                scan(csm[:, tok], notstart[:, tok], e[:, tok])
                ep = fa(); em = fa(); ed = fa()
                actf(ep[:, tok], csm[:, tok], AF.Exp, scale=-1.0 / 16, bias=float(np.log(0.125)))
                actf(em[:, tok], csm[:, tok], AF.Exp, scale=1.0 / 16)
                c3 = csm[:, tok].re("p (n c) -> p n c", c=C)
                tt(e[:, tok].re("p (n c) -> p n c", c=C), c3[:, :, C - 1:C].bc([128, NCH, C]), c3, ALU.subtract)
                actf(ed[:, tok], e[:, tok], AF.Exp, scale=-1.0 / 16)
                actf(ebC[:, j, 0:NCH], c3[:, :, C - 1], AF.Exp, scale=-1.0 / 16)
                EP.append(ep); EM.append(em); ED.append(ed)
                ffr(e, csm)
            bfr(aT)
            phase(2.2)
            qe, ke, kd = [], [], []
            s3 = wload(Wl, 0, KT, GLA0 + 0, 512)
            for j in range(2):
                ps = psn(); dense_fm(s3, j * 128, 128, KT, hr, ps)
                for i in range(2):
                    r0 = i * 64
                    q = ba()
                    tt(q[r0:r0 + 64, tok], ps[r0:r0 + 64, tok], EP[j][r0:r0 + 64, tok], ALU.mult)
                    memset(q[64 - r0:128 - r0, tok], 0.0)
                    qe.append(q)
            for j in range(2):
                ps = psn(); dense_fm(s3, 256 + j * 128, 128, KT, hr, ps)
                k1 = ba(); tt(k1[:, tok], ps[:, tok], EM[j][:, tok], ALU.mult); ke.append(k1)
                k2 = ba(); tt(k2[:, tok], ps[:, tok], ED[j][:, tok], ALU.mult); kd.append(k2)
            ffr(*EP, *EM, *ED)
            phase(2.3)
            s4 = wload(Wl, 0, KT, GLA0 + 512, 512)
            cp(Sbf_gla.v(), Sgla[l].v(), eng=act)
            for c in range(NCH):
                cc = slice(c * C, (c + 1) * C)
                ps = psn()
                for k in range(KT):
                    mm(ps[cs, 0:512], hT[k][:, cc], wk(s4, k, 0, 512), start=(k == 0), stop=(k == KT - 1))
                vt = wa()
                cp(vt[cs, 0:512], ps[cs, 0:512], eng=act)
                vtv = lambda h: vt[cs, h * 128:(h + 1) * 128]
                pst = psn()
                ptb = pst.v()
                for j in range(2):
                    mm(ptb[cs, j * 128:(j + 1) * 128], kd[j][:, cc], ident_b.v())
                kdt = wa()
                cp(kdt[cs, 0:256], ptb[cs, 0:256])
                pss = psn()
                for h in range(4):
                    r0 = (h % 2) * 64
                    mm(pss[cs, h * C:(h + 1) * C], ke[h // 2][:, cc], qe[h][:, cc])
                scm = wa()
                tt(scm[cs, 0:4 * C].re("p (h c) -> p h c", h=4), pss[cs, 0:4 * C].re("p (h c) -> p h c", h=4),
                   mUi[cs, cs].un(1).bc([C, 4, C]), ALU.mult)
                po = psn()
                for h in range(4):
                    r0 = (h % 2) * 64
                    mm(po[:, h * C:(h + 1) * C], Sbf_gla[:, h // 2, :], qe[h][:, cc], start=True, stop=False)
                    mm(po[:, h * C:(h + 1) * C], vtv(h), scm[cs, h * C:(h + 1) * C], start=False, stop=True)
                cp(oT[:, 0:4, cc], po[:, 0:4 * C].re("p (h c) -> p h c", h=4), eng=act)
                pS = psn()
                for h in range(4):
                    r0 = (h % 2) * 64
                    mm(pS[r0:r0 + 64, (h // 2) * 128:(h // 2) * 128 + 128], kdt[cs, h * 64:(h + 1) * 64], vtv(h))
                for j in range(2):
                    stt(Sgla[l][:, j, :], Sgla[l][:, j, :], ebC[:, j, c:c + 1], pS[:, j * 128:(j + 1) * 128], ALU.mult, ALU.add)
                cp(Sbf_gla.v(), Sgla[l].v(), eng=act)
                wfr(vt, kdt, scm)
            bfr(*qe, *ke, *kd)
            for h in range(4):
                headnorm_out(oT[:, h, tok], ones_b.v(), 1.0 / 128, glan[:, l:l + 1], sg[h][:, tok], mixT[h][:, tok])
            bfr(*sg)

            phase(3)
            sA = wload(Wl, 0, KT, GDN0 + 2304, 396)
            sB = wload(Wl, 0, KT, GDN0 + 2304 + 396, 384)
            ebrow = [ba() for _ in range(6)]; betarow = [ba() for _ in range(6)]
            ps = psn(); dense_fm(sA, 0, 6, KT, hr, ps)
            bT6 = fa(); actf(bT6[0:6, tok], ps[0:6, tok], AF.Sigmoid)
            ps = psn(); dense_fm(sA, 6, 6, KT, hr, ps)
            b6 = fa(); e6 = fa()
            actf(e6[0:6, tok], ps[0:6, tok], AF.Exp, bias=dtb[:, l:l + 1])
            actf(e6[0:6, tok], e6[0:6, tok], AF.Ln, bias=1.0)
            ts(e6[0:6, tok], e6[0:6, tok], negA[:, l:l + 1], ALU.mult)
            scan(b6[0:6, tok], notstart[0:6, tok], e6[0:6, tok])
            ffr(e6)
            for h in range(6):
                msk_ = fa()
                ts(msk_[0:6, tok], b6[0:6, tok], ident_f[0:6, h:h + 1], ALU.mult)
                ps = psn()
                mm(ps[:, tok], ones6.v(), msk_[0:6, tok])
                cp(browall[:, h, tok], ps[:, tok], eng=act)
                ts(msk_[0:6, tok], bT6[0:6, tok], ident_f[0:6, h:h + 1], ALU.mult)
                ps = psn()
                mm(ps[:, tok], ones6.v(), msk_[0:6, tok])
                cp(betarow[h][:, tok], ps[:, tok], eng=act)
                ffr(msk_)
            for h in range(6):
                actf(ebrow[h][:, tok], browall[:, h, tok], AF.Exp)
            b4 = browall[:, :, tok].re("p h (n c) -> p h n c", c=C)
            actf(ebC[:, :, 0:NCH], b4[:, :, :, C - 1], AF.Exp)
            sgd = []
            for i in range(6):
                ps = psn()
                if i < 3:
                    dense_fm(sA, 12 + i * 128, 128, KT, hr, ps)
                else:
                    dense_fm(sB, (i - 3) * 128, 128, KT, hr, ps)
                g = ba(); actf(g[:, tok], ps[:, tok], AF.Silu); sgd.append(g)
            qn, qe, kn, kbt, kbe, kd, vb = [], [], [], [], [], [], []
            slot = None
            for i in range(18):
                if i % 4 == 0:
                    slot = wload(Wl, 0, KT, GDN0 + i * 128, min(512, 2304 - i * 128))
                ps = psn(); dense_fm(slot, (i % 4) * 128, 128, KT, hr, ps)
                xc = fa()
                cp(xc[:, 0:3], chist[l][:, i, :])
                cp(xc[:, 3:3 + NT], ps[:, tok], eng=act)
                acc = fa()
                ts(acc[:, tok], xc[:, 0:NT], cw[:, l, 0, i:i + 1], ALU.mult)
                for j in range(1, 4):
                    stt(acc[:, tok], xc[:, j:j + NT], cw[:, l, j, i:i + 1], acc[:, tok], ALU.mult, ALU.add)
                cp(chist[l][:, i, :], xc[:, NT:NT + 3])
                h = i % 6
                if i < 12:
                    sl = fa()
                    actf(sl[:, tok], acc[:, tok], AF.Silu)
                    rn = fa()
                    if i < 6:
                        pnorm(sl[:, tok], ones_b.v(), 128.0, 128e-6, rn[:, tok])
                        a = ba(); tt(a[:, tok], sl[:, tok], rn[:, tok], ALU.mult); qn.append(a)
                        b = ba(); tt(b[:, tok], a[:, tok], ebrow[h][:, tok], ALU.mult); qe.append(b)
                    else:
                        pnorm(sl[:, tok], ones_b.v(), 1.0, 1e-6, rn[:, tok])
                        a = ba(); tt(a[:, tok], sl[:, tok], rn[:, tok], ALU.mult); kn.append(a)
                        b = ba(); tt(b[:, tok], a[:, tok], betarow[h][:, tok], ALU.mult); kbt.append(b)
                        b2 = ba(); tt(b2[:, tok], b[:, tok], ebrow[h][:, tok], ALU.mult); kbe.append(b2)
                        dd = rn
                        tt(dd[:, tok].re("p (n c) -> p n c", c=C), b4[:, h, :, C - 1:C].bc([128, NCH, C]), b4[:, h, :, :], ALU.subtract)
                        actf(dd[:, tok], dd[:, tok], AF.Exp)
                        b3 = ba(); tt(b3[:, tok], a[:, tok], dd[:, tok], ALU.mult); kd.append(b3)
                    ffr(sl, rn)
                else:
                    sl = fa()
                    actf(sl[:, tok], acc[:, tok], AF.Silu)
                    a = ba(); tt(a[:, tok], sl[:, tok], betarow[h][:, tok], ALU.mult); vb.append(a)
                    ffr(sl)
                ffr(xc, acc)
            bfr(*ebrow, *betarow)
            cp(Sbf_gdn.v(), Sgdn[l].v(), eng=act)
            g6 = lambda b_: b_[cs, 0:6 * C].re("p (h c) -> p h c", h=6)

            def gdn_A(c):
                cc = slice(c * C, (c + 1) * C)

                def trans6(srcs):
                    outs = []
                    for g in range(2):
                        p = psn(); pb = p.v()
                        for hh in range(3):
                            mm(pb[cs, hh * 128:(hh + 1) * 128], srcs[g * 3 + hh][:, cc], ident_b.v())
                        o = wa(); cp(o[cs, 0:384], pb[cs, 0:384], eng=(act if g else dve)); outs.append(o)
                    return outs
                p = psn()
                mm(p[cs, 0:6], b6[0:6, cc], ident_f[0:6, 0:6])
                btok = fa(); cp(btok[cs, 0:6], p[cs, 0:6])
                vbt = trans6(vb)
                yield
                Dm = fa(); Dv = g6(Dm)
                tt(Dv, browall[cs, :, cc], btok[cs, 0:6].un(2).bc([C, 6, C]), ALU.subtract)
                aT_ = fa(); aTv = g6(aT_)
                tt(aTv, Dv, nUi[cs, cs].un(1).bc([C, 6, C]), ALU.add)
                actf(aTv, aTv, AF.Exp)
                a2 = fa(); a2v = g6(a2)
                tt(a2v, nLs[cs, cs].un(1).bc([C, 6, C]), Dv, ALU.subtract)
                actf(a2v, a2v, AF.Exp)
                tt(Dv, aTv, mUs[cs, cs].un(1).bc([C, 6, C]), ALU.mult)
                kdt = trans6(kd)
                yield
                pk1 = psn(); pk2 = psn(); pq = psn()
                v1 = g6(pk1); v2 = g6(pk2); v3 = g6(pq)
                for h in range(6):
                    mm(v1[:, h, :], kn[h][:, cc], kbt[h][:, cc])
                    mm(v2[:, h, :], kbt[h][:, cc], kn[h][:, cc])
                    mm(v3[:, h, :], kn[h][:, cc], qn[h][:, cc])
                g6b = lambda b_: b_.v().cast(BF16)[cs, 0:6 * C].re("p (h c) -> p h c", h=6)
                Aa = fa(); Aav = g6b(Aa)
                stt(Aav, v1, -1.0, Dv, ALU.mult, ALU.mult)
                Nn = fa(); Nnv = g6b(Nn)
                stt(Nnv, v2, -1.0, a2v, ALU.mult, ALU.mult)
                PT = wa()
                tt(g6(PT), v3, aTv, ALU.mult)
                ffr(Dm, aT_, a2, btok)
                yield
                P, Pv = yield from tform_gen(Nnv, Aav)
                ffr(Aa, Nn)
                return (vbt, kdt, PT, P, Pv)

            def gdn_B(c, aout):
                cc = slice(c * C, (c + 1) * C)
                vbt, kdt, PT, P, Pv = aout
                hv = lambda lst, h: lst[h // 3][cs, (h % 3) * 128:(h % 3) * 128 + 128]
                ptv = lambda h: PT[cs, h * C:(h + 1) * C]
                X0 = [fa(), fa()]; U = [wa(), wa()]
                for g in range(2):
                    p = psn()
                    for hh in range(3):
                        h = g * 3 + hh
                        mm(p[cs, hh * 128:(hh + 1) * 128], kbe[h][:, cc], Sbf_gdn[:, h, :])
                    tt(X0[g].v().cast(BF16)[cs, 0:384], vbt[g][cs, 0:384], p[cs, 0:384], ALU.subtract)
                yield
                for g in range(2):
                    p = psn()
                    for hh in range(3):
                        h = g * 3 + hh
                        mm(p[cs, hh * 128:(hh + 1) * 128], Pv[:, h, :], X0[g].v().cast(BF16)[cs, hh * 128:(hh + 1) * 128])
                    cp(U[g][cs, 0:384], p[cs, 0:384], eng=act)
                yield
                po = psn()
                for h in range(6):
                    mm(po[:, h * C:(h + 1) * C], Sbf_gdn[:, h, :], qe[h][:, cc], start=True, stop=False)
                    mm(po[:, h * C:(h + 1) * C], hv(U, h), ptv(h), start=False, stop=True)
                cp(oT[:, :, cc], po[:, 0:6 * C].re("p (h c) -> p h c", h=6), eng=act)
                pS = [psn(), psn()]
                for h in range(6):
                    mm(pS[h // 3][:, (h % 3) * 128:(h % 3) * 128 + 128], hv(kdt, h), hv(U, h))
                tt(Sgdn[l].v(), Sgdn[l].v(), ebC[:, :, c:c + 1].bc([128, 6, 128]), ALU.mult)
                for g in range(2):
                    tt(Sgdn[l][:, g * 3:g * 3 + 3, :], Sgdn[l][:, g * 3:g * 3 + 3, :],
                       pS[g][:, 0:384].re("p (h e) -> p h e", h=3), ALU.add)
                cp(Sbf_gdn.v(), Sgdn[l].v(), eng=act)
                ffr(P, *X0); wfr(PT, *U, *vbt, *kdt)

            aout = interleave([gdn_A(0)])[0]
            for c in range(NCH):
                gens = [gdn_B(c, aout)]
                if c + 1 < NCH:
                    gens.append(gdn_A(c + 1))
                res_ = interleave(gens)
                if c + 1 < NCH:
                    aout = res_[1]
            bfr(*qn, *qe, *kn, *kbt, *kbe, *kd, *vb)
            ffr(bT6, b6)
            for h in range(6):
                headnorm_out(oT[:, h, tok], ones_b.v(), 1.0 / 128, gdnn[:, l:l + 1], sgd[h][:, tok], mixT[4 + h][:, tok])
            bfr(*sgd)

            phase(4)
            def shiftmix(ps_v, rows, hcol, mucol, dst_v):
                zb = fa()
                cp(zb[0:rows, 0:1], shist[l][0:rows, hcol:hcol + 1])
                cp(zb[0:rows, 1:1 + NT], ps_v, eng=act)
                d = fa()
                tt(d[0:rows, tok], zb[0:rows, 0:NT], zb[0:rows, 1:1 + NT], ALU.subtract)
                stt(dst_v, d[0:rows, tok], mucol, zb[0:rows, 1:1 + NT], ALU.mult, ALU.add)
                cp(shist[l][0:rows, hcol:hcol + 1], zb[0:rows, NT:NT + 1])
                ffr(zb, d)

            sL = wload(Wl, 0, KT, RW0 + 2304, 256)
            tmpf = fa()
            ps = psn(); dense_fm(sL, 0, 64, KT, hr, ps)
            shiftmix(ps[0:64, tok], 64, 18, mu_w[:, l:l + 1], tmpf[0:64, tok])
            twT = ba(); actf(twT[0:64, tok], tmpf[0:64, tok], AF.Tanh)
            ps = psn(); dense_fm(sL, 64, 64, KT, hr, ps)
            xaT = ba(); shiftmix(ps[0:64, tok], 64, 19, mu_a[:, l:l + 1], xaT[0:64, tok])
            ps = psn(); dense_fm(sL, 128, 128, KT, hr, ps)
            shiftmix(ps[:, tok], 128, 20, mu_g[:, l:l + 1], tmpf[:, tok])
            sgT = ba(); actf(sgT[:, tok], tmpf[:, tok], AF.Sigmoid)
            ffr(tmpf)
            rT, kT_, vT = [], [], []
            slot = None
            for i in range(18):
                if i % 4 == 0:
                    slot = wload(Wl, 0, KT, RW0 + i * 128, min(512, 2304 - i * 128))
                ps = psn(); dense_fm(slot, (i % 4) * 128, 128, KT, hr, ps)
                z = ba()
                shiftmix(ps[:, tok], 128, i, mu_rkv[:, l, i:i + 1], z[:, tok])
                (rT if i < 6 else kT_ if i < 12 else vT).append(z)
            if l == 0:
                for j in range(6):
                    cp(vfirst[:, j, tok], vT[j][:, tok])
            else:
                ps = psn()
                for j in range(6):
                    mm(ps[0:32, tok], v_dn_b[:, j, :], vT[j][:, tok], start=(j == 0), stop=(j == 5))
                t1 = ba(); cp(t1[0:32, tok], ps[0:32, tok], eng=act)
                for j in range(6):
                    ps = psn(); mm(ps[:, tok], v_up_b[0:32, j * 128:(j + 1) * 128], t1[0:32, tok])
                    nu = fa(); actf(nu[:, tok], ps[:, tok], AF.Sigmoid, bias=v0[:, l - 1, j:j + 1])
                    d = fa()
                    tt(d[:, tok], vfirst[:, j, tok], vT[j][:, tok], ALU.subtract)
                    tt(d[:, tok], d[:, tok], nu[:, tok], ALU.mult)
                    tt(vT[j][:, tok], vT[j][:, tok], d[:, tok], ALU.add)
                    ffr(nu, d)
                bfr(t1)
            rt, kt_, at, kpt, kdl, adl, bonus, gate = [], [], [], [], [], [], [], []
            for j in range(6):
                ps = psn(); mm(ps[:, tok], w_up_b[0:64, j * 128:(j + 1) * 128], twT[0:64, tok])
                sig = fa(); actf(sig[:, tok], ps[:, tok], AF.Sigmoid, bias=w0[:, l, j:j + 1])
                csm = fa(); scan(csm[:, tok], notstart[:, tok], sig[:, tok])
                ps = psn(); mm(ps[:, tok], a_upr_b[0:64, j * 128:(j + 1) * 128], xaT[0:64, tok])
                aa = fa(); actf(aa[:, tok], ps[:, tok], AF.Sigmoid, bias=a0[:, l, j:j + 1])
                ps = psn(); mm(ps[:, tok], g_up_b[:, j * 128:(j + 1) * 128], sgT[:, tok])
                g_ = ba(); cp(g_[:, tok], ps[:, tok], eng=act); gate.append(g_)
                kr = fa()
                ts(kr[:, tok], kT_[j][:, tok], kk_c[:, l, j:j + 1], ALU.mult)
                rn = fa()
                pnorm(kr[:, tok], bd64_b.v(), 1.0, 1e-6, rn[:, tok])
                kk = kr
                tt(kk[:, tok], kr[:, tok], rn[:, tok], ALU.mult)
                ka = rn
                tt(ka[:, tok], kk[:, tok], aa[:, tok], ALU.mult)
                k2 = fa()
                ts(k2[:, tok], aa[:, tok], ka_c[:, l, j:j + 1], ALU.mult, omka[:, l, j:j + 1], ALU.add)
                tt(k2[:, tok], k2[:, tok], kT_[j][:, tok], ALU.mult)
                rk = ba()
                stt(rk[:, tok], rT[j][:, tok], rk_c[:, l, j:j + 1], k2[:, tok], ALU.mult, ALU.mult)
                ps = psn(); mm(ps[:, tok], bd64_b.v(), rk[:, tok])
                bo = ba(); tt(bo[:, tok], ps[:, tok], vT[j][:, tok], ALU.mult); bonus.append(bo)
                bfr(rk)
                e = fa()
                Epl = ba()
                actf(Epl[:, tok], csm[:, tok], AF.Exp, scale=-K0)
                c3 = csm[:, tok].re("p (n c) -> p n c", c=C)
                actf(ebC[:, j, 0:NCH], c3[:, :, C - 1], AF.Exp, scale=-K0)
                for i_ in range(2):
                    q0 = i_ * 64
                    a_ = ba()
                    tt(a_[q0:q0 + 64, tok], rT[j][q0:q0 + 64, tok], Epl[q0:q0 + 64, tok], ALU.mult)
                    memset(a_[64 - q0:128 - q0, tok], 0.0)
                    rt.append(a_)
                bfr(Epl)
                actf(e[:, tok], csm[:, tok], AF.Exp, scale=K0)
                a_ = ba(); tt(a_[:, tok], k2[:, tok], e[:, tok], ALU.mult); kt_.append(a_)
                a_ = ba(); tt(a_[:, tok], ka[:, tok], e[:, tok], ALU.mult); at.append(a_)
                tt(e[:, tok], csm[:, tok], sig[:, tok], ALU.subtract)
                actf(e[:, tok], e[:, tok], AF.Exp, scale=-K0)
                for i_ in range(2):
                    q0 = i_ * 64
                    a_ = ba()
                    tt(a_[q0:q0 + 64, tok], kk[q0:q0 + 64, tok], e[q0:q0 + 64, tok], ALU.mult)
                    memset(a_[64 - q0:128 - q0, tok], 0.0)
                    kpt.append(a_)
                tt(e[:, tok].re("p (n c) -> p n c", c=C), c3[:, :, C - 1:C].bc([128, NCH, C]), c3, ALU.subtract)
                actf(e[:, tok], e[:, tok], AF.Exp, scale=-K0)
                a_ = ba(); tt(a_[:, tok], k2[:, tok], e[:, tok], ALU.mult); kdl.append(a_)
                a_ = ba(); tt(a_[:, tok], ka[:, tok], e[:, tok], ALU.mult); adl.append(a_)
                ffr(kr, rn, k2, e, sig, csm, aa)
                bfr(rT[j], kT_[j])
            bfr(twT, xaT, sgT)
            vbl = vT
            cp(Sbf_rw.v(), Srw[l].v(), eng=act)

            v6 = lambda b_: b_[cs, 0:6 * C].re("p (h c) -> p h c", h=6)
            v6b = lambda b_: b_.v().cast(BF16)[cs, 0:6 * C].re("p (h c) -> p h c", h=6)

            def rw_A(c, g):
                cc = slice(c * C, (c + 1) * C)
                Aa = fa(); Nn = fa(); m3 = []
                kinds = ((at, kpt, 0), (kpt, at, 1), (kt_, kpt, 2), (at, rt, 3), (kt_, rt, 4))
                for (la, lb, idx) in kinds:
                    p = psn(); pv = v6(p)
                    for hh in range(6):
                        j = g * 3 + hh // 2; hg = g * 6 + hh
                        x_ = (la[hg] if (la is kpt or la is rt) else la[j])[:, cc]
                        y_ = (lb[hg] if (lb is kpt or lb is rt) else lb[j])[:, cc]
                        mm(pv[:, hh, :], x_, y_)
                    if idx == 0:
                        stt(v6b(Aa), pv, -1.0, mUs[cs, cs].un(1).bc([C, 6, C]), ALU.mult, ALU.mult)
                    elif idx == 1:
                        stt(v6b(Nn), pv, -1.0, mLs[cs, cs].un(1).bc([C, 6, C]), ALU.mult, ALU.mult)
                    else:
                        o = wa()
                        msk = mUs if idx == 2 else mUi
                        tt(v6(o), pv, msk[cs, cs].un(1).bc([C, 6, C]), ALU.mult)
                        m3.append(o)
                    if idx in (1, 4):
                        yield
                P, Pv = yield from tform_gen(v6b(Nn), v6b(Aa))
                ffr(Aa, Nn)
                return (m3, P, Pv)

            def rw_B(c, g, aout, po, pS):
                cc = slice(c * C, (c + 1) * C)
                m3, P, Pv = aout
                mv = lambda k, hh: m3[k][cs, hh * C:(hh + 1) * C]

                def trans3(srcs, e_):
                    p = psn(); pb = p.v()
                    for jj in range(3):
                        mm(pb[cs, jj * 128:(jj + 1) * 128], srcs[g * 3 + jj][:, cc], ident_b.v())
                    o = wa(); cp(o[cs, 0:384], pb[cs, 0:384], eng=e_); return o
                vtk = trans3(vbl, dve)
                yield
                p = psn()
                for hh in range(6):
                    j = g * 3 + hh // 2
                    mm(p[cs, hh * 64:(hh + 1) * 64], kpt[g * 6 + hh][:, cc], Sbf_rw[:, j, :], start=True, stop=False)
                    mm(p[cs, hh * 64:(hh + 1) * 64], mv(0, hh), vtk[cs, hh * 64:(hh + 1) * 64], start=False, stop=True)
                X0 = fa(); X0b = X0.v().cast(BF16)
                ts(X0b[cs, 0:384], p[cs, 0:384], -1.0, ALU.mult)
                adt = trans3(adl, act); kdt = trans3(kdl, dve)
                yield
                p = psn()
                for hh in range(6):
                    mm(p[cs, hh * 64:(hh + 1) * 64], Pv[:, hh, :], X0b[cs, hh * 64:(hh + 1) * 64])
                U = wa(); cp(U[cs, 0:384], p[cs, 0:384], eng=act)
                yield
                for hh in range(6):
                    j = g * 3 + hh // 2; r0 = (hh % 2) * 64
                    ov = po[r0:r0 + 64, j * C:(j + 1) * C]
                    mm(ov, Sbf_rw[:, j, :], rt[g * 6 + hh][:, cc], start=True, stop=False)
                    mm(ov, U[cs, hh * 64:(hh + 1) * 64], mv(1, hh), start=False, stop=False)
                    mm(ov, vtk[cs, hh * 64:(hh + 1) * 64], mv(2, hh), start=False, stop=True)
                for hh in range(6):
                    j = g * 3 + hh // 2; r0 = (hh % 2) * 64
                    sv = pS[r0:r0 + 64, j * 64:(j + 1) * 64]
                    mm(sv, adt[cs, hh * 64:(hh + 1) * 64], U[cs, hh * 64:(hh + 1) * 64], start=True, stop=False)
                    mm(sv, kdt[cs, hh * 64:(hh + 1) * 64], vtk[cs, hh * 64:(hh + 1) * 64], start=False, stop=True)
                ffr(P, X0); wfr(U, adt, kdt, vtk, *m3)

            aouts = interleave([rw_A(0, 0), rw_A(0, 1)])
            for c in range(NCH):
                cc = slice(c * C, (c + 1) * C)
                po = psb[6]; pS = psb[7]
                gens = [rw_B(c, 0, aouts[0], po, pS), rw_B(c, 1, aouts[1], po, pS)]
                if c + 1 < NCH:
                    gens += [rw_A(c + 1, 0), rw_A(c + 1, 1)]
                res_ = interleave(gens)
                if c + 1 < NCH:
                    aouts = res_[2:4]
                cp(oT[:, :, cc], po[:, 0:6 * C].re("p (h c) -> p h c", h=6), eng=act)
                tt(Srw[l].v(), Srw[l].v(), ebC[:, :, c:c + 1].bc([128, 6, 64]), ALU.mult)
                tt(Srw[l].v(), Srw[l].v(), pS[:, 0:384].re("p (h e) -> p h e", h=6), ALU.add)
                cp(Sbf_rw.v(), Srw[l].v(), eng=act)
            bfr(*rt, *kt_, *at, *kpt, *kdl, *adl, *vbl)
            for j in range(6):
                ob = ba(); cp(ob[:, tok], oT[:, j, tok], eng=act)
                ps = psn(); mm(ps[:, tok], bd64s_b.v(), ob[:, tok])
                cen = fa(); tt(cen[:, tok], oT[:, j, tok], ps[:, tok], ALU.subtract)
                rstd = fa()
                pnorm(cen[:, tok], bd64s_b.v(), 1.0, 64e-5, rstd[:, tok])
                tt(cen[:, tok], cen[:, tok], rstd[:, tok], ALU.mult)
                ts(cen[:, tok], cen[:, tok], lng[:, l, j:j + 1], ALU.mult, lnb[:, l, j:j + 1], ALU.add)
                tt(cen[:, tok], cen[:, tok], bonus[j][:, tok], ALU.add)
                tt(mixT[10 + j][:, tok], cen[:, tok], gate[j][:, tok], ALU.mult)
                ffr(cen, rstd); bfr(ob)
            bfr(*bonus, *gate)

            phase(5)
            for cg in range(4):
                slot = wload(w_out[l], 0, KT, cg * 512, 512)
                for m in range(4):
                    ps = psn(); dense_fm(slot, m * 128, 128, KT, lambda k: mixT[k][:, tok], ps)
                    o = cg * 4 + m
                    tt(xT[o][:, tok], xT[o][:, tok], ps[:, tok], ALU.add)

        def ffn(l):
            phase(6)
            rmsnorm(lambda k: g2[:, l, k:k + 1], hT)
            hm = []
            for cg in range(DFF // 512):
                sg_ = wload(w_gate[l], 0, KT, cg * 512, 512)
                su_ = wload(w_up[l], 0, KT, cg * 512, 512)
                bg = [psn(), psn()]; bu = [psn(), psn()]
                reg = lambda banks, m: banks[m // 2][:, (m % 2) * NT:(m % 2) * NT + NT]
                for m in range(4):
                    for k in range(KT):
                        mm(reg(bg, m), wk(sg_, k, m * 128, (m + 1) * 128), hT[k][:, tok], start=(k == 0), stop=(k == KT - 1))
                for m in range(4):
                    for k in range(KT):
                        mm(reg(bu, m), wk(su_, k, m * 128, (m + 1) * 128), hT[k][:, tok], start=(k == 0), stop=(k == KT - 1))
                for m in range(4):
                    s_ = ba(); actf(s_[:, tok], reg(bg, m), AF.Silu)
                    o = ba(); tt(o[:, tok], s_[:, tok], reg(bu, m), ALU.mult)
                    bfr(s_); hm.append(o)
            for cg in range(4):
                pd = [psn() for _ in range(4)]
                for kg in range(4):
                    slot = wload(w_down[l], kg * 1408, 11, cg * 512, 512)
                    for m in range(4):
                        for k in range(11):
                            mm(pd[m][:, tok], wk(slot, k, m * 128, (m + 1) * 128), hm[kg * 11 + k][:, tok],
                               start=(kg == 0 and k == 0), stop=(kg == 3 and k == 10))
                for m in range(4):
                    o = cg * 4 + m
                    tt(xT[o][:, tok], xT[o][:, tok], pd[m][:, tok], ALU.add)
            bfr(*hm)

        for t0 in range(0, T, NT):
            phase(1)
            for s0, rows in _chunks(NT, 128):
                for q in range(4):
                    phase(0.2)
                    kb.dma(sp, xio.t[0:rows, :], Xd[t0 + s0:t0 + s0 + rows, q * 512:(q + 1) * 512], [], [xio])
                    ps = psn()
                    for kk_ in range(4):
                        k = q * 4 + kk_
                        phase(0.5)
                        mm(ps[:, kk_ * rows:(kk_ + 1) * rows], xio[0:rows, kk_ * 128:(kk_ + 1) * 128], ident_f[0:rows, 0:rows])
                    for kk_ in range(4):
                        k = q * 4 + kk_
                        phase(0.8)
                        cp(xT[k][:, s0:s0 + rows], ps[:, kk_ * rows:(kk_ + 1) * rows], eng=(act if kk_ % 2 else dve))
            for l in range(L):
                phase(1.5)
                rmsnorm(lambda k: g1[:, l, k:k + 1], hT)
                mixer(l)
                ffn(l)
            phase(7)
            yT = [fa() for _ in range(4)]
            for q in range(4):
                pass
            ps = psn()
            for k in range(KT):
                sq = ba()
                actf(sq[:, tok], xT[k][:, tok], AF.Square)
                mm(ps[:, tok], ones_b.v(), sq[:, tok], start=(k == 0), stop=(k == KT - 1))
                bfr(sq)
            rstd = fa()
            actf(rstd[:, tok], ps[:, tok], AF.Sqrt, scale=1.0 / D, bias=1e-6)
            recip(rstd[:, tok], rstd[:, tok])
            for s0, rows in _chunks(NT, 128):
                for q in range(4):
                    ps = psn()
                    for kk_ in range(4):
                        k = q * 4 + kk_
                        y = yT[kk_]
                        stt(y[:, 0:rows], xT[k][:, s0:s0 + rows], gf[:, k:k + 1], rstd[:, s0:s0 + rows], ALU.mult, ALU.mult)
                        mm(ps[0:rows, kk_ * 128:(kk_ + 1) * 128], y[:, 0:rows], ident_f)
                    cp(xio[0:rows, 0:512], ps[0:rows, 0:512], eng=(act if q % 2 else dve))
                    kb.dma(sp, Yd[t0 + s0:t0 + s0 + rows, q * 512:(q + 1) * 512], xio.t[0:rows, :], [xio], [])
            ffr(rstd, *yT)
        kb.enabled = True
        for l in range(L):
            for h in range(4):
                kb.dma(sp, O_gla[sk][l, h], Sgla[l].t[(h % 2) * 64:(h % 2) * 64 + 64, h // 2, :], [Sgla[l]], [])
            kb.dma(sp, O_gdn[sk][l].rearrange("h d e -> d h e"), Sgdn[l].t[:], [Sgdn[l]], [])
            for h in range(12):
                kb.dma(sp, O_rw[sk][l, h], Srw[l].t[(h % 2) * 64:(h % 2) * 64 + 64, h // 2, :], [Srw[l]], [])
            for t_ in range(3):
                kb.dma(sp, O_conv[sk][l, t_].rearrange("(i p) -> p i", p=128), chist[l].t[:, :, t_], [chist[l]], [], slow=True)
            kb.dma(sp, O_shift[sk][l, 0, 0:2304].rearrange("(i p) -> p i", p=128), shist[l].t[:, 0:18], [shist[l]], [], slow=True)
            kb.dma(sp, O_shift[sk][l, 0, 2304:2368].rearrange("(p o) -> p o", o=1), shist[l].t[0:64, 18:19], [shist[l]], [], slow=True)
            kb.dma(sp, O_shift[sk][l, 0, 2368:2432].rearrange("(p o) -> p o", o=1), shist[l].t[0:64, 19:20], [shist[l]], [], slow=True)
            kb.dma(sp, O_shift[sk][l, 0, 2432:2560].rearrange("(p o) -> p o", o=1), shist[l].t[:, 20:21], [shist[l]], [], slow=True)

    if os.environ.get('KSEQ', 'ps').find('p') >= 0:
        run_seq('p', TP)
    if os.environ.get('KSEQ', 'ps').find('s') >= 0:
        run_seq('s', TS)

    for i, sem in enumerate(sp.dsems):
        n = (sp.dcount - i + len(sp.dsems) - 1) // len(sp.dsems)
        if n > 0:
            kb._need(sp, (sem, 16 * n))

    with nc.Block() as block:
        @block.tensor
        def _(e):
            for f in kb.pe.q:
                f(e)

        @block.scalar
        def _(e):
            for f in kb.act.q:
                f(e)

        @block.vector
        def _(e):
            for f in kb.dve.q:
                f(e)

        @block.gpsimd
        def _(e):
            for f in kb.pool.q:
                f(e)

        @block.sync
        def _(e):
            for f in kb.sp.q:
                f(e)
    es.close()
    return nc, kb


def make_consts():
    c = np.zeros((128, 448), np.float32)
    c[:, 0:128] = np.eye(128)
    c[0:64, 128:192] = 1.0; c[64:128, 192:256] = 1.0
    s = np.arange(64)[:, None]; t = np.arange(64)[None, :]
    c[0:64, 256:320] = (s <= t); c[0:64, 320:384] = (s < t); c[0:64, 384:448] = (t < s)
    sel = np.zeros((6, 6, 128), np.float32)
    for h in range(6):
        sel[h, h, :] = 1.0
    return c, sel.reshape(6, 768)


_CACHE = {}


def run(inputs, TP, TS, L, NTMAX=256, ncores=8, trace=False):
    key = (TP, TS, L, NTMAX)
    if key not in _CACHE:
        _CACHE[key] = build(TP, TS, L, NTMAX)
    nc, kb = _CACHE[key]
    cst, selc = make_consts()
    f = lambda a: np.ascontiguousarray(a, dtype=np.float32)
    shared = {k: f(inputs[k]) for k in (
        'norm1_g', 'w_in', 'gla_a_up', 'gla_a_bias', 'gla_norm_g', 'gdn_conv_w', 'gdn_A_log', 'gdn_dt_bias', 'gdn_norm_g',
        'rw_mu', 'rw_w0', 'rw_w_up', 'rw_a0', 'rw_a_up', 'rw_g_up', 'rw_k_k', 'rw_k_a', 'rw_r_k', 'rw_ln_g', 'rw_ln_b',
        'w_out', 'norm2_g', 'w_ffn_gate', 'w_ffn_up', 'w_ffn_down', 'final_norm_g')}
    for k in ('rw_v0', 'rw_v_down', 'rw_v_up'):
        a = f(inputs[k])
        if a.shape[0] == 0:
            a = np.zeros((1,) + a.shape[1:], np.float32)
        shared[k] = a
    shared['cst'] = cst
    in_maps = []
    for c in range(ncores):
        m = dict(shared)
        m['x_p'] = f(inputs['x_prompt'][c]); m['x_s'] = f(inputs['x_sample'][c])
        m['st_gla'] = f(inputs['state_gla'][:, c]); m['st_gdn'] = f(inputs['state_gdn'][:, c])
        m['st_conv'] = f(inputs['cache_gdn_conv'][:, c]); m['st_rw'] = f(inputs['state_rwkv'][:, c])
        m['st_shift'] = f(inputs['cache_rwkv_shift'][:, c])
        in_maps.append(m)
    res = run_bass_kernel_spmd(nc, in_maps, core_ids=list(range(ncores)), trace=trace)
    R = res.results
    st = lambda name: np.stack([R[c][name] for c in range(ncores)], axis=0)
    st1 = lambda name: np.stack([R[c][name] for c in range(ncores)], axis=1)
    outs = (st('y_p'), st('y_s'),
            st1('gla_p'), st1('gdn_p'), st1('conv_p'), st1('rwkv_p'), st1('shift_p'),
            st1('gla_s'), st1('gdn_s'), st1('conv_s'), st1('rwkv_s'), st1('shift_s'))
    return tuple(np.ascontiguousarray(o, dtype=np.float32) for o in outs), res


def kernel(**inputs):
    TP = inputs['x_prompt'].shape[1]; TS = inputs['x_sample'].shape[1]; L = inputs['norm1_g'].shape[0]
    outs, _ = run(inputs, TP, TS, L)
    return outs
```

```python
import numpy as np
from contextlib import ExitStack
import concourse.bass as bass
import concourse.mybir as mybir
from concourse.bass_utils import run_bass_kernel_spmd

F32 = mybir.dt.float32
BF16 = mybir.dt.bfloat16
AF = mybir.ActivationFunctionType
ALU = mybir.AluOpType

D = 2048
KT = 16
DFF = 5632
PROJ = 7196
GLA0, GDN0, RW0 = 0, 1552, 4636
NEG = -30000.0
K0 = float(np.exp(-0.5))


class Eng:
    def __init__(s, name):
        s.name = name; s.sem = None; s.seq = 0; s.q = []; s.seen = {}
        s.dsems = []; s.dcount = 0


class V:
    __slots__ = ('b', 'ap')

    def __init__(s, b, ap):
        s.b = b; s.ap = ap

    def __getitem__(s, i):
        return V(s.b, s.ap[i])

    def bc(s, shape):
        return V(s.b, s.ap.broadcast_to(list(shape)))

    def un(s, d):
        return V(s.b, s.ap.unsqueeze(d))

    def cast(s, dt):
        return V(s.b, s.ap.bitcast(dt))

    def re(s, pat, **kw):
        return V(s.b, s.ap.rearrange(pat, **kw))


class Buf:
    excl = False

    def __init__(s, t):
        s.t = t; s.w = None; s.r = {}

    def __getitem__(s, i):
        return V(s, s.t[i])

    def v(s):
        return V(s, s.t[:])


class KB:
    def __init__(s, nc):
        s.nc = nc
        s.pe, s.act, s.dve, s.pool, s.sp = Eng('pe'), Eng('act'), Eng('dve'), Eng('pool'), Eng('sp')
        s.engs = [s.pe, s.act, s.dve, s.pool, s.sp]
        s.semid = {}

    def _need(s, eng, ev):
        sem, val = ev
        k = id(sem)
        if eng.seen.get(k, 0) < val:
            eng.q.append(lambda e, sem=sem, val=val: e.wait_ge(sem, val))
            eng.seen[k] = val

    def _deps(s, eng, reads, writes):
        for b in reads:
            if b.w is not None:
                s._need(eng, b.w)
            if b.excl:
                for k, ev in b.r.items():
                    if ev[0] is not eng.sem:
                        s._need(eng, ev)
        pe = eng is s.pe
        for b in writes:
            if b.w is not None and not (pe and b.w[0] is eng.sem):
                s._need(eng, b.w)
            for k, ev in b.r.items():
                if not (pe and ev[0] is eng.sem):
                    s._need(eng, ev)

    def _mark(s, ev, reads, writes):
        for b in writes:
            b.w = ev; b.r = {}
        for b in reads:
            if b not in writes:
                b.r[id(ev[0])] = ev

    enabled = True

    def I(s, eng, fn, reads, writes):
        if not s.enabled:
            return
        reads = [x for x in reads if x is not None]
        s._deps(eng, reads, writes)
        eng.seq += 1
        ev = (eng.sem, eng.seq)
        eng.q.append(lambda e, fn=fn, sem=eng.sem: fn(e).then_inc(sem, 1))
        s._mark(ev, reads, writes)

    def dma(s, q, out_ap, in_ap, reads, writes, slow=False):
        if not s.enabled:
            return
        K = len(q.dsems)
        i = q.dcount; q.dcount += 1
        sem = q.dsems[i % K]
        if i >= K:
            s._need(q, (sem, 16 * (i // K)))
        for b in reads:
            if b.w is not None:
                s._need(q, b.w)
        for b in writes:
            if b.w is not None:
                s._need(q, b.w)
            for k, ev in b.r.items():
                s._need(q, ev)
        ev = (sem, 16 * (i // K + 1))
        if slow:
            q.q.append(lambda e, o=out_ap, a=in_ap, sem=sem: e.dma_start(out=o, in_=a, allow_slow_non_contiguous=True).then_inc(sem, 16))
        else:
            q.q.append(lambda e, o=out_ap, a=in_ap, sem=sem: e.dma_start(out=o, in_=a).then_inc(sem, 16))
        s._mark(ev, reads, writes)
        return ev

    def mm(s, out, lhsT, rhs, start=True, stop=True):
        s.I(s.pe, lambda e, o=out.ap, l=lhsT.ap, r=rhs.ap, st=start, sp=stop: e.matmul(o, l, r, start=st, stop=sp),
            [lhsT.b, rhs.b], [out.b])

    def tr(s, out, in_, ident):
        s.I(s.pe, lambda e, o=out.ap, i=in_.ap, d=ident.ap: e.transpose(o, i, d), [in_.b, ident.b], [out.b])

    def actf(s, out, in_, func, scale=1.0, bias=0.0, eng=None):
        rd = [in_.b]
        sc = scale; bi = bias
        if isinstance(scale, V):
            rd.append(scale.b); sc = scale.ap
        if isinstance(bias, V):
            rd.append(bias.b); bi = bias.ap
        s.I(s.act, lambda e, o=out.ap, i=in_.ap, f=func, sc=sc, bi=bi: e.activation(out=o, in_=i, func=f, bias=bi, scale=sc),
            rd, [out.b])

    def tt(s, out, in0, in1, op, eng=None):
        eng = eng or s.dve
        s.I(eng, lambda e, o=out.ap, a=in0.ap, b=in1.ap, op=op: e.tensor_tensor(o, a, b, op), [in0.b, in1.b], [out.b])

    def ts(s, out, in0, s1, op0, s2=None, op1=None, eng=None):
        eng = eng or s.dve
        rd = [in0.b]
        a1 = s1; a2 = s2
        if isinstance(s1, V):
            rd.append(s1.b); a1 = s1.ap
        if isinstance(s2, V):
            rd.append(s2.b); a2 = s2.ap
        if op1 is None:
            s.I(eng, lambda e, o=out.ap, a=in0.ap, a1=a1, op0=op0: e.tensor_scalar(o, a, a1, None, op0), rd, [out.b])
        else:
            s.I(eng, lambda e, o=out.ap, a=in0.ap, a1=a1, a2=a2, op0=op0, op1=op1: e.tensor_scalar(o, a, a1, a2, op0, op1), rd, [out.b])

    def aff(s, out, in0, s1, s2=0.0):
        rd = [in0.b]
        a1 = s1; a2 = s2
        if isinstance(s1, V):
            rd.append(s1.b); a1 = s1.ap
        if isinstance(s2, V):
            rd.append(s2.b); a2 = s2.ap
        s.I(s.act, lambda e, o=out.ap, i=in0.ap, a1=a1, a2=a2: e.activation(out=o, in_=i, func=AF.Identity, bias=a2, scale=a1),
            rd, [out.b])

    def stt(s, out, in0, sc, in1, op0, op1):
        rd = [in0.b, in1.b]
        a = sc
        if isinstance(sc, V):
            rd.append(sc.b); a = sc.ap
        s.I(s.dve, lambda e, o=out.ap, i0=in0.ap, a=a, i1=in1.ap, op0=op0, op1=op1: e.scalar_tensor_tensor(o, i0, a, i1, op0, op1),
            rd, [out.b])

    def scan(s, out, d0, d1):
        s.I(s.dve, lambda e, o=out.ap, a=d0.ap, b=d1.ap: e.tensor_tensor_scan(o, a, b, 0.0, ALU.mult, ALU.add),
            [d0.b, d1.b], [out.b])

    def cp(s, out, in_, eng=None):
        eng = eng or s.dve
        if eng is s.act:
            s.I(eng, lambda e, o=out.ap, i=in_.ap: e.activation(out=o, in_=i, func=AF.Copy), [in_.b], [out.b])
        else:
            s.I(eng, lambda e, o=out.ap, i=in_.ap: e.tensor_copy(o, i), [in_.b], [out.b])

    def recip(s, out, in_):
        s.I(s.dve, lambda e, o=out.ap, i=in_.ap: e.reciprocal(o, i), [in_.b], [out.b])

    def memset(s, out, val, eng=None):
        eng = eng or s.dve
        s.I(eng, lambda e, o=out.ap, v=val: e.memset(o, v), [], [out.b])


def _chunks(n, m):
    return [(i, min(m, n - i)) for i in range(0, n, m)]


def build(TP, TS, L, NTMAX=256):
    import os
    STOP = float(os.environ.get('KSTOP', '99'))
    nc = bass.Bass("TRN2", target_bir_lowering=False)
    kb = KB(nc)

    def phase(n):
        if n > STOP:
            kb.enabled = False
    dt = nc.dram_tensor
    es = ExitStack()
    for e_ in kb.engs:
        e_.sem = es.enter_context(nc.semaphore("s_%s" % e_.name))
    kb.sp.dsems = [es.enter_context(nc.semaphore("dsp%d" % i)) for i in range(8)]
    kb.pool.dsems = [es.enter_context(nc.semaphore("dpl%d" % i)) for i in range(8)]

    def din(name, shape):
        return dt(name, list(shape), F32, kind="ExternalInput").ap()

    def dout(name, shape):
        return dt(name, list(shape), F32, kind="ExternalOutput").ap()

    X = {'p': din("x_p", [TP, D]), 's': din("x_s", [TS, D])}
    st_gla = din("st_gla", [L, 4, 64, 128]); st_gdn = din("st_gdn", [L, 6, 128, 128])
    st_conv = din("st_conv", [L, 3, 2304]); st_rw = din("st_rw", [L, 12, 64, 64]); st_shift = din("st_shift", [L, 1, 2560])
    norm1_g = din("norm1_g", [L, D]); w_in = din("w_in", [L, D, PROJ])
    gla_a_up = din("gla_a_up", [L, 16, 256]); gla_a_bias = din("gla_a_bias", [L, 256]); gla_norm_g = din("gla_norm_g", [L, 128])
    gdn_conv_w = din("gdn_conv_w", [L, 4, 2304]); gdn_A_log = din("gdn_A_log", [L, 6]); gdn_dt_bias = din("gdn_dt_bias", [L, 6])
    gdn_norm_g = din("gdn_norm_g", [L, 128])
    rw_mu = din("rw_mu", [L, 2560]); rw_w0 = din("rw_w0", [L, 768]); rw_w_up = din("rw_w_up", [L, 64, 768])
    rw_a0 = din("rw_a0", [L, 768]); rw_a_up = din("rw_a_up", [L, 64, 768])
    LV = max(L - 1, 1)
    rw_v0 = din("rw_v0", [LV, 768]); rw_v_down = din("rw_v_down", [LV, 768, 32]); rw_v_up = din("rw_v_up", [LV, 32, 768])
    rw_g_up = din("rw_g_up", [L, 128, 768]); rw_k_k = din("rw_k_k", [L, 768]); rw_k_a = din("rw_k_a", [L, 768])
    rw_r_k = din("rw_r_k", [L, 12, 64]); rw_ln_g = din("rw_ln_g", [L, 768]); rw_ln_b = din("rw_ln_b", [L, 768])
    w_out = din("w_out", [L, D, D]); norm2_g = din("norm2_g", [L, D])
    w_gate = din("w_ffn_gate", [L, D, DFF]); w_up = din("w_ffn_up", [L, D, DFF]); w_down = din("w_ffn_down", [L, DFF, D])
    final_g = din("final_norm_g", [D])
    cst = din("cst", [128, 448])
    Y = {'p': dout("y_p", [TP, D]), 's': dout("y_s", [TS, D])}
    O_gla = {k: dout("gla_" + k, [L, 4, 64, 128]) for k in 'ps'}
    O_gdn = {k: dout("gdn_" + k, [L, 6, 128, 128]) for k in 'ps'}
    O_conv = {k: dout("conv_" + k, [L, 3, 2304]) for k in 'ps'}
    O_rw = {k: dout("rwkv_" + k, [L, 12, 64, 64]) for k in 'ps'}
    O_shift = {k: dout("shift_" + k, [L, 1, 2560]) for k in 'ps'}

    NT0 = min(NTMAX, TP)
    NTW = NT0 + 4

    def sb(name, shape, dtype=F32):
        return Buf(nc.alloc_sbuf_tensor(name, list(shape), dtype))

    xT = [sb("xT%d" % k, [128, NT0]) for k in range(KT)]
    hT = [sb("hT%d" % k, [128, NT0], BF16) for k in range(KT)]
    mixT = [sb("mx%d" % k, [128, NT0], BF16) for k in range(KT)]
    xio = sb("xio", [128, 512])
    NF, NB, NW = 19, 74, 20
    FW = max(NTW, 384)
    fpool = [sb("fp%d" % i, [128, FW]) for i in range(NF)]
    bpool = [sb("bp%d" % i, [128, NT0], BF16) for i in range(NB)]
    wpool = [sb("wp%d" % i, [128, 512], BF16) for i in range(NW)]
    ffree, bfree, wfree = list(range(NF)), list(range(NB)), list(range(NW))

    def wa():
        return wpool[wfree.pop(0)]

    def wfr(*bs):
        for b in bs:
            wfree.append(wpool.index(b))

    def fa():
        return fpool[ffree.pop(0)]

    def ba():
        return bpool[bfree.pop(0)]

    def ffr(*bs):
        for b in bs:
            ffree.append(fpool.index(b))

    def bfr(*bs):
        for b in bs:
            bfree.append(bpool.index(b))

    NSLOT = 2
    wslots = [[sb("w%d_%d" % (i, q), [128, 4, 512], BF16) for q in range(4)] for i in range(NSLOT)]
    wctr = [0]
    Sgla = [sb("Sgla%d" % l, [128, 2, 128]) for l in range(L)]
    Sgdn = [sb("Sgdn%d" % l, [128, 6, 128]) for l in range(L)]
    Srw = [sb("Srw%d" % l, [128, 6, 64]) for l in range(L)]
    chist = [sb("chist%d" % l, [128, 18, 3]) for l in range(L)]
    shist = [sb("shist%d" % l, [128, 21]) for l in range(L)]
    Sbf_gla = sb("Sbf_gla", [128, 2, 128], BF16); Sbf_gdn = sb("Sbf_gdn", [128, 6, 128], BF16); Sbf_rw = sb("Sbf_rw", [128, 6, 64], BF16)
    oT = sb("oT", [128, 6, NT0])
    browall = sb("browall", [128, 6, NT0])
    vfirst = sb("vfirst", [128, 6, NT0], BF16)
    NCHM = max(NT0 // 64, 1)
    ebC = sb("ebC", [128, 6, NCHM])
    cstb = sb("cstb", [128, 448])
    ident_f = cstb[:, 0:128]; bd64_f = cstb[:, 128:256]
    mUi = cstb[0:64, 256:320]; mUs = cstb[0:64, 320:384]; mLs = cstb[0:64, 384:448]
    ones6 = sb("ones6", [6, 128])
    ident_b = sb("ident_b", [128, 128], BF16); ones_b = sb("ones_b", [128, 128], BF16)
    bd64_b = sb("bd64_b", [128, 128], BF16); bd64s_b = sb("bd64s_b", [128, 128], BF16)
    nUi = sb("nUi", [64, 64]); nLs = sb("nLs", [64, 64])
    notstart = sb("notstart", [128, NT0])
    g1 = sb("g1", [128, L, 16]); g2 = sb("g2", [128, L, 16]); gf = sb("gf", [128, 16])
    a_up_b = sb("a_up_b", [16, 256], BF16); nabias = sb("nabias", [128, L, 2]); glan = sb("glan", [128, L])
    cw = sb("cw", [128, L, 4, 18]); negA = sb("negA", [6, L]); dtb = sb("dtb", [6, L]); gdnn = sb("gdnn", [128, L])
    mu_rkv = sb("mu_rkv", [128, L, 18]); mu_w = sb("mu_w", [64, L]); mu_a = sb("mu_a", [64, L]); mu_g = sb("mu_g", [128, L])
    w0 = sb("w0", [128, L, 6]); a0 = sb("a0", [128, L, 6]); v0 = sb("v0", [128, LV, 6])
    kk_c = sb("kk_c", [128, L, 6]); ka_c = sb("ka_c", [128, L, 6]); omka = sb("omka", [128, L, 6]); rk_c = sb("rk_c", [128, L, 6])
    lng = sb("lng", [128, L, 6]); lnb = sb("lnb", [128, L, 6])
    w_up_b = sb("w_up_b", [64, 768], BF16); a_upr_b = sb("a_upr_b", [64, 768], BF16)
    g_up_b = sb("g_up_b", [128, 768], BF16); v_up_b = sb("v_up_b", [32, 768], BF16); v_dn_b = sb("v_dn_b", [128, 6, 32], BF16)
    psb = [Buf(nc.alloc_psum_tensor("ps%d" % i, [128, 512], F32)) for i in range(8)]
    for b_ in psb:
        b_.excl = True
    pctr = [0]

    def psn():
        b = psb[pctr[0] % 6]; pctr[0] += 1
        return b

    act, dve, pool, sp = kb.act, kb.dve, kb.pool, kb.sp
    mm, tr, actf, tt, ts, stt, scan, cp, recip, memset = kb.mm, kb.tr, kb.actf, kb.tt, kb.ts, kb.stt, kb.scan, kb.cp, kb.recip, kb.memset
    aff = kb.aff

    def ld(dst, src_ap, q=sp, slow=True):
        kb.dma(q, dst.ap, src_ap, [], [dst.b], slow=slow)

    ld(cstb.v(), cst, slow=False)
    memset(ones6.v(), 1.0)
    cp(ident_b.v(), ident_f); memset(ones_b.v(), 1.0); cp(bd64_b.v(), bd64_f)
    ts(bd64s_b.v(), bd64_f, 1.0 / 64, ALU.mult)
    ts(nUi.v(), mUi, -1.0, ALU.add, -NEG, ALU.mult)
    ts(nLs.v(), mLs, -1.0, ALU.add, -NEG, ALU.mult)

    def pcol(dst, src, pat, **kw):
        ld(dst, src.rearrange(pat, **kw))

    for l in range(L):
        pcol(g1[:, l, :], norm1_g[l], "(k p) -> p k", p=128)
        pcol(g2[:, l, :], norm2_g[l], "(k p) -> p k", p=128)
        pcol(nabias[:, l, :], gla_a_bias[l], "(j p) -> p j", p=128)
        pcol(glan[:, l:l + 1], gla_norm_g[l], "(p o) -> p o", o=1)
        pcol(gdnn[:, l:l + 1], gdn_norm_g[l], "(p o) -> p o", o=1)
        for j in range(4):
            pcol(cw[:, l, j, :], gdn_conv_w[l, j], "(i p) -> p i", p=128)
        pcol(negA[:, l:l + 1], gdn_A_log[l], "(p o) -> p o", o=1)
        pcol(dtb[:, l:l + 1], gdn_dt_bias[l], "(p o) -> p o", o=1)
        pcol(mu_rkv[:, l, :], rw_mu[l, 0:2304], "(i p) -> p i", p=128)
        pcol(mu_w[:, l:l + 1], rw_mu[l, 2304:2368], "(p o) -> p o", o=1)
        pcol(mu_a[:, l:l + 1], rw_mu[l, 2368:2432], "(p o) -> p o", o=1)
        pcol(mu_g[:, l:l + 1], rw_mu[l, 2432:2560], "(p o) -> p o", o=1)
        pcol(w0[:, l, :], rw_w0[l], "(i p) -> p i", p=128)
        pcol(a0[:, l, :], rw_a0[l], "(i p) -> p i", p=128)
        pcol(kk_c[:, l, :], rw_k_k[l], "(i p) -> p i", p=128)
        pcol(ka_c[:, l, :], rw_k_a[l], "(i p) -> p i", p=128)
        pcol(rk_c[:, l, :], rw_r_k[l], "(j i) d -> (i d) j", i=2)
        pcol(lng[:, l, :], rw_ln_g[l], "(i p) -> p i", p=128)
        pcol(lnb[:, l, :], rw_ln_b[l], "(i p) -> p i", p=128)
    for l in range(L - 1):
        pcol(v0[:, l, :], rw_v0[l], "(i p) -> p i", p=128)
    pcol(gf.v(), final_g, "(k p) -> p k", p=128)
    ts(nabias.v(), nabias.v(), -1.0, ALU.mult)
    actf(negA.v(), negA.v(), AF.Exp)
    ts(negA.v(), negA.v(), -1.0, ALU.mult)
    ts(omka.v(), ka_c.v(), -1.0, ALU.mult, 1.0, ALU.add)

    def wload(src2d, r0, kt, c0, ncols):
        slot = wslots[wctr[0] % NSLOT]; wctr[0] += 1
        for q, (k0, kn) in enumerate(_chunks(kt, 4)):
            src = src2d[r0 + k0 * 128: r0 + (k0 + kn) * 128, c0:c0 + ncols].rearrange("(k p) n -> p k n", p=128)
            kb.dma(pool, slot[q].t[:, 0:kn, 0:ncols], src, [], [slot[q]])
        return slot

    def wk(slot, k, c0, c1):
        return slot[k // 4][:, k % 4, c0:c1]

    def run_seq(sk, T):
        NT = min(NT0, T)
        C = min(64, T)
        NCH = NT // C
        NLEV = int(np.log2(C))
        Xd, Yd = X[sk], Y[sk]
        cs = slice(0, C)
        memset(notstart.v(), 1.0)
        memset(notstart[:, 0:NT].re("p (n c) -> p n c", c=C)[:, :, 0:1], 0.0)
        for l in range(L):
            if sk == 'p':
                for b in (Sgla[l], Sgdn[l], Srw[l], chist[l], shist[l]):
                    memset(b.v(), 0.0)
            else:
                for h in range(4):
                    ld(Sgla[l][(h % 2) * 64:(h % 2) * 64 + 64, h // 2, :], st_gla[l, h], slow=False)
                ld(Sgdn[l].v(), st_gdn[l].rearrange("h d e -> d h e"), slow=False)
                for h in range(12):
                    ld(Srw[l][(h % 2) * 64:(h % 2) * 64 + 64, h // 2, :], st_rw[l, h], slow=False)
                for t_ in range(3):
                    ld(chist[l][:, :, t_], st_conv[l, t_].rearrange("(i p) -> p i", p=128))
                ld(shist[l][:, 0:18], st_shift[l, 0, 0:2304].rearrange("(i p) -> p i", p=128))
                ld(shist[l][0:64, 18:19], st_shift[l, 0, 2304:2368].rearrange("(p o) -> p o", o=1))
                ld(shist[l][0:64, 19:20], st_shift[l, 0, 2368:2432].rearrange("(p o) -> p o", o=1))
                ld(shist[l][:, 20:21], st_shift[l, 0, 2432:2560].rearrange("(p o) -> p o", o=1))

        tok = slice(0, NT)

        def rmsnorm(gcol, outs):
            ps = psn()
            for k in range(KT):
                sq = ba()
                actf(sq[:, tok], xT[k][:, tok], AF.Square)
                mm(ps[:, tok], ones_b.v(), sq[:, tok], start=(k == 0), stop=(k == KT - 1))
                bfr(sq)
            rstd = fa()
            actf(rstd[:, tok], ps[:, tok], AF.Sqrt, scale=1.0 / D, bias=1e-6)
            recip(rstd[:, tok], rstd[:, tok])
            for k in range(KT):
                stt(outs[k][:, tok], xT[k][:, tok], gcol(k), rstd[:, tok], ALU.mult, ALU.mult)
            ffr(rstd)

        def dense_fm(slot, c0, width, kt, rhs, ps, prow=128):
            for k in range(kt):
                mm(ps[0:width, tok], wk(slot, k, c0, c0 + width), rhs(k), start=(k == 0), stop=(k == kt - 1))

        def pnorm(src_v, ones_v, scale, bias, dst):
            sq = ba()
            actf(sq[:, tok], src_v, AF.Square)
            ps = psn()
            mm(ps[:, tok], ones_v, sq[:, tok])
            actf(dst, ps[:, tok], AF.Sqrt, scale=scale, bias=bias)
            recip(dst, dst)
            bfr(sq)

        def tform_gen(Nn, Aa):
            v6 = lambda b_: b_[cs, 0:6 * C].re("p (h c) -> p h c", h=6)
            v6b = lambda b_: b_.v().cast(BF16)[cs, 0:6 * C].re("p (h c) -> p h c", h=6)
            P = fa()
            Pv = v6b(P)
            tt(Pv, Aa, ident_f[cs, cs].un(1).bc([C, 6, C]), ALU.add)
            curN, curA = Nn, Aa
            prev = []
            for lev in range(1, NLEV):
                psN = psn(); pv = v6(psN)
                for h in range(6):
                    mm(pv[:, h, :], curA[:, h, :], curN[:, h, :])
                nA = None; nAv = None
                if lev < NLEV - 1:
                    psA = psn(); pa = v6(psA)
                    for h in range(6):
                        mm(pa[:, h, :], curN[:, h, :], curA[:, h, :])
                nN = fa(); nNv = v6b(nN)
                cp(nNv, pv, eng=act)
                if lev < NLEV - 1:
                    nA = fa(); nAv = v6b(nA)
                    cp(nAv, pa, eng=act)
                yield
                psP = psn(); pp = v6(psP)
                for h in range(6):
                    mm(pp[:, h, :], nNv[:, h, :], Pv[:, h, :])
                tt(Pv, Pv, pp, ALU.add)
                curN, curA = nNv, nAv
                if prev:
                    ffr(*prev)
                prev = [x for x in (nN, nA) if x is not None]
                yield
            if prev:
                ffr(*prev)
            return P, Pv

        def run_gen(g_):
            try:
                while True:
                    next(g_)
            except StopIteration as e_:
                return e_.value

        def tform(Nn, Aa):
            return run_gen(tform_gen(Nn, Aa))

        def interleave(gens):
            gens = list(gens)
            res = {}
            live = list(gens)
            while live:
                for g_ in list(live):
                    try:
                        next(g_)
                    except StopIteration as e_:
                        res[g_] = e_.value
                        live.remove(g_)
            return [res[g_] for g_ in gens]

        def headnorm_out(o_v, ones_v, nrm_scale, gcol, gate_b, dst):
            rstd = fa()
            pnorm(o_v, ones_v, nrm_scale, 1e-6, rstd[:, tok])
            t1 = fa()
            stt(t1[:, tok], o_v, gcol, rstd[:, tok], ALU.mult, ALU.mult)
            tt(dst, t1[:, tok], gate_b, ALU.mult)
            ffr(rstd, t1)

        def mixer(l):
            Wl = w_in[l]
            hr = lambda k: hT[k][:, tok]
            ld(a_up_b.v(), gla_a_up[l], q=pool, slow=False)
            ld(w_up_b.v(), rw_w_up[l], q=pool, slow=False)
            ld(a_upr_b.v(), rw_a_up[l], q=pool, slow=False)
            ld(g_up_b.v(), rw_g_up[l], q=pool, slow=False)
            if l > 0:
                ld(v_up_b.v(), rw_v_up[l - 1], q=pool, slow=False)
                ld(v_dn_b.v(), rw_v_down[l - 1].rearrange("(k p) r -> p k r", p=128), q=pool, slow=False)
            phase(2)
            s1 = wload(Wl, 0, KT, GLA0 + 1024, 272)
            ps = psn(); dense_fm(s1, 0, 16, KT, hr, ps)
            aT = ba(); cp(aT[0:16, tok], ps[0:16, tok], eng=act)
            sg = []
            s2 = None
            for i in range(4):
                if i == 2:
                    s2 = wload(Wl, 0, KT, GLA0 + 1296, 256)
                ps = psn()
                dense_fm(s1 if i < 2 else s2, (16 + i * 128) if i < 2 else (i - 2) * 128, 128, KT, hr, ps)
                g = ba(); actf(g[:, tok], ps[:, tok], AF.Silu); sg.append(g)
            phase(2.1)
            EP, EM, ED = [], [], []
            for j in range(2):
                ps = psn()
                mm(ps[:, tok], a_up_b[0:16, j * 128:(j + 1) * 128], aT[0:16, tok])
                e = fa()
                actf(e[:, tok], ps[:, tok], AF.Exp, scale=-1.0, bias=nabias[:, l, j:j + 1])
                actf(e[:, tok], e[:, tok], AF.Ln, bias=1.0)
                csm = fa()
                scan(csm[:, tok], notstart[:, tok], e[:, tok])
                ep = fa(); em = fa(); ed = fa()
                actf(ep[:, tok], csm[:, tok], AF.Exp, scale=-1.0 / 16, bias=float(np.log(0.125)))
                actf(em[:, tok], csm[:, tok], AF.Exp, scale=1.0 / 16)
                c3 = csm[:, tok].re("p (n c) -> p n c", c=C)
                tt(e[:, tok].re("p (n c) -> p n c", c=C), c3[:, :, C - 1:C].bc([128, NCH, C]), c3, ALU.subtract)
                actf(ed[:, tok], e[:, tok], AF.Exp, scale=-1.0 / 16)
                actf(ebC[:, j, 0:NCH], c3[:, :, C - 1], AF.Exp, scale=-1.0 / 16)
                EP.append(ep); EM.append(em); ED.append(ed)
                ffr(e, csm)
            bfr(aT)
            phase(2.2)
            qe, ke, kd = [], [], []
            s3 = wload(Wl, 0, KT, GLA0 + 0, 512)
            for j in range(2):
                ps = psn(); dense_fm(s3, j * 128, 128, KT, hr, ps)
                for i in range(2):
                    r0 = i * 64
                    q = ba()
                    tt(q[r0:r0 + 64, tok], ps[r0:r0 + 64, tok], EP[j][r0:r0 + 64, tok], ALU.mult)
                    memset(q[64 - r0:128 - r0, tok], 0.0)
                    qe.append(q)
            for j in range(2):
                ps = psn(); dense_fm(s3, 256 + j * 128, 128, KT, hr, ps)
                k1 = ba(); tt(k1[:, tok], ps[:, tok], EM[j][:, tok], ALU.mult); ke.append(k1)
                k2 = ba(); tt(k2[:, tok], ps[:, tok], ED[j][:, tok], ALU.mult); kd.append(k2)
            ffr(*EP, *EM, *ED)
            phase(2.3)
            s4 = wload(Wl, 0, KT, GLA0 + 512, 512)
            cp(Sbf_gla.v(), Sgla[l].v(), eng=act)
            for c in range(NCH):
                cc = slice(c * C, (c + 1) * C)
                ps = psn()
                for k in range(KT):
                    mm(ps[cs, 0:512], hT[k][:, cc], wk(s4, k, 0, 512), start=(k == 0), stop=(k == KT - 1))
                vt = wa()
                cp(vt[cs, 0:512], ps[cs, 0:512], eng=act)
                vtv = lambda h: vt[cs, h * 128:(h + 1) * 128]
                pst = psn()
                ptb = pst.v()
                for j in range(2):
                    mm(ptb[cs, j * 128:(j + 1) * 128], kd[j][:, cc], ident_b.v())
                kdt = wa()
                cp(kdt[cs, 0:256], ptb[cs, 0:256])
                pss = psn()
                for h in range(4):
                    r0 = (h % 2) * 64
                    mm(pss[cs, h * C:(h + 1) * C], ke[h // 2][:, cc], qe[h][:, cc])
                scm = wa()
                tt(scm[cs, 0:4 * C].re("p (h c) -> p h c", h=4), pss[cs, 0:4 * C].re("p (h c) -> p h c", h=4),
                   mUi[cs, cs].un(1).bc([C, 4, C]), ALU.mult)
                po = psn()
                for h in range(4):
                    r0 = (h % 2) * 64
                    mm(po[:, h * C:(h + 1) * C], Sbf_gla[:, h // 2, :], qe[h][:, cc], start=True, stop=False)
                    mm(po[:, h * C:(h + 1) * C], vtv(h), scm[cs, h * C:(h + 1) * C], start=False, stop=True)
                cp(oT[:, 0:4, cc], po[:, 0:4 * C].re("p (h c) -> p h c", h=4), eng=act)
                pS = psn()
                for h in range(4):
                    r0 = (h % 2) * 64
                    mm(pS[r0:r0 + 64, (h // 2) * 128:(h // 2) * 128 + 128], kdt[cs, h * 64:(h + 1) * 64], vtv(h))
                for j in range(2):
                    stt(Sgla[l][:, j, :], Sgla[l][:, j, :], ebC[:, j, c:c + 1], pS[:, j * 128:(j + 1) * 128], ALU.mult, ALU.add)
                cp(Sbf_gla.v(), Sgla[l].v(), eng=act)
                wfr(vt, kdt, scm)
            bfr(*qe, *ke, *kd)
            for h in range(4):
                headnorm_out(oT[:, h, tok], ones_b.v(), 1.0 / 128, glan[:, l:l + 1], sg[h][:, tok], mixT[h][:, tok])
            bfr(*sg)

            phase(3)
            sA = wload(Wl, 0, KT, GDN0 + 2304, 396)
            sB = wload(Wl, 0, KT, GDN0 + 2304 + 396, 384)
            ebrow = [ba() for _ in range(6)]; betarow = [ba() for _ in range(6)]
            ps = psn(); dense_fm(sA, 0, 6, KT, hr, ps)
            bT6 = fa(); actf(bT6[0:6, tok], ps[0:6, tok], AF.Sigmoid)
            ps = psn(); dense_fm(sA, 6, 6, KT, hr, ps)
            b6 = fa(); e6 = fa()
            actf(e6[0:6, tok], ps[0:6, tok], AF.Exp, bias=dtb[:, l:l + 1])
            actf(e6[0:6, tok], e6[0:6, tok], AF.Ln, bias=1.0)
            ts(e6[0:6, tok], e6[0:6, tok], negA[:, l:l + 1], ALU.mult)
            scan(b6[0:6, tok], notstart[0:6, tok], e6[0:6, tok])
            ffr(e6)
            for h in range(6):
                msk_ = fa()
                ts(msk_[0:6, tok], b6[0:6, tok], ident_f[0:6, h:h + 1], ALU.mult)
                ps = psn()
                mm(ps[:, tok], ones6.v(), msk_[0:6, tok])
                cp(browall[:, h, tok], ps[:, tok], eng=act)
                ts(msk_[0:6, tok], bT6[0:6, tok], ident_f[0:6, h:h + 1], ALU.mult)
                ps = psn()
                mm(ps[:, tok], ones6.v(), msk_[0:6, tok])
                cp(betarow[h][:, tok], ps[:, tok], eng=act)
                ffr(msk_)
            for h in range(6):
                actf(ebrow[h][:, tok], browall[:, h, tok], AF.Exp)
            b4 = browall[:, :, tok].re("p h (n c) -> p h n c", c=C)
            actf(ebC[:, :, 0:NCH], b4[:, :, :, C - 1], AF.Exp)
            sgd = []
            for i in range(6):
                ps = psn()
                if i < 3:
                    dense_fm(sA, 12 + i * 128, 128, KT, hr, ps)
                else:
                    dense_fm(sB, (i - 3) * 128, 128, KT, hr, ps)
                g = ba(); actf(g[:, tok], ps[:, tok], AF.Silu); sgd.append(g)
            qn, qe, kn, kbt, kbe, kd, vb = [], [], [], [], [], [], []
            slot = None
            for i in range(18):
                if i % 4 == 0:
                    slot = wload(Wl, 0, KT, GDN0 + i * 128, min(512, 2304 - i * 128))
                ps = psn(); dense_fm(slot, (i % 4) * 128, 128, KT, hr, ps)
                xc = fa()
                cp(xc[:, 0:3], chist[l][:, i, :])
                cp(xc[:, 3:3 + NT], ps[:, tok], eng=act)
                acc = fa()
                aff(acc[:, tok], xc[:, 0:NT], cw[:, l, 0, i:i + 1])
                for j in range(1, 4):
                    stt(acc[:, tok], xc[:, j:j + NT], cw[:, l, j, i:i + 1], acc[:, tok], ALU.mult, ALU.add)
                cp(chist[l][:, i, :], xc[:, NT:NT + 3])
                h = i % 6
                if i < 12:
                    sl = fa()
                    actf(sl[:, tok], acc[:, tok], AF.Silu)
                    rn = fa()
                    if i < 6:
                        pnorm(sl[:, tok], ones_b.v(), 128.0, 128e-6, rn[:, tok])
                        a = ba(); tt(a[:, tok], sl[:, tok], rn[:, tok], ALU.mult); qn.append(a)
                        b = ba(); tt(b[:, tok], a[:, tok], ebrow[h][:, tok], ALU.mult); qe.append(b)
                    else:
                        pnorm(sl[:, tok], ones_b.v(), 1.0, 1e-6, rn[:, tok])
                        a = ba(); tt(a[:, tok], sl[:, tok], rn[:, tok], ALU.mult); kn.append(a)
                        b = ba(); tt(b[:, tok], a[:, tok], betarow[h][:, tok], ALU.mult); kbt.append(b)
                        b2 = ba(); tt(b2[:, tok], b[:, tok], ebrow[h][:, tok], ALU.mult); kbe.append(b2)
                        dd = rn
                        tt(dd[:, tok].re("p (n c) -> p n c", c=C), b4[:, h, :, C - 1:C].bc([128, NCH, C]), b4[:, h, :, :], ALU.subtract)
                        actf(dd[:, tok], dd[:, tok], AF.Exp)
                        b3 = ba(); tt(b3[:, tok], a[:, tok], dd[:, tok], ALU.mult); kd.append(b3)
                    ffr(sl, rn)
                else:
                    sl = fa()
                    actf(sl[:, tok], acc[:, tok], AF.Silu)
                    a = ba(); tt(a[:, tok], sl[:, tok], betarow[h][:, tok], ALU.mult); vb.append(a)
                    ffr(sl)
                ffr(xc, acc)
            bfr(*ebrow, *betarow)
            cp(Sbf_gdn.v(), Sgdn[l].v(), eng=act)
            g6 = lambda b_: b_[cs, 0:6 * C].re("p (h c) -> p h c", h=6)

            def gdn_A(c):
                cc = slice(c * C, (c + 1) * C)

                def trans6(srcs):
                    outs = []
                    for g in range(2):
                        p = psn(); pb = p.v()
                        for hh in range(3):
                            mm(pb[cs, hh * 128:(hh + 1) * 128], srcs[g * 3 + hh][:, cc], ident_b.v())
                        o = wa(); cp(o[cs, 0:384], pb[cs, 0:384], eng=act); outs.append(o)
                    return outs
                p = psn()
                mm(p[cs, 0:6], b6[0:6, cc], ident_f[0:6, 0:6])
                btok = fa(); cp(btok[cs, 0:6], p[cs, 0:6], eng=act)
                vbt = trans6(vb)
                yield
                Dm = fa(); Dv = g6(Dm)
                tt(Dv, browall[cs, :, cc], btok[cs, 0:6].un(2).bc([C, 6, C]), ALU.subtract)
                aT_ = fa(); aTv = g6(aT_)
                tt(aTv, Dv, nUi[cs, cs].un(1).bc([C, 6, C]), ALU.add)
                actf(aTv, aTv, AF.Exp)
                a2 = fa(); a2v = g6(a2)
                tt(a2v, nLs[cs, cs].un(1).bc([C, 6, C]), Dv, ALU.subtract)
                actf(a2v, a2v, AF.Exp)
                tt(Dv, aTv, mUs[cs, cs].un(1).bc([C, 6, C]), ALU.mult)
                kdt = trans6(kd)
                yield
                pk1 = psn(); pk2 = psn(); pq = psn()
                v1 = g6(pk1); v2 = g6(pk2); v3 = g6(pq)
                for h in range(6):
                    mm(v1[:, h, :], kn[h][:, cc], kbt[h][:, cc])
                    mm(v2[:, h, :], kbt[h][:, cc], kn[h][:, cc])
                    mm(v3[:, h, :], kn[h][:, cc], qn[h][:, cc])
                g6b = lambda b_: b_.v().cast(BF16)[cs, 0:6 * C].re("p (h c) -> p h c", h=6)
                Aa = fa(); Aav = g6b(Aa)
                stt(Aav, v1, -1.0, Dv, ALU.mult, ALU.mult)
                Nn = fa(); Nnv = g6b(Nn)
                stt(Nnv, v2, -1.0, a2v, ALU.mult, ALU.mult)
                PT = wa()
                tt(g6(PT), v3, aTv, ALU.mult)
                ffr(Dm, aT_, a2, btok)
                yield
                P, Pv = yield from tform_gen(Nnv, Aav)
                ffr(Aa, Nn)
                return (vbt, kdt, PT, P, Pv)

            def gdn_B(c, aout):
                cc = slice(c * C, (c + 1) * C)
                vbt, kdt, PT, P, Pv = aout
                hv = lambda lst, h: lst[h // 3][cs, (h % 3) * 128:(h % 3) * 128 + 128]
                ptv = lambda h: PT[cs, h * C:(h + 1) * C]
                X0 = [fa(), fa()]; U = [wa(), wa()]
                for g in range(2):
                    p = psn()
                    for hh in range(3):
                        h = g * 3 + hh
                        mm(p[cs, hh * 128:(hh + 1) * 128], kbe[h][:, cc], Sbf_gdn[:, h, :])
                    tt(X0[g].v().cast(BF16)[cs, 0:384], vbt[g][cs, 0:384], p[cs, 0:384], ALU.subtract)
                yield
                for g in range(2):
                    p = psn()
                    for hh in range(3):
                        h = g * 3 + hh
                        mm(p[cs, hh * 128:(hh + 1) * 128], Pv[:, h, :], X0[g].v().cast(BF16)[cs, hh * 128:(hh + 1) * 128])
                    cp(U[g][cs, 0:384], p[cs, 0:384], eng=act)
                yield
                po = psn()
                for h in range(6):
                    mm(po[:, h * C:(h + 1) * C], Sbf_gdn[:, h, :], qe[h][:, cc], start=True, stop=False)
                    mm(po[:, h * C:(h + 1) * C], hv(U, h), ptv(h), start=False, stop=True)
                cp(oT[:, :, cc], po[:, 0:6 * C].re("p (h c) -> p h c", h=6), eng=act)
                pS = [psn(), psn()]
                for h in range(6):
                    mm(pS[h // 3][:, (h % 3) * 128:(h % 3) * 128 + 128], hv(kdt, h), hv(U, h))
                tt(Sgdn[l].v(), Sgdn[l].v(), ebC[:, :, c:c + 1].bc([128, 6, 128]), ALU.mult)
                for g in range(2):
                    tt(Sgdn[l][:, g * 3:g * 3 + 3, :], Sgdn[l][:, g * 3:g * 3 + 3, :],
                       pS[g][:, 0:384].re("p (h e) -> p h e", h=3), ALU.add)
                cp(Sbf_gdn.v(), Sgdn[l].v(), eng=act)
                ffr(P, *X0); wfr(PT, *U, *vbt, *kdt)

            aout = interleave([gdn_A(0)])[0]
            for c in range(NCH):
                gens = [gdn_B(c, aout)]
                if c + 1 < NCH:
                    gens.append(gdn_A(c + 1))
                res_ = interleave(gens)
                if c + 1 < NCH:
                    aout = res_[1]
            bfr(*qn, *qe, *kn, *kbt, *kbe, *kd, *vb)
            ffr(bT6, b6)
            for h in range(6):
                headnorm_out(oT[:, h, tok], ones_b.v(), 1.0 / 128, gdnn[:, l:l + 1], sgd[h][:, tok], mixT[4 + h][:, tok])
            bfr(*sgd)

            phase(4)
            def shiftmix(ps_v, rows, hcol, mucol, dst_v):
                zb = fa()
                cp(zb[0:rows, 0:1], shist[l][0:rows, hcol:hcol + 1])
                cp(zb[0:rows, 1:1 + NT], ps_v, eng=act)
                d = fa()
                tt(d[0:rows, tok], zb[0:rows, 0:NT], zb[0:rows, 1:1 + NT], ALU.subtract)
                stt(dst_v, d[0:rows, tok], mucol, zb[0:rows, 1:1 + NT], ALU.mult, ALU.add)
                cp(shist[l][0:rows, hcol:hcol + 1], zb[0:rows, NT:NT + 1])
                ffr(zb, d)

            sL = wload(Wl, 0, KT, RW0 + 2304, 256)
            tmpf = fa()
            ps = psn(); dense_fm(sL, 0, 64, KT, hr, ps)
            shiftmix(ps[0:64, tok], 64, 18, mu_w[:, l:l + 1], tmpf[0:64, tok])
            twT = ba(); actf(twT[0:64, tok], tmpf[0:64, tok], AF.Tanh)
            ps = psn(); dense_fm(sL, 64, 64, KT, hr, ps)
            xaT = ba(); shiftmix(ps[0:64, tok], 64, 19, mu_a[:, l:l + 1], xaT[0:64, tok])
            ps = psn(); dense_fm(sL, 128, 128, KT, hr, ps)
            shiftmix(ps[:, tok], 128, 20, mu_g[:, l:l + 1], tmpf[:, tok])
            sgT = ba(); actf(sgT[:, tok], tmpf[:, tok], AF.Sigmoid)
            ffr(tmpf)
            rT, kT_, vT = [], [], []
            slot = None
            for i in range(18):
                if i % 4 == 0:
                    slot = wload(Wl, 0, KT, RW0 + i * 128, min(512, 2304 - i * 128))
                ps = psn(); dense_fm(slot, (i % 4) * 128, 128, KT, hr, ps)
                z = ba()
                shiftmix(ps[:, tok], 128, i, mu_rkv[:, l, i:i + 1], z[:, tok])
                (rT if i < 6 else kT_ if i < 12 else vT).append(z)
            if l == 0:
                for j in range(6):
                    cp(vfirst[:, j, tok], vT[j][:, tok])
            else:
                ps = psn()
                for j in range(6):
                    mm(ps[0:32, tok], v_dn_b[:, j, :], vT[j][:, tok], start=(j == 0), stop=(j == 5))
                t1 = ba(); cp(t1[0:32, tok], ps[0:32, tok], eng=act)
                for j in range(6):
                    ps = psn(); mm(ps[:, tok], v_up_b[0:32, j * 128:(j + 1) * 128], t1[0:32, tok])
                    nu = fa(); actf(nu[:, tok], ps[:, tok], AF.Sigmoid, bias=v0[:, l - 1, j:j + 1])
                    d = fa()
                    tt(d[:, tok], vfirst[:, j, tok], vT[j][:, tok], ALU.subtract)
                    tt(d[:, tok], d[:, tok], nu[:, tok], ALU.mult)
                    tt(vT[j][:, tok], vT[j][:, tok], d[:, tok], ALU.add)
                    ffr(nu, d)
                bfr(t1)
            rt, kt_, at, kpt, kdl, adl, bonus, gate = [], [], [], [], [], [], [], []
            for j in range(6):
                ps = psn(); mm(ps[:, tok], w_up_b[0:64, j * 128:(j + 1) * 128], twT[0:64, tok])
                sig = fa(); actf(sig[:, tok], ps[:, tok], AF.Sigmoid, bias=w0[:, l, j:j + 1])
                csm = fa(); scan(csm[:, tok], notstart[:, tok], sig[:, tok])
                ps = psn(); mm(ps[:, tok], a_upr_b[0:64, j * 128:(j + 1) * 128], xaT[0:64, tok])
                aa = fa(); actf(aa[:, tok], ps[:, tok], AF.Sigmoid, bias=a0[:, l, j:j + 1])
                ps = psn(); mm(ps[:, tok], g_up_b[:, j * 128:(j + 1) * 128], sgT[:, tok])
                g_ = ba(); cp(g_[:, tok], ps[:, tok], eng=act); gate.append(g_)
                kr = fa()
                aff(kr[:, tok], kT_[j][:, tok], kk_c[:, l, j:j + 1])
                rn = fa()
                pnorm(kr[:, tok], bd64_b.v(), 1.0, 1e-6, rn[:, tok])
                kk = kr
                tt(kk[:, tok], kr[:, tok], rn[:, tok], ALU.mult)
                ka = rn
                tt(ka[:, tok], kk[:, tok], aa[:, tok], ALU.mult)
                k2 = fa()
                aff(k2[:, tok], aa[:, tok], ka_c[:, l, j:j + 1], omka[:, l, j:j + 1])
                tt(k2[:, tok], k2[:, tok], kT_[j][:, tok], ALU.mult)
                rk = ba()
                stt(rk[:, tok], rT[j][:, tok], rk_c[:, l, j:j + 1], k2[:, tok], ALU.mult, ALU.mult)
                ps = psn(); mm(ps[:, tok], bd64_b.v(), rk[:, tok])
                bo = ba(); tt(bo[:, tok], ps[:, tok], vT[j][:, tok], ALU.mult); bonus.append(bo)
                bfr(rk)
                e = fa()
                Epl = ba()
                actf(Epl[:, tok], csm[:, tok], AF.Exp, scale=-K0)
                c3 = csm[:, tok].re("p (n c) -> p n c", c=C)
                actf(ebC[:, j, 0:NCH], c3[:, :, C - 1], AF.Exp, scale=-K0)
                for i_ in range(2):
                    q0 = i_ * 64
                    a_ = ba()
                    tt(a_[q0:q0 + 64, tok], rT[j][q0:q0 + 64, tok], Epl[q0:q0 + 64, tok], ALU.mult)
                    memset(a_[64 - q0:128 - q0, tok], 0.0)
                    rt.append(a_)
                bfr(Epl)
                actf(e[:, tok], csm[:, tok], AF.Exp, scale=K0)
                a_ = ba(); tt(a_[:, tok], k2[:, tok], e[:, tok], ALU.mult); kt_.append(a_)
                a_ = ba(); tt(a_[:, tok], ka[:, tok], e[:, tok], ALU.mult); at.append(a_)
                tt(e[:, tok], csm[:, tok], sig[:, tok], ALU.subtract)
                actf(e[:, tok], e[:, tok], AF.Exp, scale=-K0)
                for i_ in range(2):
                    q0 = i_ * 64
                    a_ = ba()
                    tt(a_[q0:q0 + 64, tok], kk[q0:q0 + 64, tok], e[q0:q0 + 64, tok], ALU.mult)
                    memset(a_[64 - q0:128 - q0, tok], 0.0)
                    kpt.append(a_)
                tt(e[:, tok].re("p (n c) -> p n c", c=C), c3[:, :, C - 1:C].bc([128, NCH, C]), c3, ALU.subtract)
                actf(e[:, tok], e[:, tok], AF.Exp, scale=-K0)
                a_ = ba(); tt(a_[:, tok], k2[:, tok], e[:, tok], ALU.mult); kdl.append(a_)
                a_ = ba(); tt(a_[:, tok], ka[:, tok], e[:, tok], ALU.mult); adl.append(a_)
                ffr(kr, rn, k2, e, sig, csm, aa)
                bfr(rT[j], kT_[j])
            bfr(twT, xaT, sgT)
            vbl = vT
            cp(Sbf_rw.v(), Srw[l].v(), eng=act)

            v6 = lambda b_: b_[cs, 0:6 * C].re("p (h c) -> p h c", h=6)
            v6b = lambda b_: b_.v().cast(BF16)[cs, 0:6 * C].re("p (h c) -> p h c", h=6)

            def rw_A(c, g):
                cc = slice(c * C, (c + 1) * C)
                Aa = fa(); Nn = fa(); m3 = []
                kinds = ((at, kpt, 0), (kpt, at, 1), (kt_, kpt, 2), (at, rt, 3), (kt_, rt, 4))
                for (la, lb, idx) in kinds:
                    p = psn(); pv = v6(p)
                    for hh in range(6):
                        j = g * 3 + hh // 2; hg = g * 6 + hh
                        x_ = (la[hg] if (la is kpt or la is rt) else la[j])[:, cc]
                        y_ = (lb[hg] if (lb is kpt or lb is rt) else lb[j])[:, cc]
                        mm(pv[:, hh, :], x_, y_)
                    if idx == 0:
                        stt(v6b(Aa), pv, -1.0, mUs[cs, cs].un(1).bc([C, 6, C]), ALU.mult, ALU.mult)
                    elif idx == 1:
                        stt(v6b(Nn), pv, -1.0, mLs[cs, cs].un(1).bc([C, 6, C]), ALU.mult, ALU.mult)
                    else:
                        o = wa()
                        msk = mUs if idx == 2 else mUi
                        tt(v6(o), pv, msk[cs, cs].un(1).bc([C, 6, C]), ALU.mult)
                        m3.append(o)
                    if idx in (1, 4):
                        yield
                P, Pv = yield from tform_gen(v6b(Nn), v6b(Aa))
                ffr(Aa, Nn)
                return (m3, P, Pv)

            def rw_B(c, g, aout, po, pS):
                cc = slice(c * C, (c + 1) * C)
                m3, P, Pv = aout
                mv = lambda k, hh: m3[k][cs, hh * C:(hh + 1) * C]

                def trans3(srcs, e_):
                    p = psn(); pb = p.v()
                    for jj in range(3):
                        mm(pb[cs, jj * 128:(jj + 1) * 128], srcs[g * 3 + jj][:, cc], ident_b.v())
                    o = wa(); cp(o[cs, 0:384], pb[cs, 0:384], eng=e_); return o
                vtk = trans3(vbl, act)
                yield
                p = psn()
                for hh in range(6):
                    j = g * 3 + hh // 2
                    mm(p[cs, hh * 64:(hh + 1) * 64], kpt[g * 6 + hh][:, cc], Sbf_rw[:, j, :], start=True, stop=False)
                    mm(p[cs, hh * 64:(hh + 1) * 64], mv(0, hh), vtk[cs, hh * 64:(hh + 1) * 64], start=False, stop=True)
                X0 = fa(); X0b = X0.v().cast(BF16)
                aff(X0b[cs, 0:384], p[cs, 0:384], -1.0)
                adt = trans3(adl, act); kdt = trans3(kdl, act)
                yield
                p = psn()
                for hh in range(6):
                    mm(p[cs, hh * 64:(hh + 1) * 64], Pv[:, hh, :], X0b[cs, hh * 64:(hh + 1) * 64])
                U = wa(); cp(U[cs, 0:384], p[cs, 0:384], eng=act)
                yield
                for hh in range(6):
                    j = g * 3 + hh // 2; r0 = (hh % 2) * 64
                    ov = po[r0:r0 + 64, j * C:(j + 1) * C]
                    mm(ov, Sbf_rw[:, j, :], rt[g * 6 + hh][:, cc], start=True, stop=False)
                    mm(ov, U[cs, hh * 64:(hh + 1) * 64], mv(1, hh), start=False, stop=False)
                    mm(ov, vtk[cs, hh * 64:(hh + 1) * 64], mv(2, hh), start=False, stop=True)
                for hh in range(6):
                    j = g * 3 + hh // 2; r0 = (hh % 2) * 64
                    sv = pS[r0:r0 + 64, j * 64:(j + 1) * 64]
                    mm(sv, adt[cs, hh * 64:(hh + 1) * 64], U[cs, hh * 64:(hh + 1) * 64], start=True, stop=False)
                    mm(sv, kdt[cs, hh * 64:(hh + 1) * 64], vtk[cs, hh * 64:(hh + 1) * 64], start=False, stop=True)
                ffr(P, X0); wfr(U, adt, kdt, vtk, *m3)

            aouts = interleave([rw_A(0, 0), rw_A(0, 1)])
            for c in range(NCH):
                cc = slice(c * C, (c + 1) * C)
                po = psb[6]; pS = psb[7]
                gens = [rw_B(c, 0, aouts[0], po, pS), rw_B(c, 1, aouts[1], po, pS)]
                if c + 1 < NCH:
                    gens += [rw_A(c + 1, 0), rw_A(c + 1, 1)]
                res_ = interleave(gens)
                if c + 1 < NCH:
                    aouts = res_[2:4]
                cp(oT[:, :, cc], po[:, 0:6 * C].re("p (h c) -> p h c", h=6), eng=act)
                tt(Srw[l].v(), Srw[l].v(), ebC[:, :, c:c + 1].bc([128, 6, 64]), ALU.mult)
                tt(Srw[l].v(), Srw[l].v(), pS[:, 0:384].re("p (h e) -> p h e", h=6), ALU.add)
                cp(Sbf_rw.v(), Srw[l].v(), eng=act)
            bfr(*rt, *kt_, *at, *kpt, *kdl, *adl, *vbl)
            for j in range(6):
                ob = ba(); cp(ob[:, tok], oT[:, j, tok], eng=act)
                ps = psn(); mm(ps[:, tok], bd64s_b.v(), ob[:, tok])
                cen = fa(); tt(cen[:, tok], oT[:, j, tok], ps[:, tok], ALU.subtract)
                rstd = fa()
                pnorm(cen[:, tok], bd64s_b.v(), 1.0, 64e-5, rstd[:, tok])
                tt(cen[:, tok], cen[:, tok], rstd[:, tok], ALU.mult)
                aff(cen[:, tok], cen[:, tok], lng[:, l, j:j + 1], lnb[:, l, j:j + 1])
                tt(cen[:, tok], cen[:, tok], bonus[j][:, tok], ALU.add)
                tt(mixT[10 + j][:, tok], cen[:, tok], gate[j][:, tok], ALU.mult)
                ffr(cen, rstd); bfr(ob)
            bfr(*bonus, *gate)

            phase(5)
            for cg in range(4):
                slot = wload(w_out[l], 0, KT, cg * 512, 512)
                for m in range(4):
                    ps = psn(); dense_fm(slot, m * 128, 128, KT, lambda k: mixT[k][:, tok], ps)
                    o = cg * 4 + m
                    tt(xT[o][:, tok], xT[o][:, tok], ps[:, tok], ALU.add)

        def ffn(l):
            phase(6)
            rmsnorm(lambda k: g2[:, l, k:k + 1], hT)
            hm = []
            for cg in range(DFF // 512):
                sg_ = wload(w_gate[l], 0, KT, cg * 512, 512)
                su_ = wload(w_up[l], 0, KT, cg * 512, 512)
                bg = [psn(), psn()]; bu = [psn(), psn()]
                reg = lambda banks, m: banks[m // 2][:, (m % 2) * NT:(m % 2) * NT + NT]
                for m in range(4):
                    for k in range(KT):
                        mm(reg(bg, m), wk(sg_, k, m * 128, (m + 1) * 128), hT[k][:, tok], start=(k == 0), stop=(k == KT - 1))
                for m in range(4):
                    for k in range(KT):
                        mm(reg(bu, m), wk(su_, k, m * 128, (m + 1) * 128), hT[k][:, tok], start=(k == 0), stop=(k == KT - 1))
                for m in range(4):
                    s_ = ba(); actf(s_[:, tok], reg(bg, m), AF.Silu)
                    o = ba(); tt(o[:, tok], s_[:, tok], reg(bu, m), ALU.mult)
                    bfr(s_); hm.append(o)
            for cg in range(4):
                pd = [psn() for _ in range(4)]
                for kg in range(4):
                    slot = wload(w_down[l], kg * 1408, 11, cg * 512, 512)
                    for m in range(4):
                        for k in range(11):
                            mm(pd[m][:, tok], wk(slot, k, m * 128, (m + 1) * 128), hm[kg * 11 + k][:, tok],
                               start=(kg == 0 and k == 0), stop=(kg == 3 and k == 10))
                for m in range(4):
                    o = cg * 4 + m
                    tt(xT[o][:, tok], xT[o][:, tok], pd[m][:, tok], ALU.add)
            bfr(*hm)

        for t0 in range(0, T, NT):
            phase(1)
            for s0, rows in _chunks(NT, 128):
                for q in range(4):
                    phase(0.2)
                    kb.dma(sp, xio.t[0:rows, :], Xd[t0 + s0:t0 + s0 + rows, q * 512:(q + 1) * 512], [], [xio])
                    ps = psn()
                    for kk_ in range(4):
                        k = q * 4 + kk_
                        phase(0.5)
                        mm(ps[:, kk_ * rows:(kk_ + 1) * rows], xio[0:rows, kk_ * 128:(kk_ + 1) * 128], ident_f[0:rows, 0:rows])
                    for kk_ in range(4):
                        k = q * 4 + kk_
                        phase(0.8)
                        cp(xT[k][:, s0:s0 + rows], ps[:, kk_ * rows:(kk_ + 1) * rows], eng=(act if kk_ % 2 else dve))
            for l in range(L):
                phase(1.5)
                rmsnorm(lambda k: g1[:, l, k:k + 1], hT)
                mixer(l)
                ffn(l)
            phase(7)
            yT = [fa() for _ in range(4)]
            for q in range(4):
                pass
            ps = psn()
            for k in range(KT):
                sq = ba()
                actf(sq[:, tok], xT[k][:, tok], AF.Square)
                mm(ps[:, tok], ones_b.v(), sq[:, tok], start=(k == 0), stop=(k == KT - 1))
                bfr(sq)
            rstd = fa()
            actf(rstd[:, tok], ps[:, tok], AF.Sqrt, scale=1.0 / D, bias=1e-6)
            recip(rstd[:, tok], rstd[:, tok])
            for s0, rows in _chunks(NT, 128):
                for q in range(4):
                    ps = psn()
                    for kk_ in range(4):
                        k = q * 4 + kk_
                        y = yT[kk_]
                        stt(y[:, 0:rows], xT[k][:, s0:s0 + rows], gf[:, k:k + 1], rstd[:, s0:s0 + rows], ALU.mult, ALU.mult)
                        mm(ps[0:rows, kk_ * 128:(kk_ + 1) * 128], y[:, 0:rows], ident_f)
                    cp(xio[0:rows, 0:512], ps[0:rows, 0:512], eng=(act if q % 2 else dve))
                    kb.dma(sp, Yd[t0 + s0:t0 + s0 + rows, q * 512:(q + 1) * 512], xio.t[0:rows, :], [xio], [])
            ffr(rstd, *yT)
        kb.enabled = True
        for l in range(L):
            for h in range(4):
                kb.dma(sp, O_gla[sk][l, h], Sgla[l].t[(h % 2) * 64:(h % 2) * 64 + 64, h // 2, :], [Sgla[l]], [])
            kb.dma(sp, O_gdn[sk][l].rearrange("h d e -> d h e"), Sgdn[l].t[:], [Sgdn[l]], [])
            for h in range(12):
                kb.dma(sp, O_rw[sk][l, h], Srw[l].t[(h % 2) * 64:(h % 2) * 64 + 64, h // 2, :], [Srw[l]], [])
            for t_ in range(3):
                kb.dma(sp, O_conv[sk][l, t_].rearrange("(i p) -> p i", p=128), chist[l].t[:, :, t_], [chist[l]], [], slow=True)
            kb.dma(sp, O_shift[sk][l, 0, 0:2304].rearrange("(i p) -> p i", p=128), shist[l].t[:, 0:18], [shist[l]], [], slow=True)
            kb.dma(sp, O_shift[sk][l, 0, 2304:2368].rearrange("(p o) -> p o", o=1), shist[l].t[0:64, 18:19], [shist[l]], [], slow=True)
            kb.dma(sp, O_shift[sk][l, 0, 2368:2432].rearrange("(p o) -> p o", o=1), shist[l].t[0:64, 19:20], [shist[l]], [], slow=True)
            kb.dma(sp, O_shift[sk][l, 0, 2432:2560].rearrange("(p o) -> p o", o=1), shist[l].t[:, 20:21], [shist[l]], [], slow=True)

    if os.environ.get('KSEQ', 'ps').find('p') >= 0:
        run_seq('p', TP)
    if os.environ.get('KSEQ', 'ps').find('s') >= 0:
        run_seq('s', TS)

    for i, sem in enumerate(sp.dsems):
        n = (sp.dcount - i + len(sp.dsems) - 1) // len(sp.dsems)
        if n > 0:
            kb._need(sp, (sem, 16 * n))

    with nc.Block() as block:
        @block.tensor
        def _(e):
            for f in kb.pe.q:
                f(e)

        @block.scalar
        def _(e):
            for f in kb.act.q:
                f(e)

        @block.vector
        def _(e):
            for f in kb.dve.q:
                f(e)

        @block.gpsimd
        def _(e):
            for f in kb.pool.q:
                f(e)

        @block.sync
        def _(e):
            for f in kb.sp.q:
                f(e)
    es.close()
    return nc, kb


def make_consts():
    c = np.zeros((128, 448), np.float32)
    c[:, 0:128] = np.eye(128)
    c[0:64, 128:192] = 1.0; c[64:128, 192:256] = 1.0
    s = np.arange(64)[:, None]; t = np.arange(64)[None, :]
    c[0:64, 256:320] = (s <= t); c[0:64, 320:384] = (s < t); c[0:64, 384:448] = (t < s)
    sel = np.zeros((6, 6, 128), np.float32)
    for h in range(6):
        sel[h, h, :] = 1.0
    return c, sel.reshape(6, 768)


_CACHE = {}


def run(inputs, TP, TS, L, NTMAX=256, ncores=8, trace=False):
    key = (TP, TS, L, NTMAX)
    if key not in _CACHE:
        _CACHE[key] = build(TP, TS, L, NTMAX)
    nc, kb = _CACHE[key]
    cst, selc = make_consts()
    f = lambda a: np.ascontiguousarray(a, dtype=np.float32)
    shared = {k: f(inputs[k]) for k in (
        'norm1_g', 'w_in', 'gla_a_up', 'gla_a_bias', 'gla_norm_g', 'gdn_conv_w', 'gdn_A_log', 'gdn_dt_bias', 'gdn_norm_g',
        'rw_mu', 'rw_w0', 'rw_w_up', 'rw_a0', 'rw_a_up', 'rw_g_up', 'rw_k_k', 'rw_k_a', 'rw_r_k', 'rw_ln_g', 'rw_ln_b',
        'w_out', 'norm2_g', 'w_ffn_gate', 'w_ffn_up', 'w_ffn_down', 'final_norm_g')}
    for k in ('rw_v0', 'rw_v_down', 'rw_v_up'):
        a = f(inputs[k])
        if a.shape[0] == 0:
            a = np.zeros((1,) + a.shape[1:], np.float32)
        shared[k] = a
    shared['cst'] = cst
    in_maps = []
    for c in range(ncores):
        m = dict(shared)
        m['x_p'] = f(inputs['x_prompt'][c]); m['x_s'] = f(inputs['x_sample'][c])
        m['st_gla'] = f(inputs['state_gla'][:, c]); m['st_gdn'] = f(inputs['state_gdn'][:, c])
        m['st_conv'] = f(inputs['cache_gdn_conv'][:, c]); m['st_rw'] = f(inputs['state_rwkv'][:, c])
        m['st_shift'] = f(inputs['cache_rwkv_shift'][:, c])
        in_maps.append(m)
    res = run_bass_kernel_spmd(nc, in_maps, core_ids=list(range(ncores)), trace=trace)
    R = res.results
    st = lambda name: np.stack([R[c][name] for c in range(ncores)], axis=0)
    st1 = lambda name: np.stack([R[c][name] for c in range(ncores)], axis=1)
    outs = (st('y_p'), st('y_s'),
            st1('gla_p'), st1('gdn_p'), st1('conv_p'), st1('rwkv_p'), st1('shift_p'),
            st1('gla_s'), st1('gdn_s'), st1('conv_s'), st1('rwkv_s'), st1('shift_s'))
    return tuple(np.ascontiguousarray(o, dtype=np.float32) for o in outs), res


def kernel(**inputs):
    TP = inputs['x_prompt'].shape[1]; TS = inputs['x_sample'].shape[1]; L = inputs['norm1_g'].shape[0]
    outs, _ = run(inputs, TP, TS, L)
    return outs
```
